# Optimizing a Trainium2 kernel written in Bass

```python
import math
import jax
import jax.numpy as jnp
from jax import lax
import numpy as np

D_MODEL = 1024
BATCH = 32
SEQ = 2048
DEPTH = 4

HEAD_DIM = 64
N_Q_HEADS = 8
N_KV_HEADS = 2
GROUP = N_Q_HEADS // N_KV_HEADS
WINDOW = 128
ATTN_BLOCK = WINDOW
Q_WIDTH = N_Q_HEADS * HEAD_DIM
KV_WIDTH = N_KV_HEADS * HEAD_DIM
NUM_BUCKETS = 32
MAX_DISTANCE = 128
CONV_CHANNELS = D_MODEL // 2
CONV_WIDTH = 31
EVEN_IN = Q_WIDTH + 2 * KV_WIDTH + 2 * CONV_CHANNELS
EVEN_CAT = Q_WIDTH + CONV_CHANNELS
LRU_WIDTH = D_MODEL
LRU_HEADS = 8
LRU_BLOCK = LRU_WIDTH // LRU_HEADS
LRU_CONV_WIDTH = 4
RG_LRU_C = 8.0
D_FF = 2816
RMS_EPS = 1e-6
LN_EPS = 1e-5
NEG_INF = -1e30
N_EVEN = (DEPTH + 1) // 2
N_ODD = DEPTH // 2

kernel_name = "hybrid_swa_conformer_rglru_macaron"


def _rmsnorm(x, g):
    xf = x.astype(jnp.float32)
    y = xf * lax.rsqrt(jnp.mean(xf * xf, axis=-1, keepdims=True) + RMS_EPS)
    return (y * g.astype(jnp.float32)).astype(x.dtype)


def _layernorm(x, g, b):
    xf = x.astype(jnp.float32)
    mu = jnp.mean(xf, axis=-1, keepdims=True)
    var = jnp.mean(jnp.square(xf - mu), axis=-1, keepdims=True)
    y = (xf - mu) * lax.rsqrt(var + LN_EPS)
    return (y * g.astype(jnp.float32) + b.astype(jnp.float32)).astype(x.dtype)


def _swiglu(x, wg, wu, wd):
    return (jax.nn.silu(x @ wg) * (x @ wu)) @ wd


def _causal_depthwise_conv(x, w, b):
    k_width, chans = w.shape
    y = lax.conv_general_dilated(
        x, w[:, None, :].astype(x.dtype), window_strides=(1,),
        padding=[(k_width - 1, 0)], dimension_numbers=("NWC", "WIO", "NWC"),
        feature_group_count=chans)
    return y + b.astype(x.dtype)


def _t5_bucket(dist):
    n = jnp.maximum(dist, 0)
    max_exact = NUM_BUCKETS // 2
    nf = jnp.maximum(n, max_exact).astype(jnp.float32)
    large = max_exact + (jnp.log(nf / max_exact) / math.log(MAX_DISTANCE / max_exact)
                         * (NUM_BUCKETS - max_exact)).astype(jnp.int32)
    large = jnp.minimum(large, NUM_BUCKETS - 1)
    return jnp.where(n < max_exact, n, large)


def _swa_sink_attention(q, k, v, sinks, rel_bias):
    bsz, seq = q.shape[:2]
    nb = seq // ATTN_BLOCK
    qb = q.reshape(bsz, nb, ATTN_BLOCK, N_KV_HEADS, GROUP, HEAD_DIM)
    kb = k.reshape(bsz, nb, ATTN_BLOCK, N_KV_HEADS, HEAD_DIM)
    vb = v.reshape(bsz, nb, ATTN_BLOCK, N_KV_HEADS, HEAD_DIM)
    kk = jnp.concatenate([jnp.concatenate([jnp.zeros_like(kb[:, :1]), kb[:, :-1]], axis=1), kb], axis=2)
    vv = jnp.concatenate([jnp.concatenate([jnp.zeros_like(vb[:, :1]), vb[:, :-1]], axis=1), vb], axis=2)
    scores = jnp.einsum("bnqhgd,bnshd->bnhgqs", qb, kk,
                        preferred_element_type=jnp.float32) * (1.0 / math.sqrt(HEAD_DIM))
    qi = jnp.arange(ATTN_BLOCK)[:, None]
    sj = jnp.arange(2 * ATTN_BLOCK)[None, :]
    dist = qi + ATTN_BLOCK - sj
    bias = rel_bias.astype(jnp.float32)[_t5_bucket(dist)]
    bias = jnp.transpose(bias, (2, 0, 1)).reshape(N_KV_HEADS, GROUP, ATTN_BLOCK, 2 * ATTN_BLOCK)
    in_window = (dist >= 0) & (dist < WINDOW)
    key_pos = jnp.arange(nb)[:, None, None] * ATTN_BLOCK + sj[None] - ATTN_BLOCK
    mask = in_window[None] & (key_pos >= 0)
    scores = jnp.where(mask[None, :, None, None], scores + bias[None, None], NEG_INF)
    sink = sinks.astype(jnp.float32).reshape(N_KV_HEADS, GROUP)[None, None, :, :, None, None]
    sink = jnp.broadcast_to(sink, scores.shape[:-1] + (1,))
    probs = jax.nn.softmax(jnp.concatenate([scores, sink], axis=-1), axis=-1)[..., :-1]
    out = jnp.einsum("bnhgqs,bnshd->bnqhgd", probs.astype(v.dtype), vv)
    return out.reshape(bsz, seq, Q_WIDTH)


def _attn_conv_mixer(h, w_in, sinks, conv_w, conv_b, ln_g, ln_b, w_out, rel_bias):
    bsz, seq, _ = h.shape
    u = h @ w_in
    o1 = Q_WIDTH
    o2 = o1 + KV_WIDTH
    o3 = o2 + KV_WIDTH
    o4 = o3 + CONV_CHANNELS
    q = u[..., :o1].reshape(bsz, seq, N_Q_HEADS, HEAD_DIM)
    k = u[..., o1:o2].reshape(bsz, seq, N_KV_HEADS, HEAD_DIM)
    v = u[..., o2:o3].reshape(bsz, seq, N_KV_HEADS, HEAD_DIM)
    attn = _swa_sink_attention(q, k, v, sinks, rel_bias)
    glu = u[..., o3:o4] * jax.nn.sigmoid(u[..., o4:])
    c = jax.nn.silu(_layernorm(_causal_depthwise_conv(glu, conv_w, conv_b), ln_g, ln_b))
    return jnp.concatenate([attn, c], axis=-1) @ w_out


def _rg_lru(x, ga_w, ga_b, gx_w, gx_b, lam):
    bsz, seq, width = x.shape
    xh = x.reshape(bsz, seq, LRU_HEADS, LRU_BLOCK)
    r = jax.nn.sigmoid(jnp.einsum("bshi,hij->bshj", xh, ga_w).reshape(bsz, seq, width) + ga_b)
    i = jax.nn.sigmoid(jnp.einsum("bshi,hij->bshj", xh, gx_w).reshape(bsz, seq, width) + gx_b)
    log_a = RG_LRU_C * r.astype(jnp.float32) * jax.nn.log_sigmoid(lam.astype(jnp.float32))
    a = jnp.exp(log_a)
    bx = jnp.sqrt(-jnp.expm1(2.0 * log_a)) * (i * x).astype(jnp.float32)

    def combine(left, right):
        a1, b1 = left
        a2, b2 = right
        return a1 * a2, a2 * b1 + b2

    _, hs = lax.associative_scan(combine, (a, bx), axis=1)
    return hs.astype(x.dtype)


def _recurrent_mixer(h, w_in, conv_w, conv_b, ga_w, ga_b, gx_w, gx_b, lam, w_out):
    u = h @ w_in
    gate = jax.nn.gelu(u[..., :LRU_WIDTH])
    rec = _causal_depthwise_conv(u[..., LRU_WIDTH:], conv_w, conv_b)
    rec = _rg_lru(rec, ga_w, ga_b, gx_w, gx_b, lam)
    return (gate * rec) @ w_out


def setup_inputs(seed: int = 0) -> dict:
    key = jax.random.key(seed)
    ks = iter(jax.random.split(key, 40))

    def nrm(shape, scale):
        return jax.random.normal(next(ks), shape, jnp.float32) * scale

    def gain(shape):
        return 1.0 + nrm(shape, 0.02)

    x = nrm((BATCH, SEQ, D_MODEL), 1.0)
    u = jax.random.uniform(next(ks), (N_ODD, LRU_WIDTH), jnp.float32, 0.9, 0.999)
    a0 = u ** (1.0 / RG_LRU_C)
    lru_lambda = jnp.log(a0) - jnp.log1p(-a0)
    return {
        "x": x,
        "norm_ffn1": gain((DEPTH, D_MODEL)),
        "ffn1_wg": nrm((DEPTH, D_MODEL, D_FF), D_MODEL ** -0.5),
        "ffn1_wu": nrm((DEPTH, D_MODEL, D_FF), D_MODEL ** -0.5),
        "ffn1_wd": nrm((DEPTH, D_FF, D_MODEL), D_FF ** -0.5),
        "norm_mix": gain((DEPTH, D_MODEL)),
        "norm_ffn2": gain((DEPTH, D_MODEL)),
        "ffn2_wg": nrm((DEPTH, D_MODEL, D_FF), D_MODEL ** -0.5),
        "ffn2_wu": nrm((DEPTH, D_MODEL, D_FF), D_MODEL ** -0.5),
        "ffn2_wd": nrm((DEPTH, D_FF, D_MODEL), D_FF ** -0.5),
        "rel_bias": nrm((NUM_BUCKETS, N_Q_HEADS), 0.3),
        "even_w_in": nrm((N_EVEN, D_MODEL, EVEN_IN), D_MODEL ** -0.5),
        "attn_sinks": nrm((N_EVEN, N_Q_HEADS), 1.0),
        "conv_b_w": nrm((N_EVEN, CONV_WIDTH, CONV_CHANNELS), CONV_WIDTH ** -0.5),
        "conv_b_b": nrm((N_EVEN, CONV_CHANNELS), 0.02),
        "conv_ln_g": gain((N_EVEN, CONV_CHANNELS)),
        "conv_ln_b": nrm((N_EVEN, CONV_CHANNELS), 0.02),
        "even_w_out": nrm((N_EVEN, EVEN_CAT, D_MODEL), EVEN_CAT ** -0.5),
        "odd_w_in": nrm((N_ODD, D_MODEL, 2 * LRU_WIDTH), D_MODEL ** -0.5),
        "lru_conv_w": nrm((N_ODD, LRU_CONV_WIDTH, LRU_WIDTH), LRU_CONV_WIDTH ** -0.5),
        "lru_conv_b": nrm((N_ODD, LRU_WIDTH), 0.02),
        "gate_a_w": nrm((N_ODD, LRU_HEADS, LRU_BLOCK, LRU_BLOCK), LRU_BLOCK ** -0.5),
        "gate_a_b": nrm((N_ODD, LRU_WIDTH), 0.02),
        "gate_x_w": nrm((N_ODD, LRU_HEADS, LRU_BLOCK, LRU_BLOCK), LRU_BLOCK ** -0.5),
        "gate_x_b": nrm((N_ODD, LRU_WIDTH), 0.02),
        "lru_lambda": lru_lambda,
        "odd_w_out": nrm((N_ODD, LRU_WIDTH, D_MODEL), LRU_WIDTH ** -0.5),
        "norm_final": gain((D_MODEL,)),
    }


def reference(x, norm_ffn1, ffn1_wg, ffn1_wu, ffn1_wd, norm_mix, norm_ffn2, ffn2_wg, ffn2_wu, ffn2_wd,
              rel_bias, even_w_in, attn_sinks, conv_b_w, conv_b_b, conv_ln_g, conv_ln_b, even_w_out,
              odd_w_in, lru_conv_w, lru_conv_b, gate_a_w, gate_a_b, gate_x_w, gate_x_b, lru_lambda,
              odd_w_out, norm_final):
    h = x
    for layer in range(DEPTH):
        h = h + 0.5 * _swiglu(_rmsnorm(h, norm_ffn1[layer]), ffn1_wg[layer], ffn1_wu[layer], ffn1_wd[layer])
        hn = _rmsnorm(h, norm_mix[layer])
        if layer % 2 == 0:
            e = layer // 2
            h = h + _attn_conv_mixer(hn, even_w_in[e], attn_sinks[e], conv_b_w[e], conv_b_b[e],
                                     conv_ln_g[e], conv_ln_b[e], even_w_out[e], rel_bias)
        else:
            o = layer // 2
            h = h + _recurrent_mixer(hn, odd_w_in[o], lru_conv_w[o], lru_conv_b[o], gate_a_w[o],
                                     gate_a_b[o], gate_x_w[o], gate_x_b[o], lru_lambda[o], odd_w_out[o])
        h = h + 0.5 * _swiglu(_rmsnorm(h, norm_ffn2[layer]), ffn2_wg[layer], ffn2_wu[layer], ffn2_wd[layer])
    return _rmsnorm(h, norm_final)
```

```python
import math
from contextlib import ExitStack

import numpy as np
import concourse.bass as bass
import concourse.mybir as mybir
from concourse.bass_utils import run_bass_kernel_spmd

F32 = mybir.dt.float32
BF16 = mybir.dt.bfloat16
AF = mybir.ActivationFunctionType
ALU = mybir.AluOpType

N_CORES = 8
D = 1024
S = 2048
DEPTH = 4
DFF = 2816
DC = 8
FC = 22
TT = 512
NT = 4
EVEN_IN = 1792
RMS_EPS = 1e-6
LN_EPS = 1e-5
GROUPS = [(0, 4), (4, 8), (8, 12), (12, 16), (16, 19), (19, 22)]

ENGS = ("pe", "act", "dve", "pool", "sp")
SEM_CAP = 30000


class Res:
    __slots__ = ("w", "r", "pr")

    def __init__(self):
        self.w = []
        self.r = []
        self.pr = []


class Op:
    __slots__ = ("eng", "fn", "deps", "dma", "signal", "tok", "waits")

    def __init__(self, eng, fn, dma):
        self.eng = eng
        self.fn = fn
        self.dma = dma
        self.deps = []
        self.signal = dma is not None
        self.tok = None
        self.waits = None


class Prog:
    def __init__(self, nc):
        self.nc = nc
        self.ops = {e: [] for e in ENGS}
        self.last = {e: None for e in ENGS}
        self.pend = {}
        self.dmacnt = {}

    def emit(self, eng, fn, reads=(), writes=(), dma=None, nosame=False):
        if dma is not None:
            name, n = dma
            j = self.dmacnt.get(name, 0)
            self.dmacnt[name] = j + 1
            dma = (name, j % n)
        op = Op(eng, fn, dma)
        deps = {}
        for r in reads:
            for o in r.w:
                deps[id(o)] = o
        joins = []
        for w in writes:
            if dma is not None and w.w and not w.r and all(o.dma is not None for o in w.w):
                joins.append(w)
                for o in w.pr:
                    deps[id(o)] = o
                continue
            for o in w.w:
                deps[id(o)] = o
            for o in w.r:
                deps[id(o)] = o
        for d in deps.values():
            if d.eng == eng and d.dma is None and dma is None:
                if eng == "pe" or nosame:
                    continue
            op.deps.append(d)
        if eng in self.pend:
            for d in self.pend.pop(eng):
                if d is not None and not (d.eng == eng and d.dma is None):
                    op.deps.append(d)
        for r in reads:
            r.r.append(op)
        for w in writes:
            if any(w is j for j in joins):
                w.w.append(op)
            else:
                w.pr = w.r
                w.w = [op]
                w.r = []
        self.ops[eng].append(op)
        self.last[eng] = op
        return op

    def barrier(self, engs):
        lasts = [self.last[e] for e in engs]
        for e in engs:
            self.pend[e] = list(self.pend.get(e, [])) + lasts

    def build(self, stack):
        nc = self.nc
        for e in ENGS:
            for op in self.ops[e]:
                for d in op.deps:
                    d.signal = True
        sems = {}
        cnt = {}
        for e in ENGS:
            c = 0
            for op in self.ops[e]:
                if op.dma is not None:
                    k = ("d",) + op.dma
                    cnt[k] = cnt.get(k, 0) + 16
                    op.tok = (k, cnt[k])
                elif op.signal:
                    c += 1
                    op.tok = (("e", e, (c - 1) // SEM_CAP), (c - 1) % SEM_CAP + 1)
        for e in ENGS:
            seen = {}
            for op in self.ops[e]:
                ws = {}
                for d in op.deps:
                    k, v = d.tok
                    if seen.get(k, 0) >= v:
                        continue
                    if ws.get(k, 0) < v:
                        ws[k] = v
                seen.update(ws)
                op.waits = list(ws.items())
                if op.tok is not None and op.tok[0] not in sems:
                    sems[op.tok[0]] = None
        for k in list(sems):
            sems[k] = stack.enter_context(nc.semaphore("s_" + "_".join(str(x) for x in k)))
        block = stack.enter_context(nc.Block())
        names = {"pe": "tensor", "act": "scalar", "dve": "vector", "pool": "gpsimd", "sp": "sync"}
        ops = self.ops

        def run(e, eng):
            for op in ops[e]:
                for k, v in op.waits:
                    eng.wait_ge(sems[k], v)
                ins = op.fn(eng)
                if op.tok is not None:
                    ins.then_inc(sems[op.tok[0]], 16 if op.dma is not None else 1)

        for e in ENGS:
            if not ops[e]:
                continue

            def f(eng, e=e):
                run(e, eng)

            getattr(block, names[e])(f)


def _pcols(v, nchunk):
    v = np.asarray(v, np.float32)
    lead = v.shape[:-1]
    v = v.reshape(lead + (nchunk, 128))
    v = np.moveaxis(v, -1, 0)
    return np.ascontiguousarray(v).reshape(128, -1)


class PL:
    pass


def pack_params(inp):
    cols = []
    off = {}

    def add(name, arr):
        off[name] = sum(c.shape[1] for c in cols)
        cols.append(np.ascontiguousarray(arr, dtype=np.float32))

    add("n1", _pcols(inp["norm_ffn1"], 8))
    add("nm", _pcols(inp["norm_mix"], 8))
    add("n2", _pcols(inp["norm_ffn2"], 8))
    add("nf", _pcols(inp["norm_final"], 8))
    cw = np.asarray(inp["conv_b_w"], np.float32)
    cw = cw.reshape(2, 31, 4, 128).transpose(3, 0, 2, 1)
    add("cw", cw.reshape(128, -1))
    add("cb", _pcols(inp["conv_b_b"], 4))
    add("lg", _pcols(inp["conv_ln_g"], 4))
    add("lb", _pcols(inp["conv_ln_b"], 4))
    lw = np.asarray(inp["lru_conv_w"], np.float32)
    lw = lw.reshape(2, 4, 8, 128).transpose(3, 0, 2, 1)
    add("lw", lw.reshape(128, -1))
    add("lcb", _pcols(inp["lru_conv_b"], 8))
    add("gab", _pcols(inp["gate_a_b"], 8))
    add("gxb", _pcols(inp["gate_x_b"], 8))
    add("lam", _pcols(inp["lru_lambda"], 8))
    sk = np.asarray(inp["attn_sinks"], np.float32).reshape(1, 16)
    add("sink", np.broadcast_to(sk, (128, 16)))
    add("eps", np.broadcast_to(np.array([[RMS_EPS, LN_EPS, 1.0, 0.25]], np.float32), (128, 4)))
    return np.concatenate(cols, axis=1), off


def t5_bucket_np(dist):
    n = np.maximum(dist, 0)
    max_exact = 16
    nf = np.maximum(n, max_exact).astype(np.float32)
    large = max_exact + (np.log(nf / np.float32(max_exact)) / np.float32(math.log(128 / max_exact))
                         * np.float32(32 - max_exact)).astype(np.int32)
    large = np.minimum(large, 31)
    return np.where(n < max_exact, n, large)


def attn_consts(rel_bias):
    q = np.arange(128)[None, :]
    s = np.arange(128)[:, None]
    out_b = np.zeros((128, 2, 8, 128), np.float32)
    out_m = np.zeros((128, 2, 8, 128), np.float32)
    rb = np.asarray(rel_bias, np.float32)
    for half in range(2):
        dist = q + 128 - (s + 128 * half)
        ok = (dist >= 0) & (dist < 128)
        bk = t5_bucket_np(dist)
        g = rb[bk]
        out_b[:, half] = np.transpose(g, (0, 2, 1))
        out_m[:, half] = np.where(ok, 0.0, -1e30)[:, None, :]
    return out_b, out_m


def qperm():
    idx = []
    for cq in range(4):
        idx += list(range(cq * 64, cq * 64 + 64))
        idx += list(range((4 + cq) * 64, (4 + cq) * 64 + 64))
    return np.array(idx + list(range(512, EVEN_IN)))


class Builder:
    def __init__(self, n_seq, layers, stages=("ffn1", "mix", "ffn2")):
        self.n_seq = n_seq
        self.layers = layers
        self.stages = stages
        self.nc = bass.Bass("TRN2", target_bir_lowering=False)
        self.nb = 0

    def bank(self):
        b = self.nb % 8
        self.nb += 1
        return self.ps[b], self.psR[b]

    def tmp(self):
        k = self.ntmp % len(self.T)
        self.ntmp += 1
        return self.T[k], self.TR[k]

    def view(self, off_bytes, shape, dt, p0=0, p1=128):
        n = int(np.prod(shape))
        if dt == F32:
            assert off_bytes % 4 == 0
            a = self.arena32[p0:p1, off_bytes // 4: off_bytes // 4 + n]
        else:
            assert off_bytes % 2 == 0
            a = self.arena[p0:p1, off_bytes // 2: off_bytes // 2 + n]
        if len(shape) == 2:
            a = a.rearrange("p (a b) -> p a b", a=shape[0])
        elif len(shape) == 3:
            a = a.rearrange("p (a b c) -> p a b c", a=shape[0], b=shape[1])
        return a

    def build(self, pcol):
        nc = self.nc
        n_seq = self.n_seq
        self.pcol = pcol
        npar = self.npar
        dr = {}

        def din(name, shape):
            dr[name] = nc.dram_tensor(name, list(shape), F32, kind="ExternalInput").ap()

        din("x", [n_seq, S, D])
        din("params", [128, npar])
        din("ident", [128, 128])
        din("biasT", [128, 2048])
        din("maskneg", [128, 2048])
        for w in ("ffn1", "ffn2"):
            din(w + "_wg", [DEPTH, D, DFF])
            din(w + "_wu", [DEPTH, D, DFF])
            din(w + "_wd", [DEPTH, DFF, D])
        din("even_w_in", [2, D, EVEN_IN])
        din("even_w_out", [2, D, D])
        din("odd_w_in", [2, D, 2 * D])
        din("odd_w_out", [2, D, D])
        din("gate_a_w", [2, 8, 128, 128])
        din("gate_x_w", [2, 8, 128, 128])
        self.dr = dr
        self.out = nc.dram_tensor("out", [n_seq, S, D], F32, kind="ExternalOutput").ap()

        with ExitStack() as st:
            sb = lambda name, shape, dt: st.enter_context(nc.sbuf_tensor(name, shape, dt))
            self.h = sb("h", [128, DC, S], F32)
            self.wsl = sb("wsl", [128, 24576], BF16)
            self.arena = sb("arena", [128, 28 * 1024], BF16)
            self.arena32 = self.arena[:].bitcast(F32)
            self.hnm = sb("hnm", [128, DC, TT], BF16)
            self.parm = sb("parm", [128, npar], F32)
            self.der = sb("der", [128, 64], F32)
            self.ident = sb("ident_sb", [128, 128], F32)
            self.ones = sb("ones", [128, 128], BF16)
            self.EBT = sb("EBT", [128, 2048], F32)
            self.sq = sb("sq", [128, DC, TT], BF16)
            self.gw = sb("gw", [128, 2, 8, 128], BF16)
            self.sinkrow = self.gw[0:1, :, :, :].rearrange("p a b c -> p (a b) c")
            self.T = [sb("T%d" % i, [128, TT], F32) for i in range(4)]
            self.TR = [Res() for _ in range(4)]
            self.ntmp = 0
            self.ps = [st.enter_context(nc.psum_tensor("ps%d" % i, [128, TT], F32)) for i in range(8)]
            self.psR = [Res() for _ in range(8)]
            self.hnf = self.view(0, [DC, S], BF16)
            self.AR = 32 * 1024

            self.P = Prog(nc)
            self.Hr = [[Res() for _ in range(NT)] for _ in range(DC)]
            self.hnfR = [[Res() for _ in range(NT)] for _ in range(DC)]
            self.hnmR = [Res() for _ in range(DC)]
            self.sqR = Res()
            self.SA, self.SB = Res(), Res()
            self.cR = Res()
            self.gwR = Res()

            self.setup()
            for i in range(n_seq):
                self.load(i)
                for l in range(self.layers):
                    if "ffn1" in self.stages:
                        self.ffn(l, "ffn1", pcol["n1"] + l * 8)
                    if "mix" in self.stages:
                        if l % 2 == 0:
                            self.mix_even(l)
                        else:
                            self.mix_odd(l)
                    if "ffn2" in self.stages:
                        self.ffn(l, "ffn2", pcol["n2"] + l * 8)
                self.store(i)
            P = self.P
            fin = P.emit("sp", lambda e: e.nop(), reads=self.outR, writes=self.outR)
            for o in self.out_ops[-4:]:
                if o not in fin.deps:
                    fin.deps.append(o)
            P.build(st)
        return nc

    def setup(self):
        P, dr = self.P, self.dr
        pc = self.pcol
        parm, der = self.parm, self.der
        cR = self.cR
        self.outR = [Res(), Res()]
        self.out_ops = []
        P.emit("sp", lambda e: e.dma_start(out=parm[:], in_=dr["params"]), writes=[cR], dma=("c0", 1))
        P.emit("sp", lambda e: e.dma_start(out=self.ident[:], in_=dr["ident"]), writes=[cR], dma=("c1", 1))
        bt = self.view(self.AR, [2048], F32)
        mk = self.view(self.AR + 8192, [2048], F32)
        tR = Res()
        P.emit("sp", lambda e: e.dma_start(out=bt, in_=dr["biasT"]), writes=[tR], dma=("c2", 1))
        P.emit("sp", lambda e: e.dma_start(out=mk, in_=dr["maskneg"]), writes=[tR], dma=("c3", 1))
        P.emit("dve", lambda e: e.memset(self.ones[:], 1.0), writes=[cR])
        P.emit("dve", lambda e: e.tensor_tensor(out=bt, in0=bt, in1=mk, op=ALU.add), reads=[tR], writes=[tR])
        P.emit("act", lambda e: e.activation(out=self.EBT[:], in_=bt, func=AF.Exp), reads=[tR], writes=[cR])
        lam = parm[:, pc["lam"]:pc["lam"] + 16]
        dR = Res()
        P.emit("act", lambda e: e.activation(out=der[:, 0:16], in_=lam, func=AF.Exp, scale=-1.0), reads=[cR], writes=[dR])
        P.emit("dve", lambda e: e.tensor_scalar_add(out=der[:, 0:16], in0=der[:, 0:16], scalar1=1.0), reads=[dR], writes=[dR])
        P.emit("act", lambda e: e.activation(out=der[:, 0:16], in_=der[:, 0:16], func=AF.Ln), reads=[dR], writes=[dR])
        P.emit("dve", lambda e: e.tensor_scalar_mul(out=der[:, 16:32], in0=der[:, 0:16], scalar1=-16.0), reads=[dR], writes=[dR])
        P.emit("dve", lambda e: e.tensor_scalar_mul(out=der[:, 0:16], in0=der[:, 0:16], scalar1=-8.0), reads=[dR], writes=[dR])
        P.emit("act", lambda e: e.activation(out=der[:, 32:48], in_=parm[:, pc["sink"]:pc["sink"] + 16], func=AF.Exp),
               reads=[cR, dR], writes=[dR])
        last = P.emit("dve", lambda e: e.memset(self.T[0][:], 0.0), reads=[dR, tR], writes=[cR, tR, dR])
        P.barrier(("pe", "act", "dve", "sp"))

    def load(self, i):
        P = self.P
        x = self.dr["x"]
        xs = [self.view(self.AR + k * 4096, [D], F32) for k in range(2)]
        xsR = [Res(), Res()]
        for j in range(16):
            k = j % 2
            t = j // 4
            P.emit("sp", lambda e, j=j, k=k: e.dma_start(out=xs[k], in_=x[i, j * 128:(j + 1) * 128, :]),
                   writes=[xsR[k]], dma=("xin", 2))
            for half in range(2):
                ps, pr = self.bank()
                for q in range(4):
                    c = half * 4 + q
                    P.emit("pe", lambda e, ps=ps, q=q, c=c, k=k: e.transpose(
                        out=ps[:, q * 128:(q + 1) * 128], in_=xs[k][:, c * 128:(c + 1) * 128], identity=self.ident[:]),
                        reads=[xsR[k], self.cR], writes=[pr])
                eng = "dve" if half == 0 else "act"
                dst = self.h[:, half * 4:(half + 1) * 4, j * 128:(j + 1) * 128]
                src = ps[:].rearrange("p (a b) -> p a b", a=4)
                if eng == "dve":
                    fn = lambda e, dst=dst, src=src: e.tensor_copy(out=dst, in_=src)
                else:
                    fn = lambda e, dst=dst, src=src: e.activation(out=dst, in_=src, func=AF.Copy)
                P.emit(eng, fn, reads=[pr], writes=[self.Hr[c2][t] for c2 in range(half * 4, half * 4 + 4)], nosame=True)
        P.barrier(("pe", "act", "dve", "sp"))

    def store(self, i):
        P = self.P
        pc = self.pcol
        hn = self.view(self.AR, [DC, TT], F32)
        ys = [self.view(self.AR + 16384 + k * 4096, [D], F32) for k in range(2)]
        hnR = [Res() for _ in range(DC)]
        ysR = self.outR
        for t in range(NT):
            if getattr(self, "dbg", False):
                for c in range(DC):
                    P.emit("dve", lambda e, c=c, t=t: e.tensor_copy(out=hn[:, c, :], in_=self.h[:, c, t * TT:(t + 1) * TT]),
                           reads=[self.Hr[c][t]], writes=[hnR[c]])
            else:
                self.rmsnorm(t, pc["nf"], lambda c: hn[:, c, :], hnR)
            for blk in range(4):
                j = t * 4 + blk
                k = j % 2
                for half in range(2):
                    ps, pr = self.bank()
                    for q in range(4):
                        c = half * 4 + q
                        P.emit("pe", lambda e, ps=ps, q=q, c=c, blk=blk: e.transpose(
                            out=ps[:, q * 128:(q + 1) * 128], in_=hn[:, c, blk * 128:(blk + 1) * 128],
                            identity=self.ident[:]), reads=[hnR[c], self.cR], writes=[pr])
                    dst = ys[k][:, half * 512:(half + 1) * 512]
                    if half == 0:
                        P.emit("act", lambda e, dst=dst, ps=ps: e.activation(out=dst, in_=ps[:], func=AF.Copy),
                               reads=[pr], writes=[ysR[k]], nosame=True)
                    else:
                        P.emit("dve", lambda e, dst=dst, ps=ps: e.tensor_copy(out=dst, in_=ps[:]),
                               reads=[pr], writes=[ysR[k]], nosame=True)
                o = P.emit("sp", lambda e, j=j, k=k: e.dma_start(out=self.out[i, j * 128:(j + 1) * 128, :], in_=ys[k]),
                           reads=[ysR[k]], dma=("yout", 2))
                self.out_ops.append(o)
        P.barrier(("pe", "act", "dve", "sp"))

    def rmsnorm(self, t, gcol, outf, outR):
        P = self.P
        pc = self.pcol
        tr = slice(t * TT, (t + 1) * TT)
        hR = [self.Hr[c][t] for c in range(DC)]
        P.emit("act", lambda e: e.activation(out=self.sq[:], in_=self.h[:, :, tr], func=AF.Square),
               reads=hR, writes=[self.sqR])
        ps, pr = self.bank()
        for c in range(DC):
            P.emit("pe", lambda e, c=c, ps=ps: e.matmul(ps[:], self.ones[:], self.sq[:, c, :], start=(c == 0), stop=(c == DC - 1)),
                   reads=[self.sqR, self.cR], writes=[pr])
        rs, rr = self.tmp()
        eps = self.parm[:, pc["eps"]:pc["eps"] + 1]
        P.emit("act", lambda e, ps=ps, rs=rs: e.activation(out=rs[:], in_=ps[:], func=AF.Sqrt, bias=eps, scale=1.0 / D),
               reads=[pr, self.cR], writes=[rr])
        P.emit("dve", lambda e, rs=rs: e.reciprocal(out=rs[:], in_=rs[:]), reads=[rr], writes=[rr])
        for c in range(DC):
            g = self.parm[:, gcol + c:gcol + c + 1]
            P.emit("dve", lambda e, c=c, g=g, rs=rs: e.scalar_tensor_tensor(
                out=outf(c), in0=self.h[:, c, tr], scalar=g, in1=rs[:], op0=ALU.mult, op1=ALU.mult),
                reads=[hR[c], rr, self.cR], writes=[outR[c]], nosame=True)

    def wslot(self, k):
        return self.wsl[:, k * 12288:(k + 1) * 12288]

    def load_ffn_group(self, l, which, gi, slot):
        P, dr = self.P, self.dr
        f0, f1 = GROUPS[gi]
        nf = f1 - f0
        gwid = nf * 128
        sl = self.wslot(slot)
        R = self.SA if slot == 0 else self.SB
        wg = dr[which + "_wg"][l, :, f0 * 128:f1 * 128].rearrange("(kc p) f -> p kc f", p=128)
        wu = dr[which + "_wu"][l, :, f0 * 128:f1 * 128].rearrange("(kc p) f -> p kc f", p=128)
        wd = dr[which + "_wd"][l, f0 * 128:f1 * 128, :].rearrange("(fc p) d -> p fc d", p=128)
        og = sl[:, 0:8 * gwid].rearrange("p (kc f) -> p kc f", kc=8)
        ou = sl[:, 4096:4096 + 8 * gwid].rearrange("p (kc f) -> p kc f", kc=8)
        od = sl[:, 8192:8192 + nf * 1024].rearrange("p (fc d) -> p fc d", fc=nf)
        P.emit("pool", lambda e: e.dma_start(out=og, in_=wg), writes=[R], dma=("w", 8))
        P.emit("pool", lambda e: e.dma_start(out=ou, in_=wu), writes=[R], dma=("w", 8))
        P.emit("pool", lambda e: e.dma_start(out=od, in_=wd), writes=[R], dma=("w", 8))
        return og, ou, od, R

    def ffn(self, l, which, gcol):
        P = self.P
        act = [self.view(self.AR + k * 4096, [4, TT], BF16) for k in range(2)]
        actR = [[Res() for _ in range(4)] for _ in range(2)]
        stm = [self.view(self.AR + 8192 + k * 2048, [TT], F32) for k in range(2)]
        stR = [Res(), Res()]
        nst = [0]
        pieces = {}
        pieces[0] = self.load_ffn_group(l, which, 0, 0)

        def up(gi, t):
            og, ou, od, R = pieces[gi]
            f0, f1 = GROUPS[gi]
            tr = slice(t * TT, (t + 1) * TT)
            for i in range(f1 - f0):
                psg, prg = self.bank()
                psu, pru = self.bank()
                for kc in range(DC):
                    P.emit("pe", lambda e, kc=kc, i=i, psg=psg: e.matmul(
                        psg[:], og[:, kc, i * 128:(i + 1) * 128], self.hnf[:, kc, tr], start=(kc == 0), stop=(kc == DC - 1)),
                        reads=[R, self.hnfR[kc][t]], writes=[prg])
                for kc in range(DC):
                    P.emit("pe", lambda e, kc=kc, i=i, psu=psu: e.matmul(
                        psu[:], ou[:, kc, i * 128:(i + 1) * 128], self.hnf[:, kc, tr], start=(kc == 0), stop=(kc == DC - 1)),
                        reads=[R, self.hnfR[kc][t]], writes=[pru])
                k = nst[0] % 2
                nst[0] += 1
                P.emit("act", lambda e, k=k, psg=psg: e.activation(out=stm[k], in_=psg[:], func=AF.Silu),
                       reads=[prg], writes=[stR[k]])
                P.emit("dve", lambda e, k=k, i=i, psu=psu, t=t: e.tensor_tensor(
                    out=act[t % 2][:, i, :], in0=stm[k], in1=psu[:], op=ALU.mult),
                    reads=[stR[k], pru], writes=[actR[t % 2][i]])

        def down(gi, t):
            og, ou, od, R = pieces[gi]
            f0, f1 = GROUPS[gi]
            nf = f1 - f0
            tr = slice(t * TT, (t + 1) * TT)
            for c in range(DC):
                ps, pr = self.bank()
                for i in range(nf):
                    P.emit("pe", lambda e, i=i, c=c, ps=ps: e.matmul(
                        ps[:], od[:, i, c * 128:(c + 1) * 128], act[t % 2][:, i, :], start=(i == 0), stop=(i == nf - 1)),
                        reads=[R, actR[t % 2][i]], writes=[pr])
                P.emit("dve", lambda e, c=c, ps=ps: e.scalar_tensor_tensor(
                    out=self.h[:, c, tr], in0=ps[:], scalar=0.5, in1=self.h[:, c, tr], op0=ALU.mult, op1=ALU.add),
                    reads=[pr, self.Hr[c][t]], writes=[self.Hr[c][t]], nosame=True)

        for gi in range(len(GROUPS)):
            if gi + 1 < len(GROUPS):
                pieces[gi + 1] = self.load_ffn_group(l, which, gi + 1, (gi + 1) % 2)
            for t in range(NT):
                if gi == 0:
                    self.rmsnorm(t, gcol, lambda c, t=t: self.hnf[:, c, t * TT:(t + 1) * TT], [self.hnfR[c][t] for c in range(DC)])
                up(gi, t)
                if t > 0:
                    down(gi, t - 1)
            down(gi, NT - 1)
        P.barrier(("pe", "act", "dve"))

    def mix_even(self, l):
        P, dr = self.P, self.dr
        pc = self.pcol
        e_ = l // 2
        wsl = self.wsl
        SA, SB = self.SA, self.SB
        wqkv = wsl[:, 0:6144].rearrange("p (kc f) -> p kc f", kc=8)
        wab = wsl[:, 6144:14336].rearrange("p (kc f) -> p kc f", kc=8)
        woA = wsl[:, 16384:20480].rearrange("p (j d) -> p j d", j=4)
        woC = wsl[:, 20480:24576].rearrange("p (j d) -> p j d", j=4)
        win = dr["even_w_in"][e_]
        wout = dr["even_w_out"][e_]
        P.emit("pool", lambda e: e.dma_start(out=wqkv, in_=win[:, 0:768].rearrange("(kc p) f -> p kc f", p=128)),
               writes=[SA], dma=("w", 8))
        P.emit("pool", lambda e: e.dma_start(out=wab, in_=win[:, 768:1792].rearrange("(kc p) f -> p kc f", p=128)),
               writes=[SA, SB], dma=("w", 8))
        for g in range(2):
            P.emit("pool", lambda e, g=g: e.dma_start(
                out=woA[g * 64:(g + 1) * 64, :, :], in_=wout[g * 256:(g + 1) * 256, :].rearrange("(j d) n -> d j n", d=64)),
                writes=[SB], dma=("w", 8))
        P.emit("pool", lambda e: e.dma_start(out=woC, in_=wout[512:1024, :].rearrange("(j p) n -> p j n", p=128)),
               writes=[SB], dma=("w", 8))
        o = 0

        def alloc(shape, dt):
            nonlocal o
            v = self.view(o, shape, dt)
            o += int(np.prod(shape)) * (4 if dt == F32 else 2)
            o = (o + 63) // 64 * 64
            return v

        qT = alloc([4, TT], BF16)
        kTb = alloc([5, 128], BF16)
        vb = alloc([5, 128], BF16)
        glu = alloc([4, 30 + TT], F32)
        y = alloc([4, TT], F32)
        PT = [alloc([TT], BF16) for _ in range(4)]
        catA = alloc([4, TT], BF16)
        catC = alloc([4, TT], BF16)
        mean_b = alloc([TT], F32)
        var_b = alloc([TT], F32)
        meanR, varR = Res(), Res()
        assert o <= 56 * 1024
        qR = [Res() for _ in range(4)]
        kR, vR, gluR, yR = Res(), Res(), [Res() for _ in range(4)], [Res() for _ in range(4)]
        PTR = [Res() for _ in range(4)]
        catAR, catCR = [Res() for _ in range(4)], [Res() for _ in range(4)]
        npt = [0]
        cw0 = pc["cw"] + e_ * 4 * 31
        P.emit("dve", lambda e: e.memset(glu[:, :, 0:30], 0.0), writes=gluR)
        sinkrow = self.sinkrow
        for i in range(8):
            P.emit("dve", lambda e, i=i: e.tensor_scalar_mul(out=sinkrow[0:1, e_ * 8 + i, :], in0=self.ones[0:1, :],
                                                             scalar1=self.der[0:1, 32 + e_ * 8 + i:33 + e_ * 8 + i]),
                   reads=[self.cR], writes=[self.gwR], nosame=(i > 0))
        for t in range(NT):
            tr = slice(t * TT, (t + 1) * TT)
            self.rmsnorm(t, pc["nm"] + l * 8, lambda c: self.hnm[:, c, :], self.hnmR)
            for cq in range(4):
                ps, pr = self.bank()
                for kc in range(DC):
                    P.emit("pe", lambda e, kc=kc, cq=cq, ps=ps: e.matmul(
                        ps[:], wqkv[:, kc, cq * 128:(cq + 1) * 128], self.hnm[:, kc, :], start=(kc == 0), stop=(kc == DC - 1)),
                        reads=[SA, self.hnmR[kc]], writes=[pr])
                P.emit("act", lambda e, cq=cq, ps=ps: e.activation(out=qT[:, cq, :], in_=ps[:], func=AF.Copy, scale=0.125),
                       reads=[pr], writes=[qR[cq]])
            ps, pr = self.bank()
            for kc in range(DC):
                P.emit("pe", lambda e, kc=kc, ps=ps: e.matmul(
                    ps[:], wqkv[:, kc, 512:640], self.hnm[:, kc, :], start=(kc == 0), stop=(kc == DC - 1)),
                    reads=[SA, self.hnmR[kc]], writes=[pr])
            P.emit("act", lambda e, ps=ps: e.activation(out=kTb[:, 1:5, :], in_=ps[:].rearrange("p (a b) -> p a b", a=4), func=AF.Copy),
                   reads=[pr], writes=[kR])
            ps, pr = self.bank()
            for blk in range(4):
                for kc in range(DC):
                    P.emit("pe", lambda e, kc=kc, blk=blk, ps=ps: e.matmul(
                        ps[:, blk * 128:(blk + 1) * 128], self.hnm[:, kc, blk * 128:(blk + 1) * 128], wqkv[:, kc, 640:768],
                        start=(kc == 0), stop=(kc == DC - 1)),
                        reads=[SA, self.hnmR[kc]], writes=[pr])
            P.emit("act", lambda e, ps=ps: e.activation(out=vb[:, 1:5, :], in_=ps[:].rearrange("p (a b) -> p a b", a=4), func=AF.Copy),
                   reads=[pr], writes=[vR])
            for n in range(4):
                nbk = t * 4 + n
                halves = ([0] if nbk > 0 else []) + [1]
                psO, prO = self.bank()
                psD, prD = self.bank()
                for gk in range(2):
                    p0, p1 = gk * 64, (gk + 1) * 64
                    for hi, half in enumerate(halves):
                        slot = n + half
                        psS, prS = self.bank()
                        P.emit("pe", lambda e, psS=psS, slot=slot, p0=p0, p1=p1, n=n: e.matmul(
                            psS[:].rearrange("p (a b) -> p a b", a=4), kTb[p0:p1, slot, :], qT[p0:p1, :, n * 128:(n + 1) * 128],
                            start=True, stop=True), reads=[kR] + qR, writes=[prS])
                        E, ER = self.tmp()
                        P.emit("act", lambda e, E=E, psS=psS: e.activation(out=E[:], in_=psS[:], func=AF.Exp),
                               reads=[prS], writes=[ER])
                        k = npt[0] % 4
                        npt[0] += 1
                        eb = self.EBT[:, half * 1024 + gk * 512: half * 1024 + (gk + 1) * 512]
                        P.emit("dve", lambda e, E=E, k=k, eb=eb: e.tensor_tensor(out=PT[k], in0=E[:], in1=eb, op=ALU.mult),
                               reads=[ER, self.cR], writes=[PTR[k]])
                        first = hi == 0
                        last = hi == len(halves) - 1
                        P.emit("pe", lambda e, k=k, slot=slot, p0=p0, p1=p1, first=first, last=last, psO=psO: e.matmul(
                            psO[p0:p1, :], vb[:, slot, p0:p1], PT[k], start=first, stop=last),
                            reads=[vR, PTR[k]], writes=[prO])
                        P.emit("pe", lambda e, k=k, p0=p0, p1=p1, first=first, psD=psD: e.matmul(
                            psD[p0:p1, :], self.ones[:, 0:64], PT[k], start=first, stop=False),
                            reads=[self.cR, PTR[k]], writes=[prD])
                    sr = sinkrow[0:1, e_ * 8 + gk * 4: e_ * 8 + gk * 4 + 4, :]
                    P.emit("pe", lambda e, p0=p0, p1=p1, sr=sr, psD=psD: e.matmul(
                        psD[p0:p1, :].rearrange("p (a b) -> p a b", a=4), self.ones[0:1, 0:64], sr, start=False, stop=True),
                        reads=[self.cR, self.gwR], writes=[prD])
                rc, rcR = self.tmp()
                P.emit("dve", lambda e, rc=rc, psD=psD: e.reciprocal(out=rc[:], in_=psD[:]), reads=[prD], writes=[rcR])
                P.emit("dve", lambda e, rc=rc, psO=psO, n=n: e.tensor_tensor(
                    out=catA[:, :, n * 128:(n + 1) * 128], in0=psO[:].rearrange("p (a b) -> p a b", a=4),
                    in1=rc[:].rearrange("p (a b) -> p a b", a=4), op=ALU.mult),
                    reads=[prO, rcR], writes=catAR, nosame=True)
            if t < NT - 1:
                P.emit("act", lambda e: e.activation(out=kTb[:, 0, :], in_=kTb[:, 4, :], func=AF.Copy), reads=[kR], writes=[kR])
                P.emit("act", lambda e: e.activation(out=vb[:, 0, :], in_=vb[:, 4, :], func=AF.Copy), reads=[vR], writes=[vR])
            for c in range(4):
                psa, pra = self.bank()
                psb, prb = self.bank()
                for kc in range(DC):
                    P.emit("pe", lambda e, kc=kc, c=c, psa=psa: e.matmul(
                        psa[:], wab[:, kc, c * 128:(c + 1) * 128], self.hnm[:, kc, :], start=(kc == 0), stop=(kc == DC - 1)),
                        reads=[SA, SB, self.hnmR[kc]], writes=[pra])
                for kc in range(DC):
                    P.emit("pe", lambda e, kc=kc, c=c, psb=psb: e.matmul(
                        psb[:], wab[:, kc, 512 + c * 128:512 + (c + 1) * 128], self.hnm[:, kc, :], start=(kc == 0), stop=(kc == DC - 1)),
                        reads=[SA, SB, self.hnmR[kc]], writes=[prb])
                sg, sgR = self.tmp()
                P.emit("act", lambda e, sg=sg, psb=psb: e.activation(out=sg[:], in_=psb[:], func=AF.Sigmoid), reads=[prb], writes=[sgR])
                P.emit("dve", lambda e, sg=sg, psa=psa, c=c: e.tensor_tensor(out=glu[:, c, 30:30 + TT], in0=sg[:], in1=psa[:], op=ALU.mult),
                       reads=[sgR, pra], writes=[gluR[c]])
            for c in range(4):
                wcol = cw0 + c * 31
                cb = self.parm[:, pc["cb"] + e_ * 4 + c: pc["cb"] + e_ * 4 + c + 1]
                P.emit("dve", lambda e, c=c, wcol=wcol, cb=cb: e.tensor_scalar(
                    out=y[:, c, :], in0=glu[:, c, 30:30 + TT], scalar1=self.parm[:, wcol + 30:wcol + 31], scalar2=cb,
                    op0=ALU.mult, op1=ALU.add), reads=[gluR[c], self.cR], writes=[yR[c]])
                for k in range(30):
                    P.emit("dve", lambda e, c=c, k=k, wcol=wcol: e.scalar_tensor_tensor(
                        out=y[:, c, :], in0=glu[:, c, k:k + TT], scalar=self.parm[:, wcol + k:wcol + k + 1], in1=y[:, c, :],
                        op0=ALU.mult, op1=ALU.add), reads=[gluR[c], yR[c]], writes=[yR[c]])
            if t < NT - 1:
                P.emit("act", lambda e: e.activation(out=glu[:, :, 0:30], in_=glu[:, :, TT:TT + 30], func=AF.Copy),
                       reads=gluR, writes=gluR)
            P.emit("act", lambda e: e.activation(out=self.sq[:, 0:4, :], in_=y[:], func=AF.Copy), reads=yR, writes=[self.sqR])
            P.emit("act", lambda e: e.activation(out=self.sq[:, 4:8, :], in_=y[:], func=AF.Square), reads=yR, writes=[self.sqR])
            ps1, pr1 = self.bank()
            ps2, pr2 = self.bank()
            for c in range(4):
                P.emit("pe", lambda e, c=c, ps1=ps1: e.matmul(ps1[:], self.ones[:], self.sq[:, c, :], start=(c == 0), stop=(c == 3)),
                       reads=[self.sqR, self.cR], writes=[pr1])
            for c in range(4):
                P.emit("pe", lambda e, c=c, ps2=ps2: e.matmul(ps2[:], self.ones[:], self.sq[:, 4 + c, :], start=(c == 0), stop=(c == 3)),
                       reads=[self.sqR, self.cR], writes=[pr2])
            mean, var = mean_b, var_b
            P.emit("dve", lambda e, ps1=ps1: e.tensor_scalar_mul(out=mean, in0=ps1[:], scalar1=1.0 / 512), reads=[pr1], writes=[meanR])
            P.emit("dve", lambda e: e.tensor_tensor(out=var, in0=mean, in1=mean, op=ALU.mult), reads=[meanR], writes=[varR])
            P.emit("dve", lambda e, ps2=ps2: e.scalar_tensor_tensor(
                out=var, in0=ps2[:], scalar=1.0 / 512, in1=var, op0=ALU.mult, op1=ALU.subtract), reads=[pr2, varR], writes=[varR])
            epsl = self.parm[:, pc["eps"] + 1:pc["eps"] + 2]
            P.emit("act", lambda e: e.activation(out=var, in_=var, func=AF.Sqrt, bias=epsl, scale=1.0), reads=[varR, self.cR], writes=[varR])
            P.emit("dve", lambda e: e.reciprocal(out=var, in_=var), reads=[varR], writes=[varR])
            for c in range(4):
                z, zR = self.tmp()
                P.emit("dve", lambda e, z=z, c=c: e.tensor_tensor(out=z[:], in0=y[:, c, :], in1=mean, op=ALU.subtract),
                       reads=[yR[c], meanR], writes=[zR])
                P.emit("dve", lambda e, z=z: e.tensor_tensor(out=z[:], in0=z[:], in1=var, op=ALU.mult), reads=[zR, varR], writes=[zR])
                lg = self.parm[:, pc["lg"] + e_ * 4 + c: pc["lg"] + e_ * 4 + c + 1]
                lb = self.parm[:, pc["lb"] + e_ * 4 + c: pc["lb"] + e_ * 4 + c + 1]
                P.emit("act", lambda e, z=z, c=c, lg=lg, lb=lb: e.activation(out=catC[:, c, :], in_=z[:], func=AF.Silu, bias=lb, scale=lg),
                       reads=[zR, self.cR], writes=[catCR[c]])
            if getattr(self, "dbg", 0):
                srcs = {1: [catA[:, c, :] for c in range(4)] + [catC[:, c, :] for c in range(4)],
                        2: [glu[:, c, 30:30 + TT] for c in range(4)] + [y[:, c, :] for c in range(4)],
                        3: [qT[:, c, :] for c in range(4)] + [kTb[:, 1:5, :], self.hnm[:, 0, :], self.hnm[:, 1, :], self.hnm[:, 7, :]],
                        4: [mean_b, var_b, self.sq[:, 0, :], self.sq[:, 4, :]] + [catC[:, c, :] for c in range(4)],
                        5: [self.EBT[:, 0:512], self.EBT[:, 1024:1536], PT[0], PT[1], PT[2], PT[3], catA[:, 0, :], catA[:, 1, :]]}[self.dbg]
                allR = catAR + catCR + gluR + yR + qR + [kR] + self.hnmR + [meanR, varR, self.sqR, self.cR] + PTR
                for c in range(8):
                    dst = self.h[:, c, tr]
                    if self.dbg == 3 and c == 4:
                        dst = dst.rearrange("p (a b) -> p a b", a=4)
                    P.emit("dve", lambda e, c=c, dst=dst: e.tensor_copy(out=dst, in_=srcs[c]), reads=allR + [self.Hr[c][t]], writes=[self.Hr[c][t]])
                continue
            for c in range(DC):
                ps, pr = self.bank()
                for j in range(4):
                    P.emit("pe", lambda e, j=j, c=c, ps=ps: e.matmul(ps[:], woA[:, j, c * 128:(c + 1) * 128], catA[:, j, :], start=(j == 0), stop=False),
                           reads=[SB] + catAR, writes=[pr])
                for j in range(4):
                    P.emit("pe", lambda e, j=j, c=c, ps=ps: e.matmul(ps[:], woC[:, j, c * 128:(c + 1) * 128], catC[:, j, :], start=False, stop=(j == 3)),
                           reads=[SB, catCR[j]], writes=[pr])
                P.emit("dve", lambda e, c=c, ps=ps, tr=tr: e.tensor_tensor(out=self.h[:, c, tr], in0=ps[:], in1=self.h[:, c, tr], op=ALU.add),
                       reads=[pr, self.Hr[c][t]], writes=[self.Hr[c][t]], nosame=True)
        P.barrier(("pe", "act", "dve"))

    def mix_odd(self, l):
        P, dr = self.P, self.dr
        pc = self.pcol
        o_ = l // 2
        wsl = self.wsl
        SA, SB = self.SA, self.SB
        wig = wsl[:, 0:8192].rearrange("p (kc f) -> p kc f", kc=8)
        wir = wsl[:, 8192:16384].rearrange("p (kc f) -> p kc f", kc=8)
        wo = wsl[:, 16384:24576].rearrange("p (j d) -> p j d", j=8)
        win = dr["odd_w_in"][o_]
        wout = dr["odd_w_out"][o_]
        P.emit("pool", lambda e: e.dma_start(out=wig, in_=win[:, 0:1024].rearrange("(kc p) f -> p kc f", p=128)), writes=[SA], dma=("w", 8))
        P.emit("pool", lambda e: e.dma_start(out=wir, in_=win[:, 1024:2048].rearrange("(kc p) f -> p kc f", p=128)), writes=[SA, SB], dma=("w", 8))
        P.emit("pool", lambda e: e.dma_start(out=wo, in_=wout.rearrange("(j p) n -> p j n", p=128)), writes=[SB], dma=("w", 8))
        gwR = self.gwR
        P.emit("pool", lambda e: e.dma_start(out=self.gw[:, 0, :, :], in_=dr["gate_a_w"][o_].rearrange("h i j -> i h j")), writes=[gwR], dma=("gw", 2))
        P.emit("pool", lambda e: e.dma_start(out=self.gw[:, 1, :, :], in_=dr["gate_x_w"][o_].rearrange("h i j -> i h j")), writes=[gwR], dma=("gw", 2))
        o = 0

        def alloc(shape, dt):
            nonlocal o
            v = self.view(o, shape, dt)
            o += int(np.prod(shape)) * (4 if dt == F32 else 2)
            o = (o + 63) // 64 * 64
            return v

        gate = alloc([DC, TT], BF16)
        mo = alloc([DC, TT], BF16)
        xr = [alloc([TT + 4], F32) for _ in range(2)]
        xcb = [alloc([TT], BF16) for _ in range(2)]
        halo = alloc([DC, 4], F32)
        carry = alloc([DC], F32)
        NB = 2
        xc = [alloc([TT], F32) for _ in range(NB)]
        bA = [alloc([TT], F32) for _ in range(NB)]
        bB = [alloc([TT], F32) for _ in range(NB)]
        bC = [alloc([TT], F32) for _ in range(NB)]
        hs = [alloc([TT], F32) for _ in range(NB)]
        assert o <= 56 * 1024
        gateR, moR = [Res() for _ in range(DC)], [Res() for _ in range(DC)]
        xrR, xcbR = [Res(), Res()], [Res(), Res()]
        haloR, carryR = [Res() for _ in range(DC)], [Res() for _ in range(DC)]
        xcR, AR_, BR, CR, hsR = ([Res() for _ in range(NB)] for _ in range(5))
        P.emit("dve", lambda e: e.memset(halo[:], 0.0), writes=haloR)
        P.emit("dve", lambda e: e.memset(carry[:], 0.0), writes=carryR)
        der = self.der
        for t in range(NT):
            tr = slice(t * TT, (t + 1) * TT)
            self.rmsnorm(t, pc["nm"] + l * 8, lambda c: self.hnm[:, c, :], self.hnmR)
            for c in range(DC):
                ps, pr = self.bank()
                for kc in range(DC):
                    P.emit("pe", lambda e, kc=kc, c=c, ps=ps: e.matmul(
                        ps[:], wig[:, kc, c * 128:(c + 1) * 128], self.hnm[:, kc, :], start=(kc == 0), stop=(kc == DC - 1)),
                        reads=[SA, self.hnmR[kc]], writes=[pr])
                P.emit("act", lambda e, c=c, ps=ps: e.activation(out=gate[:, c, :], in_=ps[:], func=AF.Gelu_apprx_tanh),
                       reads=[pr], writes=[gateR[c]])
            for c in range(DC):
                k = c % 2
                ps, pr = self.bank()
                for kc in range(DC):
                    P.emit("pe", lambda e, kc=kc, c=c, ps=ps: e.matmul(
                        ps[:], wir[:, kc, c * 128:(c + 1) * 128], self.hnm[:, kc, :], start=(kc == 0), stop=(kc == DC - 1)),
                        reads=[SA, SB, self.hnmR[kc]], writes=[pr])
                P.emit("act", lambda e, k=k, ps=ps: e.activation(out=xr[k][:, 3:3 + TT], in_=ps[:], func=AF.Copy), reads=[pr], writes=[xrR[k]])
                P.emit("dve", lambda e, k=k, c=c: e.tensor_copy(out=xr[k][:, 0:3], in_=halo[:, c, 0:3]), reads=[haloR[c]], writes=[xrR[k]])
                P.emit("act", lambda e, k=k, c=c: e.activation(out=halo[:, c, 0:3], in_=xr[k][:, TT:TT + 3], func=AF.Copy), reads=[xrR[k]], writes=[haloR[c]])
                wc = pc["lw"] + (o_ * 8 + c) * 4
                cb = self.parm[:, pc["lcb"] + o_ * 8 + c: pc["lcb"] + o_ * 8 + c + 1]
                P.emit("dve", lambda e, k=k, wc=wc, cb=cb: e.tensor_scalar(
                    out=xc[k], in0=xr[k][:, 3:3 + TT], scalar1=self.parm[:, wc + 3:wc + 4], scalar2=cb, op0=ALU.mult, op1=ALU.add),
                    reads=[xrR[k], self.cR], writes=[xcR[k]])
                for j in range(3):
                    P.emit("dve", lambda e, k=k, j=j, wc=wc: e.scalar_tensor_tensor(
                        out=xc[k], in0=xr[k][:, j:j + TT], scalar=self.parm[:, wc + j:wc + j + 1], in1=xc[k], op0=ALU.mult, op1=ALU.add),
                        reads=[xrR[k], xcR[k]], writes=[xcR[k]])
                P.emit("act", lambda e, k=k: e.activation(out=xcb[k], in_=xc[k], func=AF.Copy), reads=[xcR[k]], writes=[xcbR[k]])
                psr, prr = self.bank()
                psi, pri = self.bank()
                P.emit("pe", lambda e, k=k, c=c, psr=psr: e.matmul(psr[:], self.gw[:, 0, c, :], xcb[k], start=True, stop=True), reads=[gwR, xcbR[k]], writes=[prr])
                P.emit("pe", lambda e, k=k, c=c, psi=psi: e.matmul(psi[:], self.gw[:, 1, c, :], xcb[k], start=True, stop=True), reads=[gwR, xcbR[k]], writes=[pri])
                gab = self.parm[:, pc["gab"] + o_ * 8 + c: pc["gab"] + o_ * 8 + c + 1]
                gxb = self.parm[:, pc["gxb"] + o_ * 8 + c: pc["gxb"] + o_ * 8 + c + 1]
                cl = der[:, o_ * 8 + c: o_ * 8 + c + 1]
                cl2 = der[:, 16 + o_ * 8 + c: 16 + o_ * 8 + c + 1]
                one = self.parm[:, pc["eps"] + 2:pc["eps"] + 3]
                P.emit("act", lambda e, k=k, psr=psr, gab=gab: e.activation(out=bA[k], in_=psr[:], func=AF.Sigmoid, bias=gab, scale=1.0), reads=[prr, self.cR], writes=[AR_[k]])
                P.emit("act", lambda e, k=k, psi=psi, gxb=gxb: e.activation(out=bC[k], in_=psi[:], func=AF.Sigmoid, bias=gxb, scale=1.0), reads=[pri, self.cR], writes=[CR[k]])
                P.emit("act", lambda e, k=k, cl2=cl2: e.activation(out=bB[k], in_=bA[k], func=AF.Exp, scale=cl2), reads=[AR_[k]], writes=[BR[k]])
                P.emit("act", lambda e, k=k, cl=cl: e.activation(out=bA[k], in_=bA[k], func=AF.Exp, scale=cl), reads=[AR_[k], BR[k]], writes=[AR_[k]])
                P.emit("act", lambda e, k=k, one=one: e.activation(out=bB[k], in_=bB[k], func=AF.Sqrt, bias=one, scale=-1.0), reads=[BR[k], self.cR], writes=[BR[k]])
                P.emit("dve", lambda e, k=k: e.tensor_tensor(out=bC[k], in0=bC[k], in1=xc[k], op=ALU.mult), reads=[CR[k], xcR[k]], writes=[CR[k]])
                P.emit("dve", lambda e, k=k: e.tensor_tensor(out=bC[k], in0=bC[k], in1=bB[k], op=ALU.mult), reads=[CR[k], BR[k]], writes=[CR[k]])
                P.emit("dve", lambda e, k=k, c=c: e.tensor_tensor_scan(out=hs[k], data0=bA[k], data1=bC[k], initial=carry[:, c:c + 1], op0=ALU.mult, op1=ALU.add),
                       reads=[AR_[k], CR[k], carryR[c]], writes=[hsR[k]])
                P.emit("act", lambda e, k=k, c=c: e.activation(out=carry[:, c:c + 1], in_=hs[k][:, TT - 1:TT], func=AF.Copy), reads=[hsR[k]], writes=[carryR[c]])
                P.emit("dve", lambda e, k=k, c=c: e.tensor_tensor(out=mo[:, c, :], in0=hs[k], in1=gate[:, c, :], op=ALU.mult), reads=[hsR[k], gateR[c]], writes=[moR[c]])
            for c in range(DC):
                ps, pr = self.bank()
                for j in range(DC):
                    P.emit("pe", lambda e, j=j, c=c, ps=ps: e.matmul(ps[:], wo[:, j, c * 128:(c + 1) * 128], mo[:, j, :], start=(j == 0), stop=(j == DC - 1)),
                           reads=[SB, moR[j]], writes=[pr])
                P.emit("dve", lambda e, c=c, ps=ps, tr=tr: e.tensor_tensor(out=self.h[:, c, tr], in0=ps[:], in1=self.h[:, c, tr], op=ALU.add),
                       reads=[pr, self.Hr[c][t]], writes=[self.Hr[c][t]], nosame=True)
        P.barrier(("pe", "act", "dve"))


def prep_shared(inp):
    params, pcol = pack_params(inp)
    biasT, maskneg = attn_consts(inp["rel_bias"])
    shared = {
        "params": params,
        "ident": np.eye(128, dtype=np.float32),
        "biasT": biasT.reshape(128, 2048),
        "maskneg": maskneg.reshape(128, 2048),
        "even_w_in": np.ascontiguousarray(np.asarray(inp["even_w_in"], np.float32)[:, :, qperm()]),
    }
    for k in ("ffn1_wg", "ffn1_wu", "ffn1_wd", "ffn2_wg", "ffn2_wu", "ffn2_wd", "even_w_out", "odd_w_in",
              "odd_w_out", "gate_a_w", "gate_x_w"):
        shared[k] = np.ascontiguousarray(np.asarray(inp[k], np.float32))
    return shared, pcol, params.shape[1]


def kernel(**inputs):
    x = np.asarray(inputs["x"], np.float32)
    shared, pcol, npar = prep_shared(inputs)
    n_seq = x.shape[0] // N_CORES
    b = Builder(n_seq, DEPTH)
    b.npar = npar
    nc = b.build(pcol)
    in_maps = []
    for c in range(N_CORES):
        m = dict(shared)
        m["x"] = np.ascontiguousarray(x[c * n_seq:(c + 1) * n_seq])
        in_maps.append(m)
    res = run_bass_kernel_spmd(nc, in_maps, core_ids=list(range(N_CORES)))
    return np.concatenate([r["out"] for r in res.results], axis=0)
```

```python
import math
from contextlib import ExitStack

import numpy as np
import concourse.bass as bass
import concourse.mybir as mybir
from concourse.bass_utils import run_bass_kernel_spmd

F32 = mybir.dt.float32
BF16 = mybir.dt.bfloat16
AF = mybir.ActivationFunctionType
ALU = mybir.AluOpType

N_CORES = 8
D = 1024
S = 2048
DEPTH = 4
DFF = 2816
DC = 8
FC = 22
TT = 512
NT = 4
EVEN_IN = 1792
RMS_EPS = 1e-6
LN_EPS = 1e-5
GROUPS = [(0, 4), (4, 8), (8, 12), (12, 16), (16, 19), (19, 22)]

ENGS = ("pe", "act", "dve", "pool", "sp")
SEM_CAP = 30000


class Res:
    __slots__ = ("w", "r", "pr")

    def __init__(self):
        self.w = []
        self.r = []
        self.pr = []


class Op:
    __slots__ = ("eng", "fn", "deps", "dma", "signal", "tok", "waits")

    def __init__(self, eng, fn, dma):
        self.eng = eng
        self.fn = fn
        self.dma = dma
        self.deps = []
        self.signal = dma is not None
        self.tok = None
        self.waits = None


class Prog:
    def __init__(self, nc):
        self.nc = nc
        self.ops = {e: [] for e in ENGS}
        self.last = {e: None for e in ENGS}
        self.pend = {}
        self.dmacnt = {}

    def emit(self, eng, fn, reads=(), writes=(), dma=None, nosame=False):
        if dma is not None:
            name, n = dma
            j = self.dmacnt.get(name, 0)
            self.dmacnt[name] = j + 1
            dma = (name, j % n)
        op = Op(eng, fn, dma)
        deps = {}
        for r in reads:
            for o in r.w:
                deps[id(o)] = o
        joins = []
        for w in writes:
            if dma is not None and w.w and not w.r and all(o.dma is not None for o in w.w):
                joins.append(w)
                for o in w.pr:
                    deps[id(o)] = o
                continue
            for o in w.w:
                deps[id(o)] = o
            for o in w.r:
                deps[id(o)] = o
        for d in deps.values():
            if d.eng == eng and d.dma is None and dma is None:
                if eng == "pe" or nosame:
                    continue
            op.deps.append(d)
        if eng in self.pend:
            for d in self.pend.pop(eng):
                if d is not None and not (d.eng == eng and d.dma is None):
                    op.deps.append(d)
        for r in reads:
            r.r.append(op)
        for w in writes:
            if any(w is j for j in joins):
                w.w.append(op)
            else:
                w.pr = w.r
                w.w = [op]
                w.r = []
        self.ops[eng].append(op)
        self.last[eng] = op
        return op

    def barrier(self, engs):
        lasts = [self.last[e] for e in engs]
        for e in engs:
            self.pend[e] = list(self.pend.get(e, [])) + lasts

    def build(self, stack):
        nc = self.nc
        for e in ENGS:
            for op in self.ops[e]:
                for d in op.deps:
                    d.signal = True
        sems = {}
        cnt = {}
        for e in ENGS:
            c = 0
            for op in self.ops[e]:
                if op.dma is not None:
                    k = ("d",) + op.dma
                    cnt[k] = cnt.get(k, 0) + 16
                    op.tok = (k, cnt[k])
                elif op.signal:
                    c += 1
                    op.tok = (("e", e, (c - 1) // SEM_CAP), (c - 1) % SEM_CAP + 1)
        for e in ENGS:
            seen = {}
            for op in self.ops[e]:
                ws = {}
                for d in op.deps:
                    k, v = d.tok
                    if seen.get(k, 0) >= v:
                        continue
                    if ws.get(k, 0) < v:
                        ws[k] = v
                seen.update(ws)
                op.waits = list(ws.items())
                if op.tok is not None and op.tok[0] not in sems:
                    sems[op.tok[0]] = None
        for k in list(sems):
            sems[k] = stack.enter_context(nc.semaphore("s_" + "_".join(str(x) for x in k)))
        block = stack.enter_context(nc.Block())
        names = {"pe": "tensor", "act": "scalar", "dve": "vector", "pool": "gpsimd", "sp": "sync"}
        ops = self.ops

        def run(e, eng):
            for op in ops[e]:
                for k, v in op.waits:
                    eng.wait_ge(sems[k], v)
                ins = op.fn(eng)
                if op.tok is not None:
                    ins.then_inc(sems[op.tok[0]], 16 if op.dma is not None else 1)

        for e in ENGS:
            if not ops[e]:
                continue

            def f(eng, e=e):
                run(e, eng)

            getattr(block, names[e])(f)


def _pcols(v, nchunk):
    v = np.asarray(v, np.float32)
    lead = v.shape[:-1]
    v = v.reshape(lead + (nchunk, 128))
    v = np.moveaxis(v, -1, 0)
    return np.ascontiguousarray(v).reshape(128, -1)


class PL:
    pass


def pack_params(inp):
    cols = []
    off = {}

    def add(name, arr):
        off[name] = sum(c.shape[1] for c in cols)
        cols.append(np.ascontiguousarray(arr, dtype=np.float32))

    add("n1", _pcols(inp["norm_ffn1"], 8))
    add("nm", _pcols(inp["norm_mix"], 8))
    add("n2", _pcols(inp["norm_ffn2"], 8))
    add("nf", _pcols(inp["norm_final"], 8))
    cw = np.asarray(inp["conv_b_w"], np.float32)
    cw = cw.reshape(2, 31, 4, 128).transpose(3, 0, 2, 1)
    add("cw", cw.reshape(128, -1))
    add("cb", _pcols(inp["conv_b_b"], 4))
    add("lg", _pcols(inp["conv_ln_g"], 4))
    add("lb", _pcols(inp["conv_ln_b"], 4))
    lw = np.asarray(inp["lru_conv_w"], np.float32)
    lw = lw.reshape(2, 4, 8, 128).transpose(3, 0, 2, 1)
    add("lw", lw.reshape(128, -1))
    add("lcb", _pcols(inp["lru_conv_b"], 8))
    add("gab", _pcols(inp["gate_a_b"], 8))
    add("gxb", _pcols(inp["gate_x_b"], 8))
    add("lam", _pcols(inp["lru_lambda"], 8))
    sk = np.asarray(inp["attn_sinks"], np.float32).reshape(1, 16)
    add("sink", np.broadcast_to(sk, (128, 16)))
    add("eps", np.broadcast_to(np.array([[RMS_EPS, LN_EPS, 1.0, 0.25]], np.float32), (128, 4)))
    return np.concatenate(cols, axis=1), off


def t5_bucket_np(dist):
    n = np.maximum(dist, 0)
    max_exact = 16
    nf = np.maximum(n, max_exact).astype(np.float32)
    large = max_exact + (np.log(nf / np.float32(max_exact)) / np.float32(math.log(128 / max_exact))
                         * np.float32(32 - max_exact)).astype(np.int32)
    large = np.minimum(large, 31)
    return np.where(n < max_exact, n, large)


def attn_consts(rel_bias):
    q = np.arange(128)[None, :]
    s = np.arange(128)[:, None]
    out_b = np.zeros((128, 2, 8, 128), np.float32)
    out_m = np.zeros((128, 2, 8, 128), np.float32)
    rb = np.asarray(rel_bias, np.float32)
    for half in range(2):
        dist = q + 128 - (s + 128 * half)
        ok = (dist >= 0) & (dist < 128)
        bk = t5_bucket_np(dist)
        g = rb[bk]
        out_b[:, half] = np.transpose(g, (0, 2, 1))
        out_m[:, half] = np.where(ok, 0.0, -1e30)[:, None, :]
    return out_b, out_m


def qperm():
    idx = []
    for cq in range(4):
        idx += list(range(cq * 64, cq * 64 + 64))
        idx += list(range((4 + cq) * 64, (4 + cq) * 64 + 64))
    return np.array(idx + list(range(512, EVEN_IN)))


class Builder:
    def __init__(self, n_seq, layers, stages=("ffn1", "mix", "ffn2")):
        self.n_seq = n_seq
        self.layers = layers
        self.stages = stages
        self.nc = bass.Bass("TRN2", target_bir_lowering=False)
        self.nb = 0
        self.bank_pool = list(range(8))

    def bank(self):
        pool = self.bank_pool
        b = pool[self.nb % len(pool)]
        self.nb += 1
        return self.ps[b], self.psR[b]

    def tmp(self):
        k = self.ntmp % 2
        self.ntmp += 1
        return self.T[k], self.TR[k]

    def view(self, off_bytes, shape, dt, p0=0, p1=128):
        n = int(np.prod(shape))
        if dt == F32:
            assert off_bytes % 4 == 0
            a = self.arena32[p0:p1, off_bytes // 4: off_bytes // 4 + n]
        else:
            assert off_bytes % 2 == 0
            a = self.arena[p0:p1, off_bytes // 2: off_bytes // 2 + n]
        if len(shape) == 2:
            a = a.rearrange("p (a b) -> p a b", a=shape[0])
        elif len(shape) == 3:
            a = a.rearrange("p (a b c) -> p a b c", a=shape[0], b=shape[1])
        return a

    def build(self, pcol):
        nc = self.nc
        n_seq = self.n_seq
        self.pcol = pcol
        npar = self.npar
        dr = {}

        def din(name, shape):
            dr[name] = nc.dram_tensor(name, list(shape), F32, kind="ExternalInput").ap()

        din("x", [n_seq, S, D])
        din("params", [128, npar])
        din("ident", [128, 128])
        din("biasT", [128, 2048])
        din("maskneg", [128, 2048])
        for w in ("ffn1", "ffn2"):
            din(w + "_wg", [DEPTH, D, DFF])
            din(w + "_wu", [DEPTH, D, DFF])
            din(w + "_wd", [DEPTH, DFF, D])
        din("even_w_in", [2, D, EVEN_IN])
        din("even_w_out", [2, D, D])
        din("odd_w_in", [2, D, 2 * D])
        din("odd_w_out", [2, D, D])
        din("gate_a_w", [2, 8, 128, 128])
        din("gate_x_w", [2, 8, 128, 128])
        self.dr = dr
        self.out = nc.dram_tensor("out", [n_seq, S, D], F32, kind="ExternalOutput").ap()

        with ExitStack() as st:
            sb = lambda name, shape, dt: st.enter_context(nc.sbuf_tensor(name, shape, dt))
            self.h = sb("h", [128, DC, S], F32)
            self.wsl = sb("wsl", [128, 24576], BF16)
            self.arena = sb("arena", [128, 28 * 1024], BF16)
            self.arena32 = self.arena[:].bitcast(F32)
            self.hnm = sb("hnm", [128, DC, TT], BF16)
            self.parm = sb("parm", [128, npar], F32)
            self.der = sb("der", [128, 128], F32)
            self.ident = sb("ident_sb", [128, 128], F32)
            self.ones = sb("ones", [128, 128], BF16)
            self.identb = sb("identb", [128, 128], BF16)
            self.EBT = sb("EBT", [128, 2048], F32)
            self.sq = sb("sq", [128, DC, TT], BF16)
            self.gw = sb("gw", [128, 2, 8, 128], BF16)
            self.sinkrow = self.gw[0:1, :, :, :].rearrange("p a b c -> p (a b) c")
            self.T = [sb("T%d" % i, [128, TT], F32) for i in range(4)]
            self.TR = [Res() for _ in range(4)]
            self.ntmp = 0
            self.ps = [st.enter_context(nc.psum_tensor("ps%d" % i, [128, TT], F32)) for i in range(8)]
            self.psR = [Res() for _ in range(8)]
            self.hnf = self.view(0, [DC, S], BF16)
            self.AR = 32 * 1024

            self.P = Prog(nc)
            self.Hr = [[Res() for _ in range(NT)] for _ in range(DC)]
            self.hnfR = [[Res() for _ in range(NT)] for _ in range(DC)]
            self.hnmR = [Res() for _ in range(DC)]
            self.sqR = Res()
            self.SA, self.SB = Res(), Res()
            self.cR = Res()
            self.gwR = Res()

            self.setup()
            for i in range(n_seq):
                self.load(i)
                for l in range(self.layers):
                    if "ffn1" in self.stages:
                        self.ffn(l, "ffn1", pcol["n1"] + l * 8)
                    if "mix" in self.stages:
                        if l % 2 == 0:
                            self.mix_even(l)
                        else:
                            self.mix_odd(l)
                    if "ffn2" in self.stages:
                        self.ffn(l, "ffn2", pcol["n2"] + l * 8)
                self.store(i)
            P = self.P
            fin = P.emit("sp", lambda e: e.nop(), reads=self.outR, writes=self.outR)
            for o in self.out_ops[-4:]:
                if o not in fin.deps:
                    fin.deps.append(o)
            P.build(st)
        return nc

    def setup(self):
        P, dr = self.P, self.dr
        pc = self.pcol
        parm, der = self.parm, self.der
        cR = self.cR
        self.outR = [Res(), Res()]
        self.out_ops = []
        P.emit("sp", lambda e: e.dma_start(out=parm[:], in_=dr["params"]), writes=[cR], dma=("c0", 1))
        P.emit("sp", lambda e: e.dma_start(out=self.ident[:], in_=dr["ident"]), writes=[cR], dma=("c1", 1))
        bt = self.view(self.AR, [2048], F32)
        mk = self.view(self.AR + 8192, [2048], F32)
        tR = Res()
        P.emit("sp", lambda e: e.dma_start(out=bt, in_=dr["biasT"]), writes=[tR], dma=("c2", 1))
        P.emit("sp", lambda e: e.dma_start(out=mk, in_=dr["maskneg"]), writes=[tR], dma=("c3", 1))
        P.emit("dve", lambda e: e.memset(self.ones[:], 1.0), writes=[cR])
        P.emit("dve", lambda e: e.tensor_copy(out=self.identb[:], in_=self.ident[:]), reads=[cR], writes=[cR])
        P.emit("dve", lambda e: e.tensor_tensor(out=bt, in0=bt, in1=mk, op=ALU.add), reads=[tR], writes=[tR])
        P.emit("act", lambda e: e.activation(out=self.EBT[:], in_=bt, func=AF.Exp), reads=[tR], writes=[cR])
        lam = parm[:, pc["lam"]:pc["lam"] + 16]
        dR = Res()
        P.emit("act", lambda e: e.activation(out=der[:, 0:16], in_=lam, func=AF.Exp, scale=-1.0), reads=[cR], writes=[dR])
        P.emit("dve", lambda e: e.tensor_scalar_add(out=der[:, 0:16], in0=der[:, 0:16], scalar1=1.0), reads=[dR], writes=[dR])
        P.emit("act", lambda e: e.activation(out=der[:, 0:16], in_=der[:, 0:16], func=AF.Ln), reads=[dR], writes=[dR])
        P.emit("dve", lambda e: e.tensor_scalar_mul(out=der[:, 16:32], in0=der[:, 0:16], scalar1=-16.0), reads=[dR], writes=[dR])
        P.emit("dve", lambda e: e.tensor_scalar_mul(out=der[:, 0:16], in0=der[:, 0:16], scalar1=-8.0), reads=[dR], writes=[dR])
        P.emit("act", lambda e: e.activation(out=der[:, 32:48], in_=parm[:, pc["sink"]:pc["sink"] + 16], func=AF.Exp),
               reads=[cR, dR], writes=[dR])
        P.emit("dve", lambda e: e.tensor_scalar_mul(out=der[:, 16:32], in0=der[:, 0:16], scalar1=0.5), reads=[dR], writes=[dR])
        P.emit("dve", lambda e: e.tensor_scalar_mul(out=der[:, 48:64], in0=parm[:, pc["gab"]:pc["gab"] + 16], scalar1=0.5), reads=[dR, cR], writes=[dR])
        P.emit("dve", lambda e: e.tensor_scalar_mul(out=der[:, 64:80], in0=parm[:, pc["gxb"]:pc["gxb"] + 16], scalar1=0.5), reads=[dR, cR], writes=[dR])
        last = P.emit("dve", lambda e: e.memset(self.T[0][:], 0.0), reads=[dR, tR], writes=[cR, tR, dR])
        P.barrier(("pe", "act", "dve", "sp"))

    def load(self, i):
        P = self.P
        x = self.dr["x"]
        xs = [self.view(self.AR + k * 4096, [D], F32) for k in range(2)]
        xsR = [Res(), Res()]
        for j in range(16):
            k = j % 2
            t = j // 4
            P.emit("sp", lambda e, j=j, k=k: e.dma_start(out=xs[k], in_=x[i, j * 128:(j + 1) * 128, :]),
                   writes=[xsR[k]], dma=("xin", 2))
            for half in range(2):
                ps, pr = self.bank()
                for q in range(4):
                    c = half * 4 + q
                    P.emit("pe", lambda e, ps=ps, q=q, c=c, k=k: e.transpose(
                        out=ps[:, q * 128:(q + 1) * 128], in_=xs[k][:, c * 128:(c + 1) * 128], identity=self.ident[:]),
                        reads=[xsR[k], self.cR], writes=[pr])
                eng = "dve" if half == 0 else "act"
                dst = self.h[:, half * 4:(half + 1) * 4, j * 128:(j + 1) * 128]
                src = ps[:].rearrange("p (a b) -> p a b", a=4)
                if eng == "dve":
                    fn = lambda e, dst=dst, src=src: e.tensor_copy(out=dst, in_=src)
                else:
                    fn = lambda e, dst=dst, src=src: e.activation(out=dst, in_=src, func=AF.Copy)
                P.emit(eng, fn, reads=[pr], writes=[self.Hr[c2][t] for c2 in range(half * 4, half * 4 + 4)], nosame=True)
        P.barrier(("pe", "act", "dve", "sp"))

    def store(self, i):
        P = self.P
        pc = self.pcol
        hn = self.view(self.AR, [DC, TT], F32)
        ys = [self.view(self.AR + 16384 + k * 4096, [D], F32) for k in range(2)]
        hnR = [Res() for _ in range(DC)]
        ysR = self.outR
        for t in range(NT):
            if getattr(self, "dbg", False):
                for c in range(DC):
                    P.emit("dve", lambda e, c=c, t=t: e.tensor_copy(out=hn[:, c, :], in_=self.h[:, c, t * TT:(t + 1) * TT]),
                           reads=[self.Hr[c][t]], writes=[hnR[c]])
            else:
                self.rmsnorm(t, pc["nf"], lambda c: hn[:, c, :], hnR)
            for blk in range(4):
                j = t * 4 + blk
                k = j % 2
                for half in range(2):
                    ps, pr = self.bank()
                    for q in range(4):
                        c = half * 4 + q
                        P.emit("pe", lambda e, ps=ps, q=q, c=c, blk=blk: e.transpose(
                            out=ps[:, q * 128:(q + 1) * 128], in_=hn[:, c, blk * 128:(blk + 1) * 128],
                            identity=self.ident[:]), reads=[hnR[c], self.cR], writes=[pr])
                    dst = ys[k][:, half * 512:(half + 1) * 512]
                    if half == 0:
                        P.emit("act", lambda e, dst=dst, ps=ps: e.activation(out=dst, in_=ps[:], func=AF.Copy),
                               reads=[pr], writes=[ysR[k]], nosame=True)
                    else:
                        P.emit("dve", lambda e, dst=dst, ps=ps: e.tensor_copy(out=dst, in_=ps[:]),
                               reads=[pr], writes=[ysR[k]], nosame=True)
                o = P.emit("sp", lambda e, j=j, k=k: e.dma_start(out=self.out[i, j * 128:(j + 1) * 128, :], in_=ys[k]),
                           reads=[ysR[k]], dma=("yout", 2))
                self.out_ops.append(o)
        P.barrier(("pe", "act", "dve", "sp"))

    def rmsnorm(self, t, gcol, outf, outR):
        P = self.P
        pc = self.pcol
        tr = slice(t * TT, (t + 1) * TT)
        hR = [self.Hr[c][t] for c in range(DC)]
        P.emit("act", lambda e: e.activation(out=self.sq[:], in_=self.h[:, :, tr], func=AF.Square),
               reads=hR, writes=[self.sqR])
        ps, pr = self.bank()
        for c in range(DC):
            P.emit("pe", lambda e, c=c, ps=ps: e.matmul(ps[:], self.ones[:], self.sq[:, c, :], start=(c == 0), stop=(c == DC - 1)),
                   reads=[self.sqR, self.cR], writes=[pr])
        rs, rr = self.tmp()
        eps = self.parm[:, pc["eps"]:pc["eps"] + 1]
        P.emit("act", lambda e, ps=ps, rs=rs: e.activation(out=rs[:], in_=ps[:], func=AF.Sqrt, bias=eps, scale=1.0 / D),
               reads=[pr, self.cR], writes=[rr])
        P.emit("dve", lambda e, rs=rs: e.reciprocal(out=rs[:], in_=rs[:]), reads=[rr], writes=[rr])
        for c in range(DC):
            g = self.parm[:, gcol + c:gcol + c + 1]
            P.emit("dve", lambda e, c=c, g=g, rs=rs: e.scalar_tensor_tensor(
                out=outf(c), in0=self.h[:, c, tr], scalar=g, in1=rs[:], op0=ALU.mult, op1=ALU.mult),
                reads=[hR[c], rr, self.cR], writes=[outR[c]], nosame=True)

    def wslot(self, k):
        return self.wsl[:, k * 12288:(k + 1) * 12288]

    def load_ffn_group(self, l, which, gi, slot):
        P, dr = self.P, self.dr
        f0, f1 = GROUPS[gi]
        nf = f1 - f0
        gwid = nf * 128
        sl = self.wslot(slot)
        R = self.SA if slot == 0 else self.SB
        wg = dr[which + "_wg"][l, :, f0 * 128:f1 * 128].rearrange("(kc p) f -> p kc f", p=128)
        wu = dr[which + "_wu"][l, :, f0 * 128:f1 * 128].rearrange("(kc p) f -> p kc f", p=128)
        wd = dr[which + "_wd"][l, f0 * 128:f1 * 128, :].rearrange("(fc p) d -> p fc d", p=128)
        og = sl[:, 0:8 * gwid].rearrange("p (kc f) -> p kc f", kc=8)
        ou = sl[:, 4096:4096 + 8 * gwid].rearrange("p (kc f) -> p kc f", kc=8)
        od = sl[:, 8192:8192 + nf * 1024].rearrange("p (fc d) -> p fc d", fc=nf)
        P.emit("pool", lambda e: e.dma_start(out=og, in_=wg), writes=[R], dma=("w", 8))
        P.emit("pool", lambda e: e.dma_start(out=ou, in_=wu), writes=[R], dma=("w", 8))
        P.emit("pool", lambda e: e.dma_start(out=od, in_=wd), writes=[R], dma=("w", 8))
        return og, ou, od, R

    def ffn(self, l, which, gcol):
        P = self.P
        act = [self.view(self.AR + k * 4096, [4, TT], BF16) for k in range(2)]
        actR = [[Res() for _ in range(4)] for _ in range(2)]
        stm = [self.view(self.AR + 8192 + k * 2048, [TT], F32) for k in range(2)]
        stR = [Res(), Res()]
        nst = [0]
        pieces = {}
        pieces[0] = self.load_ffn_group(l, which, 0, 0)

        def up(gi, t):
            og, ou, od, R = pieces[gi]
            f0, f1 = GROUPS[gi]
            tr = slice(t * TT, (t + 1) * TT)
            for i in range(f1 - f0):
                psg, prg = self.bank()
                psu, pru = self.bank()
                for kc in range(DC):
                    P.emit("pe", lambda e, kc=kc, i=i, psg=psg: e.matmul(
                        psg[:], og[:, kc, i * 128:(i + 1) * 128], self.hnf[:, kc, tr], start=(kc == 0), stop=(kc == DC - 1)),
                        reads=[R, self.hnfR[kc][t]], writes=[prg])
                for kc in range(DC):
                    P.emit("pe", lambda e, kc=kc, i=i, psu=psu: e.matmul(
                        psu[:], ou[:, kc, i * 128:(i + 1) * 128], self.hnf[:, kc, tr], start=(kc == 0), stop=(kc == DC - 1)),
                        reads=[R, self.hnfR[kc][t]], writes=[pru])
                k = nst[0] % 2
                nst[0] += 1
                P.emit("act", lambda e, k=k, psg=psg: e.activation(out=stm[k], in_=psg[:], func=AF.Silu),
                       reads=[prg], writes=[stR[k]])
                P.emit("dve", lambda e, k=k, i=i, psu=psu, t=t: e.tensor_tensor(
                    out=act[t % 2][:, i, :], in0=stm[k], in1=psu[:], op=ALU.mult),
                    reads=[stR[k], pru], writes=[actR[t % 2][i]])

        def down(gi, t):
            og, ou, od, R = pieces[gi]
            f0, f1 = GROUPS[gi]
            nf = f1 - f0
            tr = slice(t * TT, (t + 1) * TT)
            for c in range(DC):
                ps, pr = self.bank()
                for i in range(nf):
                    P.emit("pe", lambda e, i=i, c=c, ps=ps: e.matmul(
                        ps[:], od[:, i, c * 128:(c + 1) * 128], act[t % 2][:, i, :], start=(i == 0), stop=(i == nf - 1)),
                        reads=[R, actR[t % 2][i]], writes=[pr])
                P.emit("dve", lambda e, c=c, ps=ps: e.scalar_tensor_tensor(
                    out=self.h[:, c, tr], in0=ps[:], scalar=0.5, in1=self.h[:, c, tr], op0=ALU.mult, op1=ALU.add),
                    reads=[pr, self.Hr[c][t]], writes=[self.Hr[c][t]], nosame=True)

        for gi in range(len(GROUPS)):
            if gi + 1 < len(GROUPS):
                pieces[gi + 1] = self.load_ffn_group(l, which, gi + 1, (gi + 1) % 2)
            for t in range(NT):
                if gi == 0:
                    self.rmsnorm(t, gcol, lambda c, t=t: self.hnf[:, c, t * TT:(t + 1) * TT], [self.hnfR[c][t] for c in range(DC)])
                up(gi, t)
                if t > 0:
                    down(gi, t - 1)
            down(gi, NT - 1)
        P.barrier(("pe", "act", "dve"))

    def mix_even(self, l):
        P, dr = self.P, self.dr
        pc = self.pcol
        e_ = l // 2
        wsl = self.wsl
        SA, SB = self.SA, self.SB
        lastpe = P.last["pe"]
        wqkv = wsl[:, 0:6144].rearrange("p (kc f) -> p kc f", kc=8)
        wab = wsl[:, 6144:14336].rearrange("p (kc f) -> p kc f", kc=8)
        woA = wsl[:, 16384:20480].rearrange("p (j d) -> p j d", j=4)
        woC = wsl[:, 20480:24576].rearrange("p (j d) -> p j d", j=4)
        win = dr["even_w_in"][e_]
        wout = dr["even_w_out"][e_]
        P.emit("pool", lambda e: e.dma_start(out=wqkv, in_=win[:, 0:768].rearrange("(kc p) f -> p kc f", p=128)),
               writes=[SA], dma=("w", 8))
        P.emit("pool", lambda e: e.dma_start(out=wab, in_=win[:, 768:1792].rearrange("(kc p) f -> p kc f", p=128)),
               writes=[SA, SB], dma=("w", 8))
        for g in range(2):
            P.emit("pool", lambda e, g=g: e.dma_start(
                out=woA[g * 64:(g + 1) * 64, :, :], in_=wout[g * 256:(g + 1) * 256, :].rearrange("(j d) n -> d j n", d=64)),
                writes=[SB], dma=("w", 8))
        P.emit("pool", lambda e: e.dma_start(out=woC, in_=wout[512:1024, :].rearrange("(j p) n -> p j n", p=128)),
               writes=[SB], dma=("w", 8))
        o = 0

        def alloc(shape, dt):
            nonlocal o
            v = self.view(o, shape, dt)
            o += int(np.prod(shape)) * (4 if dt == F32 else 2)
            o = (o + 63) // 64 * 64
            return v

        dg = alloc([124, 128], BF16)
        qT = alloc([4, TT], BF16)
        kTb = alloc([5, 128], BF16)
        vb = alloc([5, 128], BF16)
        glub = alloc([4, 30 + TT], BF16)
        PT = [alloc([TT], BF16) for _ in range(4)]
        catA = alloc([4, TT], BF16)
        catC = alloc([4, TT], BF16)
        assert o <= 56 * 1024, o
        mean, var = self.T[2][:], self.T[3][:]
        meanR, varR = self.TR[2], self.TR[3]
        qR = [Res() for _ in range(4)]
        kR, vR, gluR = Res(), Res(), [Res() for _ in range(4)]
        PTR = [Res() for _ in range(4)]
        catAR, catCR = [Res() for _ in range(4)], [Res() for _ in range(4)]
        dgR = [[Res(), Res()] for _ in range(4)]
        npt = [0]
        cw0 = pc["cw"] + e_ * 4 * 31
        for idx in range(124):
            c = idx // 31
            which = idx % 2
            wcol_ = self.parm[:, cw0 + idx:cw0 + idx + 1]
            if which:
                P.emit("act", lambda e, idx=idx, wcol_=wcol_: e.activation(out=dg[:, idx, :], in_=self.identb[:], func=AF.Copy, scale=wcol_),
                       reads=[self.cR], writes=[dgR[c][which]], nosame=True)
            else:
                P.emit("dve", lambda e, idx=idx, wcol_=wcol_: e.tensor_scalar_mul(out=dg[:, idx, :], in0=self.identb[:], scalar1=wcol_),
                       reads=[self.cR], writes=[dgR[c][which]], nosame=True)
        P.emit("dve", lambda e: e.memset(glub[:, :, 0:30], 0.0), writes=gluR)
        sinkrow = self.sinkrow
        for i in range(8):
            P.emit("dve", lambda e, i=i: e.tensor_scalar_mul(out=sinkrow[0:1, e_ * 8 + i, :], in0=self.ones[0:1, :],
                                                             scalar1=self.der[0:1, 32 + e_ * 8 + i:33 + e_ * 8 + i]),
                   reads=[self.cR], writes=[self.gwR], nosame=(i > 0))
        self.bank_pool = [0, 1, 2, 3]
        psY = [self.ps[4 + c] for c in range(4)]
        psYR = [self.psR[4 + c] for c in range(4)]
        self.rmsnorm(0, pc["nm"] + l * 8, lambda c: self.hnm[:, c, :], self.hnmR)
        for t in range(NT):
            tr = slice(t * TT, (t + 1) * TT)
            for cq in range(4):
                ps, pr = self.bank()
                for kc in range(DC):
                    P.emit("pe", lambda e, kc=kc, cq=cq, ps=ps: e.matmul(
                        ps[:], wqkv[:, kc, cq * 128:(cq + 1) * 128], self.hnm[:, kc, :], start=(kc == 0), stop=(kc == DC - 1)),
                        reads=[SA, self.hnmR[kc]], writes=[pr])
                P.emit("act", lambda e, cq=cq, ps=ps: e.activation(out=qT[:, cq, :], in_=ps[:], func=AF.Copy, scale=0.125),
                       reads=[pr], writes=[qR[cq]])
            ps, pr = self.bank()
            for kc in range(DC):
                P.emit("pe", lambda e, kc=kc, ps=ps: e.matmul(
                    ps[:], wqkv[:, kc, 512:640], self.hnm[:, kc, :], start=(kc == 0), stop=(kc == DC - 1)),
                    reads=[SA, self.hnmR[kc]], writes=[pr])
            P.emit("act", lambda e, ps=ps: e.activation(out=kTb[:, 1:5, :], in_=ps[:].rearrange("p (a b) -> p a b", a=4), func=AF.Copy),
                   reads=[pr], writes=[kR])
            ps, pr = self.bank()
            for blk in range(4):
                for kc in range(DC):
                    P.emit("pe", lambda e, kc=kc, blk=blk, ps=ps: e.matmul(
                        ps[:, blk * 128:(blk + 1) * 128], self.hnm[:, kc, blk * 128:(blk + 1) * 128], wqkv[:, kc, 640:768],
                        start=(kc == 0), stop=(kc == DC - 1)),
                        reads=[SA, self.hnmR[kc]], writes=[pr])
            P.emit("act", lambda e, ps=ps: e.activation(out=vb[:, 1:5, :], in_=ps[:].rearrange("p (a b) -> p a b", a=4), func=AF.Copy),
                   reads=[pr], writes=[vR])
            for c in range(4):
                psa, pra = self.bank()
                psb, prb = self.bank()
                for kc in range(DC):
                    P.emit("pe", lambda e, kc=kc, c=c, psa=psa: e.matmul(
                        psa[:], wab[:, kc, c * 128:(c + 1) * 128], self.hnm[:, kc, :], start=(kc == 0), stop=(kc == DC - 1)),
                        reads=[SA, SB, self.hnmR[kc]], writes=[pra])
                for kc in range(DC):
                    P.emit("pe", lambda e, kc=kc, c=c, psb=psb: e.matmul(
                        psb[:], wab[:, kc, 512 + c * 128:512 + (c + 1) * 128], self.hnm[:, kc, :], start=(kc == 0), stop=(kc == DC - 1)),
                        reads=[SA, SB, self.hnmR[kc]], writes=[prb])
                sg, sgR = self.tmp()
                P.emit("act", lambda e, sg=sg, psb=psb: e.activation(out=sg[:], in_=psb[:], func=AF.Sigmoid), reads=[prb], writes=[sgR])
                P.emit("dve", lambda e, sg=sg, psa=psa, c=c: e.tensor_tensor(out=glub[:, c, 30:30 + TT], in0=sg[:], in1=psa[:], op=ALU.mult),
                       reads=[sgR, pra], writes=[gluR[c]])
            if t + 1 < NT:
                self.rmsnorm(t + 1, pc["nm"] + l * 8, lambda c: self.hnm[:, c, :], self.hnmR)
            for n in range(4):
                nbk = t * 4 + n
                halves = ([0] if nbk > 0 else []) + [1]
                psO, prO = self.bank()
                psD, prD = self.bank()
                psSs = [self.bank() for _ in range(2)]
                for gk in range(2):
                    p0, p1 = gk * 64, (gk + 1) * 64
                    units = []
                    for hi, half in enumerate(halves):
                        slot = n + half
                        psS, prS = psSs[hi]
                        P.emit("pe", lambda e, psS=psS, slot=slot, p0=p0, p1=p1, n=n: e.matmul(
                            psS[:].rearrange("p (a b) -> p a b", a=4), kTb[p0:p1, slot, :], qT[p0:p1, :, n * 128:(n + 1) * 128],
                            start=True, stop=True), reads=[kR] + qR, writes=[prS])
                        units.append((hi, half, slot, psS, prS))
                    if gk == 0:
                        c = n
                        for k in range(31):
                            P.emit("pe", lambda e, c=c, k=k: e.matmul(
                                psY[c][:], dg[:, c * 31 + k, :], glub[:, c, k:k + TT], start=(k == 0), stop=(k == 30)),
                                reads=[dgR[c][0], dgR[c][1], gluR[c]], writes=[psYR[c]])
                    pts = []
                    for hi, half, slot, psS, prS in units:
                        E, ER = self.tmp()
                        P.emit("act", lambda e, E=E, psS=psS: e.activation(out=E[:], in_=psS[:], func=AF.Exp),
                               reads=[prS], writes=[ER])
                        k = npt[0] % 4
                        npt[0] += 1
                        eb = self.EBT[:, half * 1024 + gk * 512: half * 1024 + (gk + 1) * 512]
                        P.emit("dve", lambda e, E=E, k=k, eb=eb: e.tensor_tensor(out=PT[k], in0=E[:], in1=eb, op=ALU.mult),
                               reads=[ER, self.cR], writes=[PTR[k]])
                        pts.append((hi, slot, k))
                    for hi, slot, k in pts:
                        first = hi == 0
                        last = hi == len(halves) - 1
                        P.emit("pe", lambda e, k=k, slot=slot, p0=p0, p1=p1, first=first, last=last, psO=psO: e.matmul(
                            psO[p0:p1, :], vb[:, slot, p0:p1], PT[k], start=first, stop=last),
                            reads=[vR, PTR[k]], writes=[prO])
                        P.emit("pe", lambda e, k=k, p0=p0, p1=p1, first=first, psD=psD: e.matmul(
                            psD[p0:p1, :], self.ones[:, 0:64], PT[k], start=first, stop=False),
                            reads=[self.cR, PTR[k]], writes=[prD])
                    sr = sinkrow[0:1, e_ * 8 + gk * 4: e_ * 8 + gk * 4 + 4, :]
                    P.emit("pe", lambda e, p0=p0, p1=p1, sr=sr, psD=psD: e.matmul(
                        psD[p0:p1, :].rearrange("p (a b) -> p a b", a=4), self.ones[0:1, 0:64], sr, start=False, stop=True),
                        reads=[self.cR, self.gwR], writes=[prD])
                rc, rcR = self.tmp()
                P.emit("dve", lambda e, rc=rc, psD=psD: e.reciprocal(out=rc[:], in_=psD[:]), reads=[prD], writes=[rcR])
                P.emit("dve", lambda e, rc=rc, psO=psO, n=n: e.tensor_tensor(
                    out=catA[:, :, n * 128:(n + 1) * 128], in0=psO[:].rearrange("p (a b) -> p a b", a=4),
                    in1=rc[:].rearrange("p (a b) -> p a b", a=4), op=ALU.mult),
                    reads=[prO, rcR], writes=catAR, nosame=True)
            if t < NT - 1:
                P.emit("act", lambda e: e.activation(out=kTb[:, 0, :], in_=kTb[:, 4, :], func=AF.Copy), reads=[kR], writes=[kR])
                P.emit("act", lambda e: e.activation(out=vb[:, 0, :], in_=vb[:, 4, :], func=AF.Copy), reads=[vR], writes=[vR])
                P.emit("act", lambda e: e.activation(out=glub[:, :, 0:30], in_=glub[:, :, TT:TT + 30], func=AF.Copy),
                       reads=gluR, writes=gluR)
            cbs = [self.parm[:, pc["cb"] + e_ * 4 + c: pc["cb"] + e_ * 4 + c + 1] for c in range(4)]
            for c in range(4):
                P.emit("act", lambda e, c=c: e.activation(out=self.sq[:, c, :], in_=psY[c][:], func=AF.Identity, bias=cbs[c], scale=1.0),
                       reads=[psYR[c], self.cR], writes=[self.sqR], nosame=(c > 0))
            for c in range(4):
                P.emit("act", lambda e, c=c: e.activation(out=self.sq[:, 4 + c, :], in_=psY[c][:], func=AF.Square, bias=cbs[c], scale=1.0),
                       reads=[psYR[c], self.cR], writes=[self.sqR], nosame=True)
            ps1, pr1 = self.bank()
            ps2, pr2 = self.bank()
            for c in range(4):
                P.emit("pe", lambda e, c=c, ps1=ps1: e.matmul(ps1[:], self.ones[:], self.sq[:, c, :], start=(c == 0), stop=(c == 3)),
                       reads=[self.sqR, self.cR], writes=[pr1])
            for c in range(4):
                P.emit("pe", lambda e, c=c, ps2=ps2: e.matmul(ps2[:], self.ones[:], self.sq[:, 4 + c, :], start=(c == 0), stop=(c == 3)),
                       reads=[self.sqR, self.cR], writes=[pr2])
            P.emit("dve", lambda e, ps1=ps1: e.tensor_scalar_mul(out=mean, in0=ps1[:], scalar1=1.0 / 512), reads=[pr1], writes=[meanR])
            P.emit("dve", lambda e: e.tensor_tensor(out=var, in0=mean, in1=mean, op=ALU.mult), reads=[meanR], writes=[varR])
            P.emit("dve", lambda e, ps2=ps2: e.scalar_tensor_tensor(
                out=var, in0=ps2[:], scalar=1.0 / 512, in1=var, op0=ALU.mult, op1=ALU.subtract), reads=[pr2, varR], writes=[varR])
            epsl = self.parm[:, pc["eps"] + 1:pc["eps"] + 2]
            P.emit("act", lambda e: e.activation(out=var, in_=var, func=AF.Sqrt, bias=epsl, scale=1.0), reads=[varR, self.cR], writes=[varR])
            P.emit("dve", lambda e: e.reciprocal(out=var, in_=var), reads=[varR], writes=[varR])
            for c in range(4):
                z, zR = self.tmp()
                P.emit("dve", lambda e, z=z, c=c: e.scalar_tensor_tensor(
                    out=z[:], in0=psY[c][:], scalar=cbs[c], in1=mean, op0=ALU.add, op1=ALU.subtract),
                    reads=[psYR[c], meanR, self.cR], writes=[zR])
                P.emit("dve", lambda e, z=z: e.tensor_tensor(out=z[:], in0=z[:], in1=var, op=ALU.mult), reads=[zR, varR], writes=[zR])
                lg = self.parm[:, pc["lg"] + e_ * 4 + c: pc["lg"] + e_ * 4 + c + 1]
                lb = self.parm[:, pc["lb"] + e_ * 4 + c: pc["lb"] + e_ * 4 + c + 1]
                P.emit("act", lambda e, z=z, c=c, lg=lg, lb=lb: e.activation(out=catC[:, c, :], in_=z[:], func=AF.Silu, bias=lb, scale=lg),
                       reads=[zR, self.cR], writes=[catCR[c]])
            for c in range(DC):
                ps, pr = self.bank()
                for j in range(4):
                    P.emit("pe", lambda e, j=j, c=c, ps=ps: e.matmul(ps[:], woA[:, j, c * 128:(c + 1) * 128], catA[:, j, :], start=(j == 0), stop=False),
                           reads=[SB] + catAR, writes=[pr])
                for j in range(4):
                    P.emit("pe", lambda e, j=j, c=c, ps=ps: e.matmul(ps[:], woC[:, j, c * 128:(c + 1) * 128], catC[:, j, :], start=False, stop=(j == 3)),
                           reads=[SB, catCR[j]], writes=[pr])
                P.emit("dve", lambda e, c=c, ps=ps, tr=tr: e.tensor_tensor(out=self.h[:, c, tr], in0=ps[:], in1=self.h[:, c, tr], op=ALU.add),
                       reads=[pr, self.Hr[c][t]], writes=[self.Hr[c][t]], nosame=True)
        self.bank_pool = list(range(8))
        P.barrier(("pe", "act", "dve"))

    def mix_even_old(self, l):
        P, dr = self.P, self.dr
        pc = self.pcol
        e_ = l // 2
        wsl = self.wsl
        SA, SB = self.SA, self.SB
        wqkv = wsl[:, 0:6144].rearrange("p (kc f) -> p kc f", kc=8)
        wab = wsl[:, 6144:14336].rearrange("p (kc f) -> p kc f", kc=8)
        woA = wsl[:, 16384:20480].rearrange("p (j d) -> p j d", j=4)
        woC = wsl[:, 20480:24576].rearrange("p (j d) -> p j d", j=4)
        win = dr["even_w_in"][e_]
        wout = dr["even_w_out"][e_]
        P.emit("pool", lambda e: e.dma_start(out=wqkv, in_=win[:, 0:768].rearrange("(kc p) f -> p kc f", p=128)),
               writes=[SA], dma=("w", 8))
        P.emit("pool", lambda e: e.dma_start(out=wab, in_=win[:, 768:1792].rearrange("(kc p) f -> p kc f", p=128)),
               writes=[SA, SB], dma=("w", 8))
        for g in range(2):
            P.emit("pool", lambda e, g=g: e.dma_start(
                out=woA[g * 64:(g + 1) * 64, :, :], in_=wout[g * 256:(g + 1) * 256, :].rearrange("(j d) n -> d j n", d=64)),
                writes=[SB], dma=("w", 8))
        P.emit("pool", lambda e: e.dma_start(out=woC, in_=wout[512:1024, :].rearrange("(j p) n -> p j n", p=128)),
               writes=[SB], dma=("w", 8))
        o = 0

        def alloc(shape, dt):
            nonlocal o
            v = self.view(o, shape, dt)
            o += int(np.prod(shape)) * (4 if dt == F32 else 2)
            o = (o + 63) // 64 * 64
            return v

        qT = alloc([4, TT], BF16)
        kTb = alloc([5, 128], BF16)
        vb = alloc([5, 128], BF16)
        glu = alloc([4, 30 + TT], F32)
        y = alloc([4, TT], F32)
        PT = [alloc([TT], BF16) for _ in range(4)]
        catA = alloc([4, TT], BF16)
        catC = alloc([4, TT], BF16)
        mean_b = alloc([TT], F32)
        var_b = alloc([TT], F32)
        meanR, varR = Res(), Res()
        assert o <= 56 * 1024
        qR = [Res() for _ in range(4)]
        kR, vR, gluR, yR = Res(), Res(), [Res() for _ in range(4)], [Res() for _ in range(4)]
        PTR = [Res() for _ in range(4)]
        catAR, catCR = [Res() for _ in range(4)], [Res() for _ in range(4)]
        npt = [0]
        cw0 = pc["cw"] + e_ * 4 * 31
        P.emit("dve", lambda e: e.memset(glu[:, :, 0:30], 0.0), writes=gluR)
        sinkrow = self.sinkrow
        for i in range(8):
            P.emit("dve", lambda e, i=i: e.tensor_scalar_mul(out=sinkrow[0:1, e_ * 8 + i, :], in0=self.ones[0:1, :],
                                                             scalar1=self.der[0:1, 32 + e_ * 8 + i:33 + e_ * 8 + i]),
                   reads=[self.cR], writes=[self.gwR], nosame=(i > 0))
        for t in range(NT):
            tr = slice(t * TT, (t + 1) * TT)
            self.rmsnorm(t, pc["nm"] + l * 8, lambda c: self.hnm[:, c, :], self.hnmR)
            for cq in range(4):
                ps, pr = self.bank()
                for kc in range(DC):
                    P.emit("pe", lambda e, kc=kc, cq=cq, ps=ps: e.matmul(
                        ps[:], wqkv[:, kc, cq * 128:(cq + 1) * 128], self.hnm[:, kc, :], start=(kc == 0), stop=(kc == DC - 1)),
                        reads=[SA, self.hnmR[kc]], writes=[pr])
                P.emit("act", lambda e, cq=cq, ps=ps: e.activation(out=qT[:, cq, :], in_=ps[:], func=AF.Copy, scale=0.125),
                       reads=[pr], writes=[qR[cq]])
            ps, pr = self.bank()
            for kc in range(DC):
                P.emit("pe", lambda e, kc=kc, ps=ps: e.matmul(
                    ps[:], wqkv[:, kc, 512:640], self.hnm[:, kc, :], start=(kc == 0), stop=(kc == DC - 1)),
                    reads=[SA, self.hnmR[kc]], writes=[pr])
            P.emit("act", lambda e, ps=ps: e.activation(out=kTb[:, 1:5, :], in_=ps[:].rearrange("p (a b) -> p a b", a=4), func=AF.Copy),
                   reads=[pr], writes=[kR])
            ps, pr = self.bank()
            for blk in range(4):
                for kc in range(DC):
                    P.emit("pe", lambda e, kc=kc, blk=blk, ps=ps: e.matmul(
                        ps[:, blk * 128:(blk + 1) * 128], self.hnm[:, kc, blk * 128:(blk + 1) * 128], wqkv[:, kc, 640:768],
                        start=(kc == 0), stop=(kc == DC - 1)),
                        reads=[SA, self.hnmR[kc]], writes=[pr])
            P.emit("act", lambda e, ps=ps: e.activation(out=vb[:, 1:5, :], in_=ps[:].rearrange("p (a b) -> p a b", a=4), func=AF.Copy),
                   reads=[pr], writes=[vR])
            for n in range(4):
                nbk = t * 4 + n
                halves = ([0] if nbk > 0 else []) + [1]
                psO, prO = self.bank()
                psD, prD = self.bank()
                for gk in range(2):
                    p0, p1 = gk * 64, (gk + 1) * 64
                    for hi, half in enumerate(halves):
                        slot = n + half
                        psS, prS = self.bank()
                        P.emit("pe", lambda e, psS=psS, slot=slot, p0=p0, p1=p1, n=n: e.matmul(
                            psS[:].rearrange("p (a b) -> p a b", a=4), kTb[p0:p1, slot, :], qT[p0:p1, :, n * 128:(n + 1) * 128],
                            start=True, stop=True), reads=[kR] + qR, writes=[prS])
                        E, ER = self.tmp()
                        P.emit("act", lambda e, E=E, psS=psS: e.activation(out=E[:], in_=psS[:], func=AF.Exp),
                               reads=[prS], writes=[ER])
                        k = npt[0] % 4
                        npt[0] += 1
                        eb = self.EBT[:, half * 1024 + gk * 512: half * 1024 + (gk + 1) * 512]
                        P.emit("dve", lambda e, E=E, k=k, eb=eb: e.tensor_tensor(out=PT[k], in0=E[:], in1=eb, op=ALU.mult),
                               reads=[ER, self.cR], writes=[PTR[k]])
                        first = hi == 0
                        last = hi == len(halves) - 1
                        P.emit("pe", lambda e, k=k, slot=slot, p0=p0, p1=p1, first=first, last=last, psO=psO: e.matmul(
                            psO[p0:p1, :], vb[:, slot, p0:p1], PT[k], start=first, stop=last),
                            reads=[vR, PTR[k]], writes=[prO])
                        P.emit("pe", lambda e, k=k, p0=p0, p1=p1, first=first, psD=psD: e.matmul(
                            psD[p0:p1, :], self.ones[:, 0:64], PT[k], start=first, stop=False),
                            reads=[self.cR, PTR[k]], writes=[prD])
                    sr = sinkrow[0:1, e_ * 8 + gk * 4: e_ * 8 + gk * 4 + 4, :]
                    P.emit("pe", lambda e, p0=p0, p1=p1, sr=sr, psD=psD: e.matmul(
                        psD[p0:p1, :].rearrange("p (a b) -> p a b", a=4), self.ones[0:1, 0:64], sr, start=False, stop=True),
                        reads=[self.cR, self.gwR], writes=[prD])
                rc, rcR = self.tmp()
                P.emit("dve", lambda e, rc=rc, psD=psD: e.reciprocal(out=rc[:], in_=psD[:]), reads=[prD], writes=[rcR])
                P.emit("dve", lambda e, rc=rc, psO=psO, n=n: e.tensor_tensor(
                    out=catA[:, :, n * 128:(n + 1) * 128], in0=psO[:].rearrange("p (a b) -> p a b", a=4),
                    in1=rc[:].rearrange("p (a b) -> p a b", a=4), op=ALU.mult),
                    reads=[prO, rcR], writes=catAR, nosame=True)
            if t < NT - 1:
                P.emit("act", lambda e: e.activation(out=kTb[:, 0, :], in_=kTb[:, 4, :], func=AF.Copy), reads=[kR], writes=[kR])
                P.emit("act", lambda e: e.activation(out=vb[:, 0, :], in_=vb[:, 4, :], func=AF.Copy), reads=[vR], writes=[vR])
            for c in range(4):
                psa, pra = self.bank()
                psb, prb = self.bank()
                for kc in range(DC):
                    P.emit("pe", lambda e, kc=kc, c=c, psa=psa: e.matmul(
                        psa[:], wab[:, kc, c * 128:(c + 1) * 128], self.hnm[:, kc, :], start=(kc == 0), stop=(kc == DC - 1)),
                        reads=[SA, SB, self.hnmR[kc]], writes=[pra])
                for kc in range(DC):
                    P.emit("pe", lambda e, kc=kc, c=c, psb=psb: e.matmul(
                        psb[:], wab[:, kc, 512 + c * 128:512 + (c + 1) * 128], self.hnm[:, kc, :], start=(kc == 0), stop=(kc == DC - 1)),
                        reads=[SA, SB, self.hnmR[kc]], writes=[prb])
                sg, sgR = self.tmp()
                P.emit("act", lambda e, sg=sg, psb=psb: e.activation(out=sg[:], in_=psb[:], func=AF.Sigmoid), reads=[prb], writes=[sgR])
                P.emit("dve", lambda e, sg=sg, psa=psa, c=c: e.tensor_tensor(out=glu[:, c, 30:30 + TT], in0=sg[:], in1=psa[:], op=ALU.mult),
                       reads=[sgR, pra], writes=[gluR[c]])
            for c in range(4):
                wcol = cw0 + c * 31
                cb = self.parm[:, pc["cb"] + e_ * 4 + c: pc["cb"] + e_ * 4 + c + 1]
                P.emit("dve", lambda e, c=c, wcol=wcol, cb=cb: e.tensor_scalar(
                    out=y[:, c, :], in0=glu[:, c, 30:30 + TT], scalar1=self.parm[:, wcol + 30:wcol + 31], scalar2=cb,
                    op0=ALU.mult, op1=ALU.add), reads=[gluR[c], self.cR], writes=[yR[c]])
                for k in range(30):
                    P.emit("dve", lambda e, c=c, k=k, wcol=wcol: e.scalar_tensor_tensor(
                        out=y[:, c, :], in0=glu[:, c, k:k + TT], scalar=self.parm[:, wcol + k:wcol + k + 1], in1=y[:, c, :],
                        op0=ALU.mult, op1=ALU.add), reads=[gluR[c], yR[c]], writes=[yR[c]])
            if t < NT - 1:
                P.emit("act", lambda e: e.activation(out=glu[:, :, 0:30], in_=glu[:, :, TT:TT + 30], func=AF.Copy),
                       reads=gluR, writes=gluR)
            P.emit("act", lambda e: e.activation(out=self.sq[:, 0:4, :], in_=y[:], func=AF.Copy), reads=yR, writes=[self.sqR])
            P.emit("act", lambda e: e.activation(out=self.sq[:, 4:8, :], in_=y[:], func=AF.Square), reads=yR, writes=[self.sqR])
            ps1, pr1 = self.bank()
            ps2, pr2 = self.bank()
            for c in range(4):
                P.emit("pe", lambda e, c=c, ps1=ps1: e.matmul(ps1[:], self.ones[:], self.sq[:, c, :], start=(c == 0), stop=(c == 3)),
                       reads=[self.sqR, self.cR], writes=[pr1])
            for c in range(4):
                P.emit("pe", lambda e, c=c, ps2=ps2: e.matmul(ps2[:], self.ones[:], self.sq[:, 4 + c, :], start=(c == 0), stop=(c == 3)),
                       reads=[self.sqR, self.cR], writes=[pr2])
            mean, var = mean_b, var_b
            P.emit("dve", lambda e, ps1=ps1: e.tensor_scalar_mul(out=mean, in0=ps1[:], scalar1=1.0 / 512), reads=[pr1], writes=[meanR])
            P.emit("dve", lambda e: e.tensor_tensor(out=var, in0=mean, in1=mean, op=ALU.mult), reads=[meanR], writes=[varR])
            P.emit("dve", lambda e, ps2=ps2: e.scalar_tensor_tensor(
                out=var, in0=ps2[:], scalar=1.0 / 512, in1=var, op0=ALU.mult, op1=ALU.subtract), reads=[pr2, varR], writes=[varR])
            epsl = self.parm[:, pc["eps"] + 1:pc["eps"] + 2]
            P.emit("act", lambda e: e.activation(out=var, in_=var, func=AF.Sqrt, bias=epsl, scale=1.0), reads=[varR, self.cR], writes=[varR])
            P.emit("dve", lambda e: e.reciprocal(out=var, in_=var), reads=[varR], writes=[varR])
            for c in range(4):
                z, zR = self.tmp()
                P.emit("dve", lambda e, z=z, c=c: e.tensor_tensor(out=z[:], in0=y[:, c, :], in1=mean, op=ALU.subtract),
                       reads=[yR[c], meanR], writes=[zR])
                P.emit("dve", lambda e, z=z: e.tensor_tensor(out=z[:], in0=z[:], in1=var, op=ALU.mult), reads=[zR, varR], writes=[zR])
                lg = self.parm[:, pc["lg"] + e_ * 4 + c: pc["lg"] + e_ * 4 + c + 1]
                lb = self.parm[:, pc["lb"] + e_ * 4 + c: pc["lb"] + e_ * 4 + c + 1]
                P.emit("act", lambda e, z=z, c=c, lg=lg, lb=lb: e.activation(out=catC[:, c, :], in_=z[:], func=AF.Silu, bias=lb, scale=lg),
                       reads=[zR, self.cR], writes=[catCR[c]])
            if getattr(self, "dbg", 0):
                srcs = {1: [catA[:, c, :] for c in range(4)] + [catC[:, c, :] for c in range(4)],
                        2: [glu[:, c, 30:30 + TT] for c in range(4)] + [y[:, c, :] for c in range(4)],
                        3: [qT[:, c, :] for c in range(4)] + [kTb[:, 1:5, :], self.hnm[:, 0, :], self.hnm[:, 1, :], self.hnm[:, 7, :]],
                        4: [mean_b, var_b, self.sq[:, 0, :], self.sq[:, 4, :]] + [catC[:, c, :] for c in range(4)],
                        5: [self.EBT[:, 0:512], self.EBT[:, 1024:1536], PT[0], PT[1], PT[2], PT[3], catA[:, 0, :], catA[:, 1, :]]}[self.dbg]
                allR = catAR + catCR + gluR + yR + qR + [kR] + self.hnmR + [meanR, varR, self.sqR, self.cR] + PTR
                for c in range(8):
                    dst = self.h[:, c, tr]
                    if self.dbg == 3 and c == 4:
                        dst = dst.rearrange("p (a b) -> p a b", a=4)
                    P.emit("dve", lambda e, c=c, dst=dst: e.tensor_copy(out=dst, in_=srcs[c]), reads=allR + [self.Hr[c][t]], writes=[self.Hr[c][t]])
                continue
            for c in range(DC):
                ps, pr = self.bank()
                for j in range(4):
                    P.emit("pe", lambda e, j=j, c=c, ps=ps: e.matmul(ps[:], woA[:, j, c * 128:(c + 1) * 128], catA[:, j, :], start=(j == 0), stop=False),
                           reads=[SB] + catAR, writes=[pr])
                for j in range(4):
                    P.emit("pe", lambda e, j=j, c=c, ps=ps: e.matmul(ps[:], woC[:, j, c * 128:(c + 1) * 128], catC[:, j, :], start=False, stop=(j == 3)),
                           reads=[SB, catCR[j]], writes=[pr])
                P.emit("dve", lambda e, c=c, ps=ps, tr=tr: e.tensor_tensor(out=self.h[:, c, tr], in0=ps[:], in1=self.h[:, c, tr], op=ALU.add),
                       reads=[pr, self.Hr[c][t]], writes=[self.Hr[c][t]], nosame=True)
        P.barrier(("pe", "act", "dve"))

    def mix_odd(self, l):
        P, dr = self.P, self.dr
        pc = self.pcol
        o_ = l // 2
        wsl = self.wsl
        SA, SB = self.SA, self.SB
        wig = wsl[:, 0:8192].rearrange("p (kc f) -> p kc f", kc=8)
        wir = wsl[:, 8192:16384].rearrange("p (kc f) -> p kc f", kc=8)
        wo = wsl[:, 16384:24576].rearrange("p (j d) -> p j d", j=8)
        win = dr["odd_w_in"][o_]
        wout = dr["odd_w_out"][o_]
        P.emit("pool", lambda e: e.dma_start(out=wig, in_=win[:, 0:1024].rearrange("(kc p) f -> p kc f", p=128)), writes=[SA], dma=("w", 8))
        P.emit("pool", lambda e: e.dma_start(out=wir, in_=win[:, 1024:2048].rearrange("(kc p) f -> p kc f", p=128)), writes=[SA, SB], dma=("w", 8))
        P.emit("pool", lambda e: e.dma_start(out=wo, in_=wout.rearrange("(j p) n -> p j n", p=128)), writes=[SB], dma=("w", 8))
        gwR = self.gwR
        P.emit("pool", lambda e: e.dma_start(out=self.gw[:, 0, :, :], in_=dr["gate_a_w"][o_].rearrange("h i j -> i h j")), writes=[gwR], dma=("gw", 2))
        P.emit("pool", lambda e: e.dma_start(out=self.gw[:, 1, :, :], in_=dr["gate_x_w"][o_].rearrange("h i j -> i h j")), writes=[gwR], dma=("gw", 2))
        o = 0

        def alloc(shape, dt):
            nonlocal o
            v = self.view(o, shape, dt)
            o += int(np.prod(shape)) * (4 if dt == F32 else 2)
            o = (o + 63) // 64 * 64
            return v

        XW = TT + 4
        gate = alloc([DC, TT], BF16)
        xr = alloc([4, XW], F32)
        xc = alloc([4, TT], F32)
        xcb = alloc([4, TT], BF16)
        bA = alloc([4, TT], F32)
        bB = alloc([4, TT], F32)
        bC = alloc([4, TT], F32)
        halo = alloc([DC, 4], F32)
        carry = alloc([DC], F32)
        assert o <= 56 * 1024, o
        gateR = [Res() for _ in range(DC)]
        xrR, xcR, xcbR, AR_, BR, CR = ([Res() for _ in range(4)] for _ in range(6))
        haloR, carryR = [Res(), Res()], [Res(), Res()]
        P.emit("dve", lambda e: e.memset(halo[:], 0.0), writes=haloR)
        P.emit("dve", lambda e: e.memset(carry[:], 0.0), writes=carryR)
        der = self.der
        q25 = self.parm[:, pc["eps"] + 3:pc["eps"] + 4]
        self.rmsnorm(0, pc["nm"] + l * 8, lambda c: self.hnm[:, c, :], self.hnmR)
        for t in range(NT):
            tr = slice(t * TT, (t + 1) * TT)
            for c in range(DC):
                ps, pr = self.bank()
                for kc in range(DC):
                    P.emit("pe", lambda e, kc=kc, c=c, ps=ps: e.matmul(
                        ps[:], wig[:, kc, c * 128:(c + 1) * 128], self.hnm[:, kc, :], start=(kc == 0), stop=(kc == DC - 1)),
                        reads=[SA, self.hnmR[kc]], writes=[pr])
                P.emit("act", lambda e, c=c, ps=ps: e.activation(out=gate[:, c, :], in_=ps[:], func=AF.Gelu_apprx_tanh),
                       reads=[pr], writes=[gateR[c]], nosame=True)
            for b in range(2):
                cs = [4 * b + j for j in range(4)]
                for j, c in enumerate(cs):
                    ps, pr = self.bank()
                    for kc in range(DC):
                        P.emit("pe", lambda e, kc=kc, c=c, ps=ps: e.matmul(
                            ps[:], wir[:, kc, c * 128:(c + 1) * 128], self.hnm[:, kc, :], start=(kc == 0), stop=(kc == DC - 1)),
                            reads=[SA, SB, self.hnmR[kc]], writes=[pr])
                    P.emit("act", lambda e, j=j, ps=ps: e.activation(out=xr[:, j, 3:3 + TT], in_=ps[:], func=AF.Copy),
                           reads=[pr], writes=[xrR[j]])
                if b == 1 and t + 1 < NT:
                    self.rmsnorm(t + 1, pc["nm"] + l * 8, lambda c: self.hnm[:, c, :], self.hnmR)
                P.emit("dve", lambda e, b=b: e.tensor_copy(out=xr[:, :, 0:3], in_=halo[:, 4 * b:4 * b + 4, 0:3]), reads=[haloR[b]], writes=xrR)
                P.emit("act", lambda e, b=b: e.activation(out=halo[:, 4 * b:4 * b + 4, 0:3], in_=xr[:, :, TT:TT + 3], func=AF.Copy),
                       reads=xrR, writes=[haloR[b]])
                for k in (3, 0, 1, 2):
                    for j, c in enumerate(cs):
                        wc = pc["lw"] + (o_ * 8 + c) * 4
                        if k == 3:
                            cb = self.parm[:, pc["lcb"] + o_ * 8 + c: pc["lcb"] + o_ * 8 + c + 1]
                            P.emit("dve", lambda e, j=j, wc=wc, cb=cb: e.tensor_scalar(
                                out=xc[:, j, :], in0=xr[:, j, 3:3 + TT], scalar1=self.parm[:, wc + 3:wc + 4], scalar2=cb, op0=ALU.mult, op1=ALU.add),
                                reads=[xrR[j], self.cR], writes=[xcR[j]])
                        else:
                            P.emit("dve", lambda e, j=j, k=k, wc=wc: e.scalar_tensor_tensor(
                                out=xc[:, j, :], in0=xr[:, j, k:k + TT], scalar=self.parm[:, wc + k:wc + k + 1], in1=xc[:, j, :], op0=ALU.mult, op1=ALU.add),
                                reads=[xrR[j], xcR[j]], writes=[xcR[j]])
                P.emit("act", lambda e: e.activation(out=xcb[:], in_=xc[:], func=AF.Copy), reads=xcR, writes=xcbR)
                for j, c in enumerate(cs):
                    psr, prr = self.bank()
                    psi, pri = self.bank()
                    P.emit("pe", lambda e, j=j, c=c, psr=psr: e.matmul(psr[:], self.gw[:, 0, c, :], xcb[:, j, :], start=True, stop=True), reads=[gwR, xcbR[j]], writes=[prr])
                    P.emit("pe", lambda e, j=j, c=c, psi=psi: e.matmul(psi[:], self.gw[:, 1, c, :], xcb[:, j, :], start=True, stop=True), reads=[gwR, xcbR[j]], writes=[pri])
                    hba = der[:, 48 + o_ * 8 + c: 48 + o_ * 8 + c + 1]
                    hbx = der[:, 64 + o_ * 8 + c: 64 + o_ * 8 + c + 1]
                    P.emit("act", lambda e, j=j, psr=psr, hba=hba: e.activation(out=bA[:, j, :], in_=psr[:], func=AF.Tanh, bias=hba, scale=0.5),
                           reads=[prr, self.cR], writes=[AR_[j]])
                    P.emit("act", lambda e, j=j, psi=psi, hbx=hbx: e.activation(out=bC[:, j, :], in_=psi[:], func=AF.Tanh, bias=hbx, scale=0.5),
                           reads=[pri, self.cR], writes=[CR[j]])
                for j, c in enumerate(cs):
                    clh = der[:, 16 + o_ * 8 + c: 16 + o_ * 8 + c + 1]
                    P.emit("act", lambda e, j=j, clh=clh: e.activation(out=bA[:, j, :], in_=bA[:, j, :], func=AF.Exp, bias=clh, scale=clh),
                           reads=[AR_[j], self.cR], writes=[AR_[j]])
                P.emit("act", lambda e: e.activation(out=bB[:], in_=bA[:], func=AF.Square), reads=AR_, writes=BR)
                P.emit("act", lambda e: e.activation(out=bB[:], in_=bB[:], func=AF.Sqrt, bias=q25, scale=-0.25), reads=BR + [self.cR], writes=BR)
                P.emit("dve", lambda e: e.scalar_tensor_tensor(out=bC[:], in0=bC[:], scalar=1.0, in1=xc[:], op0=ALU.add, op1=ALU.mult),
                       reads=CR + xcR, writes=CR)
                P.emit("dve", lambda e: e.tensor_tensor(out=bC[:], in0=bC[:], in1=bB[:], op=ALU.mult), reads=CR + BR, writes=CR)
                for j, c in enumerate(cs):
                    P.emit("dve", lambda e, j=j, c=c: e.tensor_tensor_scan(
                        out=xc[:, j, :], data0=bA[:, j, :], data1=bC[:, j, :], initial=carry[:, c:c + 1], op0=ALU.mult, op1=ALU.add),
                        reads=[AR_[j], CR[j], carryR[b], xcR[j]], writes=[xcR[j]], nosame=(j > 0))
                P.emit("act", lambda e, b=b: e.activation(out=carry[:, 4 * b:4 * b + 4], in_=xc[:, :, TT - 1], func=AF.Copy),
                       reads=xcR, writes=[carryR[b]])
                P.emit("dve", lambda e, b=b: e.tensor_tensor(out=gate[:, 4 * b:4 * b + 4, :], in0=gate[:, 4 * b:4 * b + 4, :], in1=xc[:], op=ALU.mult),
                       reads=xcR + [gateR[c] for c in cs], writes=[gateR[c] for c in cs])
            for c in range(DC):
                ps, pr = self.bank()
                for j in range(DC):
                    P.emit("pe", lambda e, j=j, c=c, ps=ps: e.matmul(ps[:], wo[:, j, c * 128:(c + 1) * 128], gate[:, j, :], start=(j == 0), stop=(j == DC - 1)),
                           reads=[SB, gateR[j]], writes=[pr])
                P.emit("dve", lambda e, c=c, ps=ps, tr=tr: e.tensor_tensor(out=self.h[:, c, tr], in0=ps[:], in1=self.h[:, c, tr], op=ALU.add),
                       reads=[pr, self.Hr[c][t]], writes=[self.Hr[c][t]], nosame=True)
        P.barrier(("pe", "act", "dve"))

    def mix_odd_old(self, l):
        P, dr = self.P, self.dr
        pc = self.pcol
        o_ = l // 2
        wsl = self.wsl
        SA, SB = self.SA, self.SB
        wig = wsl[:, 0:8192].rearrange("p (kc f) -> p kc f", kc=8)
        wir = wsl[:, 8192:16384].rearrange("p (kc f) -> p kc f", kc=8)
        wo = wsl[:, 16384:24576].rearrange("p (j d) -> p j d", j=8)
        win = dr["odd_w_in"][o_]
        wout = dr["odd_w_out"][o_]
        P.emit("pool", lambda e: e.dma_start(out=wig, in_=win[:, 0:1024].rearrange("(kc p) f -> p kc f", p=128)), writes=[SA], dma=("w", 8))
        P.emit("pool", lambda e: e.dma_start(out=wir, in_=win[:, 1024:2048].rearrange("(kc p) f -> p kc f", p=128)), writes=[SA, SB], dma=("w", 8))
        P.emit("pool", lambda e: e.dma_start(out=wo, in_=wout.rearrange("(j p) n -> p j n", p=128)), writes=[SB], dma=("w", 8))
        gwR = self.gwR
        P.emit("pool", lambda e: e.dma_start(out=self.gw[:, 0, :, :], in_=dr["gate_a_w"][o_].rearrange("h i j -> i h j")), writes=[gwR], dma=("gw", 2))
        P.emit("pool", lambda e: e.dma_start(out=self.gw[:, 1, :, :], in_=dr["gate_x_w"][o_].rearrange("h i j -> i h j")), writes=[gwR], dma=("gw", 2))
        o = 0

        def alloc(shape, dt):
            nonlocal o
            v = self.view(o, shape, dt)
            o += int(np.prod(shape)) * (4 if dt == F32 else 2)
            o = (o + 63) // 64 * 64
            return v

        gate = alloc([DC, TT], BF16)
        mo = alloc([DC, TT], BF16)
        xr = [alloc([TT + 4], F32) for _ in range(2)]
        xcb = [alloc([TT], BF16) for _ in range(2)]
        halo = alloc([DC, 4], F32)
        carry = alloc([DC], F32)
        NB = 2
        xc = [alloc([TT], F32) for _ in range(NB)]
        bA = [alloc([TT], F32) for _ in range(NB)]
        bB = [alloc([TT], F32) for _ in range(NB)]
        bC = [alloc([TT], F32) for _ in range(NB)]
        hs = [alloc([TT], F32) for _ in range(NB)]
        assert o <= 56 * 1024
        gateR, moR = [Res() for _ in range(DC)], [Res() for _ in range(DC)]
        xrR, xcbR = [Res(), Res()], [Res(), Res()]
        haloR, carryR = [Res() for _ in range(DC)], [Res() for _ in range(DC)]
        xcR, AR_, BR, CR, hsR = ([Res() for _ in range(NB)] for _ in range(5))
        P.emit("dve", lambda e: e.memset(halo[:], 0.0), writes=haloR)
        P.emit("dve", lambda e: e.memset(carry[:], 0.0), writes=carryR)
        der = self.der
        for t in range(NT):
            tr = slice(t * TT, (t + 1) * TT)
            self.rmsnorm(t, pc["nm"] + l * 8, lambda c: self.hnm[:, c, :], self.hnmR)
            for c in range(DC):
                ps, pr = self.bank()
                for kc in range(DC):
                    P.emit("pe", lambda e, kc=kc, c=c, ps=ps: e.matmul(
                        ps[:], wig[:, kc, c * 128:(c + 1) * 128], self.hnm[:, kc, :], start=(kc == 0), stop=(kc == DC - 1)),
                        reads=[SA, self.hnmR[kc]], writes=[pr])
                P.emit("act", lambda e, c=c, ps=ps: e.activation(out=gate[:, c, :], in_=ps[:], func=AF.Gelu_apprx_tanh),
                       reads=[pr], writes=[gateR[c]])
            for c in range(DC):
                k = c % 2
                ps, pr = self.bank()
                for kc in range(DC):
                    P.emit("pe", lambda e, kc=kc, c=c, ps=ps: e.matmul(
                        ps[:], wir[:, kc, c * 128:(c + 1) * 128], self.hnm[:, kc, :], start=(kc == 0), stop=(kc == DC - 1)),
                        reads=[SA, SB, self.hnmR[kc]], writes=[pr])
                P.emit("act", lambda e, k=k, ps=ps: e.activation(out=xr[k][:, 3:3 + TT], in_=ps[:], func=AF.Copy), reads=[pr], writes=[xrR[k]])
                P.emit("dve", lambda e, k=k, c=c: e.tensor_copy(out=xr[k][:, 0:3], in_=halo[:, c, 0:3]), reads=[haloR[c]], writes=[xrR[k]])
                P.emit("act", lambda e, k=k, c=c: e.activation(out=halo[:, c, 0:3], in_=xr[k][:, TT:TT + 3], func=AF.Copy), reads=[xrR[k]], writes=[haloR[c]])
                wc = pc["lw"] + (o_ * 8 + c) * 4
                cb = self.parm[:, pc["lcb"] + o_ * 8 + c: pc["lcb"] + o_ * 8 + c + 1]
                P.emit("dve", lambda e, k=k, wc=wc, cb=cb: e.tensor_scalar(
                    out=xc[k], in0=xr[k][:, 3:3 + TT], scalar1=self.parm[:, wc + 3:wc + 4], scalar2=cb, op0=ALU.mult, op1=ALU.add),
                    reads=[xrR[k], self.cR], writes=[xcR[k]])
                for j in range(3):
                    P.emit("dve", lambda e, k=k, j=j, wc=wc: e.scalar_tensor_tensor(
                        out=xc[k], in0=xr[k][:, j:j + TT], scalar=self.parm[:, wc + j:wc + j + 1], in1=xc[k], op0=ALU.mult, op1=ALU.add),
                        reads=[xrR[k], xcR[k]], writes=[xcR[k]])
                P.emit("act", lambda e, k=k: e.activation(out=xcb[k], in_=xc[k], func=AF.Copy), reads=[xcR[k]], writes=[xcbR[k]])
                psr, prr = self.bank()
                psi, pri = self.bank()
                P.emit("pe", lambda e, k=k, c=c, psr=psr: e.matmul(psr[:], self.gw[:, 0, c, :], xcb[k], start=True, stop=True), reads=[gwR, xcbR[k]], writes=[prr])
                P.emit("pe", lambda e, k=k, c=c, psi=psi: e.matmul(psi[:], self.gw[:, 1, c, :], xcb[k], start=True, stop=True), reads=[gwR, xcbR[k]], writes=[pri])
                gab = self.parm[:, pc["gab"] + o_ * 8 + c: pc["gab"] + o_ * 8 + c + 1]
                gxb = self.parm[:, pc["gxb"] + o_ * 8 + c: pc["gxb"] + o_ * 8 + c + 1]
                cl = der[:, o_ * 8 + c: o_ * 8 + c + 1]
                cl2 = der[:, 16 + o_ * 8 + c: 16 + o_ * 8 + c + 1]
                one = self.parm[:, pc["eps"] + 2:pc["eps"] + 3]
                P.emit("act", lambda e, k=k, psr=psr, gab=gab: e.activation(out=bA[k], in_=psr[:], func=AF.Sigmoid, bias=gab, scale=1.0), reads=[prr, self.cR], writes=[AR_[k]])
                P.emit("act", lambda e, k=k, psi=psi, gxb=gxb: e.activation(out=bC[k], in_=psi[:], func=AF.Sigmoid, bias=gxb, scale=1.0), reads=[pri, self.cR], writes=[CR[k]])
                P.emit("act", lambda e, k=k, cl2=cl2: e.activation(out=bB[k], in_=bA[k], func=AF.Exp, scale=cl2), reads=[AR_[k]], writes=[BR[k]])
                P.emit("act", lambda e, k=k, cl=cl: e.activation(out=bA[k], in_=bA[k], func=AF.Exp, scale=cl), reads=[AR_[k], BR[k]], writes=[AR_[k]])
                P.emit("act", lambda e, k=k, one=one: e.activation(out=bB[k], in_=bB[k], func=AF.Sqrt, bias=one, scale=-1.0), reads=[BR[k], self.cR], writes=[BR[k]])
                P.emit("dve", lambda e, k=k: e.tensor_tensor(out=bC[k], in0=bC[k], in1=xc[k], op=ALU.mult), reads=[CR[k], xcR[k]], writes=[CR[k]])
                P.emit("dve", lambda e, k=k: e.tensor_tensor(out=bC[k], in0=bC[k], in1=bB[k], op=ALU.mult), reads=[CR[k], BR[k]], writes=[CR[k]])
                P.emit("dve", lambda e, k=k, c=c: e.tensor_tensor_scan(out=hs[k], data0=bA[k], data1=bC[k], initial=carry[:, c:c + 1], op0=ALU.mult, op1=ALU.add),
                       reads=[AR_[k], CR[k], carryR[c]], writes=[hsR[k]])
                P.emit("act", lambda e, k=k, c=c: e.activation(out=carry[:, c:c + 1], in_=hs[k][:, TT - 1:TT], func=AF.Copy), reads=[hsR[k]], writes=[carryR[c]])
                P.emit("dve", lambda e, k=k, c=c: e.tensor_tensor(out=mo[:, c, :], in0=hs[k], in1=gate[:, c, :], op=ALU.mult), reads=[hsR[k], gateR[c]], writes=[moR[c]])
            for c in range(DC):
                ps, pr = self.bank()
                for j in range(DC):
                    P.emit("pe", lambda e, j=j, c=c, ps=ps: e.matmul(ps[:], wo[:, j, c * 128:(c + 1) * 128], mo[:, j, :], start=(j == 0), stop=(j == DC - 1)),
                           reads=[SB, moR[j]], writes=[pr])
                P.emit("dve", lambda e, c=c, ps=ps, tr=tr: e.tensor_tensor(out=self.h[:, c, tr], in0=ps[:], in1=self.h[:, c, tr], op=ALU.add),
                       reads=[pr, self.Hr[c][t]], writes=[self.Hr[c][t]], nosame=True)
        P.barrier(("pe", "act", "dve"))


def prep_shared(inp):
    params, pcol = pack_params(inp)
    biasT, maskneg = attn_consts(inp["rel_bias"])
    shared = {
        "params": params,
        "ident": np.eye(128, dtype=np.float32),
        "biasT": biasT.reshape(128, 2048),
        "maskneg": maskneg.reshape(128, 2048),
        "even_w_in": np.ascontiguousarray(np.asarray(inp["even_w_in"], np.float32)[:, :, qperm()]),
    }
    for k in ("ffn1_wg", "ffn1_wu", "ffn1_wd", "ffn2_wg", "ffn2_wu", "ffn2_wd", "even_w_out", "odd_w_in",
              "odd_w_out", "gate_a_w", "gate_x_w"):
        shared[k] = np.ascontiguousarray(np.asarray(inp[k], np.float32))
    return shared, pcol, params.shape[1]


def kernel(**inputs):
    x = np.asarray(inputs["x"], np.float32)
    shared, pcol, npar = prep_shared(inputs)
    n_seq = x.shape[0] // N_CORES
    b = Builder(n_seq, DEPTH)
    b.npar = npar
    nc = b.build(pcol)
    in_maps = []
    for c in range(N_CORES):
        m = dict(shared)
        m["x"] = np.ascontiguousarray(x[c * n_seq:(c + 1) * n_seq])
        in_maps.append(m)
    res = run_bass_kernel_spmd(nc, in_maps, core_ids=list(range(N_CORES)))
    return np.concatenate([r["out"] for r in res.results], axis=0)
```

```python
import math
from contextlib import ExitStack

import numpy as np
import concourse.bass as bass
import concourse.mybir as mybir
from concourse.bass_utils import run_bass_kernel_spmd

F32 = mybir.dt.float32
BF16 = mybir.dt.bfloat16
AF = mybir.ActivationFunctionType
ALU = mybir.AluOpType

N_CORES = 8
D = 1024
S = 2048
DEPTH = 4
DFF = 2816
DC = 8
FC = 22
TT = 512
NT = 4
EVEN_IN = 1792
RMS_EPS = 1e-6
LN_EPS = 1e-5
GROUPS = [(0, 4), (4, 8), (8, 12), (12, 16), (16, 19), (19, 22)]

ENGS = ("pe", "act", "dve", "pool", "sp")
SEM_CAP = 30000


class Res:
    __slots__ = ("w", "r", "pr")

    def __init__(self):
        self.w = []
        self.r = []
        self.pr = []


class Op:
    __slots__ = ("eng", "fn", "deps", "dma", "signal", "tok", "waits")

    def __init__(self, eng, fn, dma):
        self.eng = eng
        self.fn = fn
        self.dma = dma
        self.deps = []
        self.signal = dma is not None
        self.tok = None
        self.waits = None


class Prog:
    def __init__(self, nc):
        self.nc = nc
        self.ops = {e: [] for e in ENGS}
        self.last = {e: None for e in ENGS}
        self.pend = {}
        self.dmacnt = {}

    def emit(self, eng, fn, reads=(), writes=(), dma=None, nosame=False):
        if dma is not None:
            name, n = dma
            j = self.dmacnt.get(name, 0)
            self.dmacnt[name] = j + 1
            dma = (name, j % n)
        op = Op(eng, fn, dma)
        deps = {}
        for r in reads:
            for o in r.w:
                deps[id(o)] = o
        joins = []
        for w in writes:
            if dma is not None and w.w and not w.r and all(o.dma is not None for o in w.w):
                joins.append(w)
                for o in w.pr:
                    deps[id(o)] = o
                continue
            for o in w.w:
                deps[id(o)] = o
            for o in w.r:
                deps[id(o)] = o
        for d in deps.values():
            if d.eng == eng and d.dma is None and dma is None:
                if eng == "pe" or nosame:
                    continue
            op.deps.append(d)
        if eng in self.pend:
            for d in self.pend.pop(eng):
                if d is not None and not (d.eng == eng and d.dma is None):
                    op.deps.append(d)
        for r in reads:
            r.r.append(op)
        for w in writes:
            if any(w is j for j in joins):
                w.w.append(op)
            else:
                w.pr = w.r
                w.w = [op]
                w.r = []
        self.ops[eng].append(op)
        self.last[eng] = op
        return op

    def barrier(self, engs):
        lasts = [self.last[e] for e in engs]
        for e in engs:
            self.pend[e] = list(self.pend.get(e, [])) + lasts

    def build(self, stack):
        nc = self.nc
        for e in ENGS:
            for op in self.ops[e]:
                for d in op.deps:
                    d.signal = True
        sems = {}
        cnt = {}
        for e in ENGS:
            c = 0
            for op in self.ops[e]:
                if op.dma is not None:
                    k = ("d",) + op.dma
                    cnt[k] = cnt.get(k, 0) + 16
                    op.tok = (k, cnt[k])
                elif op.signal:
                    c += 1
                    op.tok = (("e", e, (c - 1) // SEM_CAP), (c - 1) % SEM_CAP + 1)
        for e in ENGS:
            seen = {}
            for op in self.ops[e]:
                ws = {}
                for d in op.deps:
                    k, v = d.tok
                    if seen.get(k, 0) >= v:
                        continue
                    if ws.get(k, 0) < v:
                        ws[k] = v
                seen.update(ws)
                op.waits = list(ws.items())
                if op.tok is not None and op.tok[0] not in sems:
                    sems[op.tok[0]] = None
        for k in list(sems):
            sems[k] = stack.enter_context(nc.semaphore("s_" + "_".join(str(x) for x in k)))
        block = stack.enter_context(nc.Block())
        names = {"pe": "tensor", "act": "scalar", "dve": "vector", "pool": "gpsimd", "sp": "sync"}
        ops = self.ops

        def run(e, eng):
            for op in ops[e]:
                for k, v in op.waits:
                    eng.wait_ge(sems[k], v)
                ins = op.fn(eng)
                if op.tok is not None:
                    ins.then_inc(sems[op.tok[0]], 16 if op.dma is not None else 1)

        for e in ENGS:
            if not ops[e]:
                continue

            def f(eng, e=e):
                run(e, eng)

            getattr(block, names[e])(f)


def _pcols(v, nchunk):
    v = np.asarray(v, np.float32)
    lead = v.shape[:-1]
    v = v.reshape(lead + (nchunk, 128))
    v = np.moveaxis(v, -1, 0)
    return np.ascontiguousarray(v).reshape(128, -1)


class PL:
    pass


def pack_params(inp):
    cols = []
    off = {}

    def add(name, arr):
        off[name] = sum(c.shape[1] for c in cols)
        cols.append(np.ascontiguousarray(arr, dtype=np.float32))

    add("n1", _pcols(inp["norm_ffn1"], 8))
    add("nm", _pcols(inp["norm_mix"], 8))
    add("n2", _pcols(inp["norm_ffn2"], 8))
    add("nf", _pcols(inp["norm_final"], 8))
    cw = np.asarray(inp["conv_b_w"], np.float32)
    cw = cw.reshape(2, 31, 4, 128).transpose(3, 0, 2, 1)
    add("cw", cw.reshape(128, -1))
    add("cb", _pcols(inp["conv_b_b"], 4))
    add("lg", _pcols(inp["conv_ln_g"], 4))
    add("lb", _pcols(inp["conv_ln_b"], 4))
    lw = np.asarray(inp["lru_conv_w"], np.float32)
    lw = lw.reshape(2, 4, 8, 128).transpose(3, 0, 2, 1)
    add("lw", lw.reshape(128, -1))
    add("lcb", _pcols(inp["lru_conv_b"], 8))
    add("gab", _pcols(inp["gate_a_b"], 8))
    add("gxb", _pcols(inp["gate_x_b"], 8))
    add("lam", _pcols(inp["lru_lambda"], 8))
    sk = np.asarray(inp["attn_sinks"], np.float32).reshape(1, 16)
    add("sink", np.broadcast_to(sk, (128, 16)))
    add("eps", np.broadcast_to(np.array([[RMS_EPS, LN_EPS, 1.0, 0.25]], np.float32), (128, 4)))
    return np.concatenate(cols, axis=1), off


def t5_bucket_np(dist):
    n = np.maximum(dist, 0)
    max_exact = 16
    nf = np.maximum(n, max_exact).astype(np.float32)
    large = max_exact + (np.log(nf / np.float32(max_exact)) / np.float32(math.log(128 / max_exact))
                         * np.float32(32 - max_exact)).astype(np.int32)
    large = np.minimum(large, 31)
    return np.where(n < max_exact, n, large)


def attn_consts(rel_bias):
    q = np.arange(128)[None, :]
    s = np.arange(128)[:, None]
    out_b = np.zeros((128, 2, 8, 128), np.float32)
    out_m = np.zeros((128, 2, 8, 128), np.float32)
    rb = np.asarray(rel_bias, np.float32)
    for half in range(2):
        dist = q + 128 - (s + 128 * half)
        ok = (dist >= 0) & (dist < 128)
        bk = t5_bucket_np(dist)
        g = rb[bk]
        out_b[:, half] = np.transpose(g, (0, 2, 1))
        out_m[:, half] = np.where(ok, 0.0, -1e30)[:, None, :]
    return out_b, out_m


def qperm():
    idx = []
    for cq in range(4):
        idx += list(range(cq * 64, cq * 64 + 64))
        idx += list(range((4 + cq) * 64, (4 + cq) * 64 + 64))
    return np.array(idx + list(range(512, EVEN_IN)))


class Builder:
    def __init__(self, n_seq, layers, stages=("ffn1", "mix", "ffn2")):
        self.n_seq = n_seq
        self.layers = layers
        self.stages = stages
        self.nc = bass.Bass("TRN2", target_bir_lowering=False)
        self.nb = 0
        self.bank_pool = list(range(8))

    def bank(self):
        pool = self.bank_pool
        b = pool[self.nb % len(pool)]
        self.nb += 1
        return self.ps[b], self.psR[b]

    def tmp(self):
        k = self.ntmp % 2
        self.ntmp += 1
        return self.T[k], self.TR[k]

    def view(self, off_bytes, shape, dt, p0=0, p1=128):
        n = int(np.prod(shape))
        if dt == F32:
            assert off_bytes % 4 == 0
            a = self.arena32[p0:p1, off_bytes // 4: off_bytes // 4 + n]
        else:
            assert off_bytes % 2 == 0
            a = self.arena[p0:p1, off_bytes // 2: off_bytes // 2 + n]
        if len(shape) == 2:
            a = a.rearrange("p (a b) -> p a b", a=shape[0])
        elif len(shape) == 3:
            a = a.rearrange("p (a b c) -> p a b c", a=shape[0], b=shape[1])
        return a

    def build(self, pcol):
        nc = self.nc
        n_seq = self.n_seq
        self.pcol = pcol
        npar = self.npar
        dr = {}

        def din(name, shape):
            dr[name] = nc.dram_tensor(name, list(shape), F32, kind="ExternalInput").ap()

        din("x", [n_seq, S, D])
        din("params", [128, npar])
        din("ident", [128, 128])
        din("biasT", [128, 2048])
        din("maskneg", [128, 2048])
        for w in ("ffn1", "ffn2"):
            din(w + "_wg", [DEPTH, D, DFF])
            din(w + "_wu", [DEPTH, D, DFF])
            din(w + "_wd", [DEPTH, DFF, D])
        din("even_w_in", [2, D, EVEN_IN])
        din("even_w_out", [2, D, D])
        din("odd_w_in", [2, D, 2 * D])
        din("odd_w_out", [2, D, D])
        din("gate_a_w", [2, 8, 128, 128])
        din("gate_x_w", [2, 8, 128, 128])
        self.dr = dr
        self.out = nc.dram_tensor("out", [n_seq, S, D], F32, kind="ExternalOutput").ap()

        with ExitStack() as st:
            sb = lambda name, shape, dt: st.enter_context(nc.sbuf_tensor(name, shape, dt))
            self.h = sb("h", [128, DC, S], F32)
            self.wsl = sb("wsl", [128, 24576], BF16)
            self.arena = sb("arena", [128, 28 * 1024], BF16)
            self.arena32 = self.arena[:].bitcast(F32)
            self.hnm = sb("hnm", [128, DC, TT], BF16)
            self.parm = sb("parm", [128, npar], F32)
            self.der = sb("der", [128, 128], F32)
            self.ident = sb("ident_sb", [128, 128], F32)
            self.ones = sb("ones", [128, 128], BF16)
            self.identb = sb("identb", [128, 128], BF16)
            self.EBT = sb("EBT", [128, 2048], F32)
            self.sq = sb("sq", [128, DC, TT], BF16)
            self.gw = sb("gw", [128, 2, 8, 128], BF16)
            self.sinkrow = self.gw[0:1, :, :, :].rearrange("p a b c -> p (a b) c")
            self.T = [sb("T%d" % i, [128, TT], F32) for i in range(4)]
            self.TR = [Res() for _ in range(4)]
            self.ntmp = 0
            self.ps = [st.enter_context(nc.psum_tensor("ps%d" % i, [128, TT], F32)) for i in range(8)]
            self.psR = [Res() for _ in range(8)]
            self.hnf = self.view(0, [DC, S], BF16)
            self.AR = 32 * 1024

            self.P = Prog(nc)
            self.Hr = [[Res() for _ in range(NT)] for _ in range(DC)]
            self.hnfR = [[Res() for _ in range(NT)] for _ in range(DC)]
            self.hnmR = [Res() for _ in range(DC)]
            self.sqR = Res()
            self.SA, self.SB = Res(), Res()
            self.cR = Res()
            self.gwR = Res()

            self.setup()
            for i in range(n_seq):
                self.load(i)
                for l in range(self.layers):
                    if "ffn1" in self.stages:
                        self.ffn(l, "ffn1", pcol["n1"] + l * 8)
                    if "mix" in self.stages:
                        if l % 2 == 0:
                            self.mix_even(l)
                        else:
                            self.mix_odd(l)
                    if "ffn2" in self.stages:
                        self.ffn(l, "ffn2", pcol["n2"] + l * 8)
                self.store(i)
            P = self.P
            fin = P.emit("sp", lambda e: e.nop(), reads=self.outR, writes=self.outR)
            for o in self.out_ops[-4:]:
                if o not in fin.deps:
                    fin.deps.append(o)
            P.build(st)
        return nc

    def setup(self):
        P, dr = self.P, self.dr
        pc = self.pcol
        parm, der = self.parm, self.der
        cR = self.cR
        self.outR = [Res(), Res()]
        self.out_ops = []
        P.emit("sp", lambda e: e.dma_start(out=parm[:], in_=dr["params"]), writes=[cR], dma=("c0", 1))
        P.emit("sp", lambda e: e.dma_start(out=self.ident[:], in_=dr["ident"]), writes=[cR], dma=("c1", 1))
        bt = self.view(self.AR, [2048], F32)
        mk = self.view(self.AR + 8192, [2048], F32)
        tR = Res()
        P.emit("sp", lambda e: e.dma_start(out=bt, in_=dr["biasT"]), writes=[tR], dma=("c2", 1))
        P.emit("sp", lambda e: e.dma_start(out=mk, in_=dr["maskneg"]), writes=[tR], dma=("c3", 1))
        P.emit("dve", lambda e: e.memset(self.ones[:], 1.0), writes=[cR])
        P.emit("dve", lambda e: e.tensor_copy(out=self.identb[:], in_=self.ident[:]), reads=[cR], writes=[cR])
        P.emit("dve", lambda e: e.tensor_tensor(out=bt, in0=bt, in1=mk, op=ALU.add), reads=[tR], writes=[tR])
        P.emit("act", lambda e: e.activation(out=self.EBT[:], in_=bt, func=AF.Exp), reads=[tR], writes=[cR])
        lam = parm[:, pc["lam"]:pc["lam"] + 16]
        dR = Res()
        P.emit("act", lambda e: e.activation(out=der[:, 0:16], in_=lam, func=AF.Exp, scale=-1.0), reads=[cR], writes=[dR])
        P.emit("dve", lambda e: e.tensor_scalar_add(out=der[:, 0:16], in0=der[:, 0:16], scalar1=1.0), reads=[dR], writes=[dR])
        P.emit("act", lambda e: e.activation(out=der[:, 0:16], in_=der[:, 0:16], func=AF.Ln), reads=[dR], writes=[dR])
        P.emit("dve", lambda e: e.tensor_scalar_mul(out=der[:, 16:32], in0=der[:, 0:16], scalar1=-16.0), reads=[dR], writes=[dR])
        P.emit("dve", lambda e: e.tensor_scalar_mul(out=der[:, 0:16], in0=der[:, 0:16], scalar1=-8.0), reads=[dR], writes=[dR])
        P.emit("act", lambda e: e.activation(out=der[:, 32:48], in_=parm[:, pc["sink"]:pc["sink"] + 16], func=AF.Exp),
               reads=[cR, dR], writes=[dR])
        P.emit("dve", lambda e: e.tensor_scalar_mul(out=der[:, 16:32], in0=der[:, 0:16], scalar1=0.5), reads=[dR], writes=[dR])
        P.emit("dve", lambda e: e.tensor_scalar_mul(out=der[:, 48:64], in0=parm[:, pc["gab"]:pc["gab"] + 16], scalar1=0.5), reads=[dR, cR], writes=[dR])
        P.emit("dve", lambda e: e.tensor_scalar_mul(out=der[:, 64:80], in0=parm[:, pc["gxb"]:pc["gxb"] + 16], scalar1=0.5), reads=[dR, cR], writes=[dR])
        last = P.emit("dve", lambda e: e.memset(self.T[0][:], 0.0), reads=[dR, tR], writes=[cR, tR, dR])
        P.barrier(("pe", "act", "dve", "sp"))

    def load(self, i):
        P = self.P
        x = self.dr["x"]
        xs = [self.view(self.AR + k * 4096, [D], F32) for k in range(2)]
        xsR = [Res(), Res()]
        for j in range(16):
            k = j % 2
            t = j // 4
            P.emit("sp", lambda e, j=j, k=k: e.dma_start(out=xs[k], in_=x[i, j * 128:(j + 1) * 128, :]),
                   writes=[xsR[k]], dma=("xin", 2))
            for half in range(2):
                ps, pr = self.bank()
                for q in range(4):
                    c = half * 4 + q
                    P.emit("pe", lambda e, ps=ps, q=q, c=c, k=k: e.transpose(
                        out=ps[:, q * 128:(q + 1) * 128], in_=xs[k][:, c * 128:(c + 1) * 128], identity=self.ident[:]),
                        reads=[xsR[k], self.cR], writes=[pr])
                eng = "dve" if half == 0 else "act"
                dst = self.h[:, half * 4:(half + 1) * 4, j * 128:(j + 1) * 128]
                src = ps[:].rearrange("p (a b) -> p a b", a=4)
                if eng == "dve":
                    fn = lambda e, dst=dst, src=src: e.tensor_copy(out=dst, in_=src)
                else:
                    fn = lambda e, dst=dst, src=src: e.activation(out=dst, in_=src, func=AF.Copy)
                P.emit(eng, fn, reads=[pr], writes=[self.Hr[c2][t] for c2 in range(half * 4, half * 4 + 4)], nosame=True)
        P.barrier(("pe", "act", "dve", "sp"))

    def store(self, i):
        P = self.P
        pc = self.pcol
        hn = self.view(self.AR, [DC, TT], F32)
        ys = [self.view(self.AR + 16384 + k * 4096, [D], F32) for k in range(2)]
        hnR = [Res() for _ in range(DC)]
        ysR = self.outR
        for t in range(NT):
            if getattr(self, "dbg", False):
                for c in range(DC):
                    P.emit("dve", lambda e, c=c, t=t: e.tensor_copy(out=hn[:, c, :], in_=self.h[:, c, t * TT:(t + 1) * TT]),
                           reads=[self.Hr[c][t]], writes=[hnR[c]])
            else:
                self.rmsnorm(t, pc["nf"], lambda c: hn[:, c, :], hnR)
            for blk in range(4):
                j = t * 4 + blk
                k = j % 2
                for half in range(2):
                    ps, pr = self.bank()
                    for q in range(4):
                        c = half * 4 + q
                        P.emit("pe", lambda e, ps=ps, q=q, c=c, blk=blk: e.transpose(
                            out=ps[:, q * 128:(q + 1) * 128], in_=hn[:, c, blk * 128:(blk + 1) * 128],
                            identity=self.ident[:]), reads=[hnR[c], self.cR], writes=[pr])
                    dst = ys[k][:, half * 512:(half + 1) * 512]
                    if half == 0:
                        P.emit("act", lambda e, dst=dst, ps=ps: e.activation(out=dst, in_=ps[:], func=AF.Copy),
                               reads=[pr], writes=[ysR[k]], nosame=True)
                    else:
                        P.emit("dve", lambda e, dst=dst, ps=ps: e.tensor_copy(out=dst, in_=ps[:]),
                               reads=[pr], writes=[ysR[k]], nosame=True)
                o = P.emit("sp", lambda e, j=j, k=k: e.dma_start(out=self.out[i, j * 128:(j + 1) * 128, :], in_=ys[k]),
                           reads=[ysR[k]], dma=("yout", 2))
                self.out_ops.append(o)
        P.barrier(("pe", "act", "dve", "sp"))

    def rmsnorm(self, t, gcol, outf, outR):
        P = self.P
        pc = self.pcol
        tr = slice(t * TT, (t + 1) * TT)
        hR = [self.Hr[c][t] for c in range(DC)]
        P.emit("act", lambda e: e.activation(out=self.sq[:], in_=self.h[:, :, tr], func=AF.Square),
               reads=hR, writes=[self.sqR])
        ps, pr = self.bank()
        for c in range(DC):
            P.emit("pe", lambda e, c=c, ps=ps: e.matmul(ps[:], self.ones[:], self.sq[:, c, :], start=(c == 0), stop=(c == DC - 1)),
                   reads=[self.sqR, self.cR], writes=[pr])
        rs, rr = self.tmp()
        eps = self.parm[:, pc["eps"]:pc["eps"] + 1]
        P.emit("act", lambda e, ps=ps, rs=rs: e.activation(out=rs[:], in_=ps[:], func=AF.Sqrt, bias=eps, scale=1.0 / D),
               reads=[pr, self.cR], writes=[rr])
        P.emit("dve", lambda e, rs=rs: e.reciprocal(out=rs[:], in_=rs[:]), reads=[rr], writes=[rr])
        for c in range(DC):
            g = self.parm[:, gcol + c:gcol + c + 1]
            P.emit("dve", lambda e, c=c, g=g, rs=rs: e.scalar_tensor_tensor(
                out=outf(c), in0=self.h[:, c, tr], scalar=g, in1=rs[:], op0=ALU.mult, op1=ALU.mult),
                reads=[hR[c], rr, self.cR], writes=[outR[c]], nosame=True)

    def wslot(self, k):
        return self.wsl[:, k * 12288:(k + 1) * 12288]

    def load_ffn_group(self, l, which, gi, slot):
        P, dr = self.P, self.dr
        f0, f1 = GROUPS[gi]
        nf = f1 - f0
        gwid = nf * 128
        sl = self.wslot(slot)
        R = self.SA if slot == 0 else self.SB
        wg = dr[which + "_wg"][l, :, f0 * 128:f1 * 128].rearrange("(kc p) f -> p kc f", p=128)
        wu = dr[which + "_wu"][l, :, f0 * 128:f1 * 128].rearrange("(kc p) f -> p kc f", p=128)
        wd = dr[which + "_wd"][l, f0 * 128:f1 * 128, :].rearrange("(fc p) d -> p fc d", p=128)
        og = sl[:, 0:8 * gwid].rearrange("p (kc f) -> p kc f", kc=8)
        ou = sl[:, 4096:4096 + 8 * gwid].rearrange("p (kc f) -> p kc f", kc=8)
        od = sl[:, 8192:8192 + nf * 1024].rearrange("p (fc d) -> p fc d", fc=nf)
        P.emit("pool", lambda e: e.dma_start(out=og, in_=wg), writes=[R], dma=("w", 8))
        P.emit("pool", lambda e: e.dma_start(out=ou, in_=wu), writes=[R], dma=("w", 8))
        P.emit("pool", lambda e: e.dma_start(out=od, in_=wd), writes=[R], dma=("w", 8))
        return og, ou, od, R

    def ffn(self, l, which, gcol):
        P = self.P
        act = [self.view(self.AR + k * 4096, [4, TT], BF16) for k in range(2)]
        actR = [[Res() for _ in range(4)] for _ in range(2)]
        stm = [self.view(self.AR + 8192 + k * 2048, [TT], F32) for k in range(2)]
        stR = [Res(), Res()]
        nst = [0]
        pieces = {}
        pieces[0] = self.load_ffn_group(l, which, 0, 0)

        def up(gi, t):
            og, ou, od, R = pieces[gi]
            f0, f1 = GROUPS[gi]
            tr = slice(t * TT, (t + 1) * TT)
            for i in range(f1 - f0):
                psg, prg = self.bank()
                psu, pru = self.bank()
                for kc in range(DC):
                    P.emit("pe", lambda e, kc=kc, i=i, psg=psg: e.matmul(
                        psg[:], og[:, kc, i * 128:(i + 1) * 128], self.hnf[:, kc, tr], start=(kc == 0), stop=(kc == DC - 1)),
                        reads=[R, self.hnfR[kc][t]], writes=[prg])
                for kc in range(DC):
                    P.emit("pe", lambda e, kc=kc, i=i, psu=psu: e.matmul(
                        psu[:], ou[:, kc, i * 128:(i + 1) * 128], self.hnf[:, kc, tr], start=(kc == 0), stop=(kc == DC - 1)),
                        reads=[R, self.hnfR[kc][t]], writes=[pru])
                k = nst[0] % 2
                nst[0] += 1
                P.emit("act", lambda e, k=k, psg=psg: e.activation(out=stm[k], in_=psg[:], func=AF.Silu),
                       reads=[prg], writes=[stR[k]])
                P.emit("dve", lambda e, k=k, i=i, psu=psu, t=t: e.tensor_tensor(
                    out=act[t % 2][:, i, :], in0=stm[k], in1=psu[:], op=ALU.mult),
                    reads=[stR[k], pru], writes=[actR[t % 2][i]])

        def down(gi, t):
            og, ou, od, R = pieces[gi]
            f0, f1 = GROUPS[gi]
            nf = f1 - f0
            tr = slice(t * TT, (t + 1) * TT)
            for c in range(DC):
                ps, pr = self.bank()
                for i in range(nf):
                    P.emit("pe", lambda e, i=i, c=c, ps=ps: e.matmul(
                        ps[:], od[:, i, c * 128:(c + 1) * 128], act[t % 2][:, i, :], start=(i == 0), stop=(i == nf - 1)),
                        reads=[R, actR[t % 2][i]], writes=[pr])
                P.emit("dve", lambda e, c=c, ps=ps: e.scalar_tensor_tensor(
                    out=self.h[:, c, tr], in0=ps[:], scalar=0.5, in1=self.h[:, c, tr], op0=ALU.mult, op1=ALU.add),
                    reads=[pr, self.Hr[c][t]], writes=[self.Hr[c][t]], nosame=True)

        for gi in range(len(GROUPS)):
            if gi + 1 < len(GROUPS):
                pieces[gi + 1] = self.load_ffn_group(l, which, gi + 1, (gi + 1) % 2)
            for t in range(NT):
                if gi == 0:
                    self.rmsnorm(t, gcol, lambda c, t=t: self.hnf[:, c, t * TT:(t + 1) * TT], [self.hnfR[c][t] for c in range(DC)])
                up(gi, t)
                if t > 0:
                    down(gi, t - 1)
            down(gi, NT - 1)
        P.barrier(("pe", "act", "dve"))

    def mix_even(self, l):
        P, dr = self.P, self.dr
        pc = self.pcol
        e_ = l // 2
        wsl = self.wsl
        SA, SB = self.SA, self.SB
        lastpe = P.last["pe"]
        wqkv = wsl[:, 0:6144].rearrange("p (kc f) -> p kc f", kc=8)
        wab = wsl[:, 6144:14336].rearrange("p (kc f) -> p kc f", kc=8)
        woA = wsl[:, 16384:20480].rearrange("p (j d) -> p j d", j=4)
        woC = wsl[:, 20480:24576].rearrange("p (j d) -> p j d", j=4)
        win = dr["even_w_in"][e_]
        wout = dr["even_w_out"][e_]
        P.emit("pool", lambda e: e.dma_start(out=wqkv, in_=win[:, 0:768].rearrange("(kc p) f -> p kc f", p=128)),
               writes=[SA], dma=("w", 8))
        P.emit("pool", lambda e: e.dma_start(out=wab, in_=win[:, 768:1792].rearrange("(kc p) f -> p kc f", p=128)),
               writes=[SA, SB], dma=("w", 8))
        for g in range(2):
            P.emit("pool", lambda e, g=g: e.dma_start(
                out=woA[g * 64:(g + 1) * 64, :, :], in_=wout[g * 256:(g + 1) * 256, :].rearrange("(j d) n -> d j n", d=64)),
                writes=[SB], dma=("w", 8))
        P.emit("pool", lambda e: e.dma_start(out=woC, in_=wout[512:1024, :].rearrange("(j p) n -> p j n", p=128)),
               writes=[SB], dma=("w", 8))
        o = 0

        def alloc(shape, dt):
            nonlocal o
            v = self.view(o, shape, dt)
            o += int(np.prod(shape)) * (4 if dt == F32 else 2)
            o = (o + 63) // 64 * 64
            return v

        dg = alloc([124, 128], BF16)
        qT = alloc([4, TT], BF16)
        kTb = alloc([5, 128], BF16)
        vb = alloc([5, 128], BF16)
        glub = alloc([4, 30 + TT], BF16)
        PT = [alloc([TT], BF16) for _ in range(4)]
        catA = alloc([4, TT], BF16)
        catC = alloc([4, TT], BF16)
        assert o <= 56 * 1024, o
        mean, var = self.T[2][:], self.T[3][:]
        meanR, varR = self.TR[2], self.TR[3]
        qR = [Res() for _ in range(4)]
        kR, vR, gluR = Res(), Res(), [Res() for _ in range(4)]
        PTR = [Res() for _ in range(4)]
        catAR, catCR = [Res() for _ in range(4)], [Res() for _ in range(4)]
        dgR = [[Res(), Res()] for _ in range(4)]
        npt = [0]
        cw0 = pc["cw"] + e_ * 4 * 31
        for idx in range(124):
            c = idx // 31
            which = idx % 2
            wcol_ = self.parm[:, cw0 + idx:cw0 + idx + 1]
            if which:
                P.emit("act", lambda e, idx=idx, wcol_=wcol_: e.activation(out=dg[:, idx, :], in_=self.identb[:], func=AF.Copy, scale=wcol_),
                       reads=[self.cR], writes=[dgR[c][which]], nosame=True)
            else:
                P.emit("dve", lambda e, idx=idx, wcol_=wcol_: e.tensor_scalar_mul(out=dg[:, idx, :], in0=self.identb[:], scalar1=wcol_),
                       reads=[self.cR], writes=[dgR[c][which]], nosame=True)
        P.emit("dve", lambda e: e.memset(glub[:, :, 0:30], 0.0), writes=gluR)
        sinkrow = self.sinkrow
        for i in range(8):
            P.emit("dve", lambda e, i=i: e.tensor_scalar_mul(out=sinkrow[0:1, e_ * 8 + i, :], in0=self.ones[0:1, :],
                                                             scalar1=self.der[0:1, 32 + e_ * 8 + i:33 + e_ * 8 + i]),
                   reads=[self.cR], writes=[self.gwR], nosame=(i > 0))
        self.bank_pool = [0, 1, 2, 3]
        psY = [self.ps[4 + c] for c in range(4)]
        psYR = [self.psR[4 + c] for c in range(4)]
        self.rmsnorm(0, pc["nm"] + l * 8, lambda c: self.hnm[:, c, :], self.hnmR)
        for t in range(NT):
            tr = slice(t * TT, (t + 1) * TT)
            for cq in range(4):
                ps, pr = self.bank()
                for kc in range(DC):
                    P.emit("pe", lambda e, kc=kc, cq=cq, ps=ps: e.matmul(
                        ps[:], wqkv[:, kc, cq * 128:(cq + 1) * 128], self.hnm[:, kc, :], start=(kc == 0), stop=(kc == DC - 1)),
                        reads=[SA, self.hnmR[kc]], writes=[pr])
                P.emit("act", lambda e, cq=cq, ps=ps: e.activation(out=qT[:, cq, :], in_=ps[:], func=AF.Copy, scale=0.125),
                       reads=[pr], writes=[qR[cq]])
            ps, pr = self.bank()
            for kc in range(DC):
                P.emit("pe", lambda e, kc=kc, ps=ps: e.matmul(
                    ps[:], wqkv[:, kc, 512:640], self.hnm[:, kc, :], start=(kc == 0), stop=(kc == DC - 1)),
                    reads=[SA, self.hnmR[kc]], writes=[pr])
            P.emit("act", lambda e, ps=ps: e.activation(out=kTb[:, 1:5, :], in_=ps[:].rearrange("p (a b) -> p a b", a=4), func=AF.Copy),
                   reads=[pr], writes=[kR])
            ps, pr = self.bank()
            for blk in range(4):
                for kc in range(DC):
                    P.emit("pe", lambda e, kc=kc, blk=blk, ps=ps: e.matmul(
                        ps[:, blk * 128:(blk + 1) * 128], self.hnm[:, kc, blk * 128:(blk + 1) * 128], wqkv[:, kc, 640:768],
                        start=(kc == 0), stop=(kc == DC - 1)),
                        reads=[SA, self.hnmR[kc]], writes=[pr])
            P.emit("act", lambda e, ps=ps: e.activation(out=vb[:, 1:5, :], in_=ps[:].rearrange("p (a b) -> p a b", a=4), func=AF.Copy),
                   reads=[pr], writes=[vR])
            for c in range(4):
                psa, pra = self.bank()
                psb, prb = self.bank()
                for kc in range(DC):
                    P.emit("pe", lambda e, kc=kc, c=c, psa=psa: e.matmul(
                        psa[:], wab[:, kc, c * 128:(c + 1) * 128], self.hnm[:, kc, :], start=(kc == 0), stop=(kc == DC - 1)),
                        reads=[SA, SB, self.hnmR[kc]], writes=[pra])
                for kc in range(DC):
                    P.emit("pe", lambda e, kc=kc, c=c, psb=psb: e.matmul(
                        psb[:], wab[:, kc, 512 + c * 128:512 + (c + 1) * 128], self.hnm[:, kc, :], start=(kc == 0), stop=(kc == DC - 1)),
                        reads=[SA, SB, self.hnmR[kc]], writes=[prb])
                sg, sgR = self.tmp()
                P.emit("act", lambda e, sg=sg, psb=psb: e.activation(out=sg[:], in_=psb[:], func=AF.Sigmoid), reads=[prb], writes=[sgR])
                P.emit("dve", lambda e, sg=sg, psa=psa, c=c: e.tensor_tensor(out=glub[:, c, 30:30 + TT], in0=sg[:], in1=psa[:], op=ALU.mult),
                       reads=[sgR, pra], writes=[gluR[c]])
            if t + 1 < NT:
                self.rmsnorm(t + 1, pc["nm"] + l * 8, lambda c: self.hnm[:, c, :], self.hnmR)
            for n in range(4):
                nbk = t * 4 + n
                halves = ([0] if nbk > 0 else []) + [1]
                psO, prO = self.bank()
                psD, prD = self.bank()
                psSs = [self.bank() for _ in range(2)]
                for gk in range(2):
                    p0, p1 = gk * 64, (gk + 1) * 64
                    units = []
                    for hi, half in enumerate(halves):
                        slot = n + half
                        psS, prS = psSs[hi]
                        P.emit("pe", lambda e, psS=psS, slot=slot, p0=p0, p1=p1, n=n: e.matmul(
                            psS[:].rearrange("p (a b) -> p a b", a=4), kTb[p0:p1, slot, :], qT[p0:p1, :, n * 128:(n + 1) * 128],
                            start=True, stop=True), reads=[kR] + qR, writes=[prS])
                        units.append((hi, half, slot, psS, prS))
                    if True:
                        c = n
                        for k in (range(0, 16) if gk == 0 else range(16, 31)):
                            P.emit("pe", lambda e, c=c, k=k: e.matmul(
                                psY[c][:], dg[:, c * 31 + k, :], glub[:, c, k:k + TT], start=(k == 0), stop=(k == 30)),
                                reads=[dgR[c][0], dgR[c][1], gluR[c]], writes=[psYR[c]])
                    pts = []
                    for hi, half, slot, psS, prS in units:
                        E, ER = self.tmp()
                        P.emit("act", lambda e, E=E, psS=psS: e.activation(out=E[:], in_=psS[:], func=AF.Exp),
                               reads=[prS], writes=[ER])
                        k = npt[0] % 4
                        npt[0] += 1
                        eb = self.EBT[:, half * 1024 + gk * 512: half * 1024 + (gk + 1) * 512]
                        P.emit("dve", lambda e, E=E, k=k, eb=eb: e.tensor_tensor(out=PT[k], in0=E[:], in1=eb, op=ALU.mult),
                               reads=[ER, self.cR], writes=[PTR[k]])
                        pts.append((hi, slot, k))
                    for hi, slot, k in pts:
                        first = hi == 0
                        last = hi == len(halves) - 1
                        P.emit("pe", lambda e, k=k, slot=slot, p0=p0, p1=p1, first=first, last=last, psO=psO: e.matmul(
                            psO[p0:p1, :], vb[:, slot, p0:p1], PT[k], start=first, stop=last),
                            reads=[vR, PTR[k]], writes=[prO])
                        P.emit("pe", lambda e, k=k, p0=p0, p1=p1, first=first, psD=psD: e.matmul(
                            psD[p0:p1, :], self.ones[:, 0:64], PT[k], start=first, stop=False),
                            reads=[self.cR, PTR[k]], writes=[prD])
                    sr = sinkrow[0:1, e_ * 8 + gk * 4: e_ * 8 + gk * 4 + 4, :]
                    P.emit("pe", lambda e, p0=p0, p1=p1, sr=sr, psD=psD: e.matmul(
                        psD[p0:p1, :].rearrange("p (a b) -> p a b", a=4), self.ones[0:1, 0:64], sr, start=False, stop=True),
                        reads=[self.cR, self.gwR], writes=[prD])
                rc, rcR = self.tmp()
                P.emit("dve", lambda e, rc=rc, psD=psD: e.reciprocal(out=rc[:], in_=psD[:]), reads=[prD], writes=[rcR])
                P.emit("dve", lambda e, rc=rc, psO=psO, n=n: e.tensor_tensor(
                    out=catA[:, :, n * 128:(n + 1) * 128], in0=psO[:].rearrange("p (a b) -> p a b", a=4),
                    in1=rc[:].rearrange("p (a b) -> p a b", a=4), op=ALU.mult),
                    reads=[prO, rcR], writes=catAR, nosame=True)
            if t < NT - 1:
                P.emit("act", lambda e: e.activation(out=kTb[:, 0, :], in_=kTb[:, 4, :], func=AF.Copy), reads=[kR], writes=[kR])
                P.emit("act", lambda e: e.activation(out=vb[:, 0, :], in_=vb[:, 4, :], func=AF.Copy), reads=[vR], writes=[vR])
                P.emit("act", lambda e: e.activation(out=glub[:, :, 0:30], in_=glub[:, :, TT:TT + 30], func=AF.Copy),
                       reads=gluR, writes=gluR)
            cbs = [self.parm[:, pc["cb"] + e_ * 4 + c: pc["cb"] + e_ * 4 + c + 1] for c in range(4)]
            for c in range(4):
                P.emit("act", lambda e, c=c: e.activation(out=self.sq[:, c, :], in_=psY[c][:], func=AF.Identity, bias=cbs[c], scale=1.0),
                       reads=[psYR[c], self.cR], writes=[self.sqR], nosame=(c > 0))
            for c in range(4):
                P.emit("act", lambda e, c=c: e.activation(out=self.sq[:, 4 + c, :], in_=psY[c][:], func=AF.Square, bias=cbs[c], scale=1.0),
                       reads=[psYR[c], self.cR], writes=[self.sqR], nosame=True)
            ps1, pr1 = self.bank()
            ps2, pr2 = self.bank()
            for c in range(4):
                P.emit("pe", lambda e, c=c, ps1=ps1: e.matmul(ps1[:], self.ones[:], self.sq[:, c, :], start=(c == 0), stop=(c == 3)),
                       reads=[self.sqR, self.cR], writes=[pr1])
            for c in range(4):
                P.emit("pe", lambda e, c=c, ps2=ps2: e.matmul(ps2[:], self.ones[:], self.sq[:, 4 + c, :], start=(c == 0), stop=(c == 3)),
                       reads=[self.sqR, self.cR], writes=[pr2])
            P.emit("dve", lambda e, ps1=ps1: e.tensor_scalar_mul(out=mean, in0=ps1[:], scalar1=1.0 / 512), reads=[pr1], writes=[meanR])
            P.emit("dve", lambda e: e.tensor_tensor(out=var, in0=mean, in1=mean, op=ALU.mult), reads=[meanR], writes=[varR])
            P.emit("dve", lambda e, ps2=ps2: e.scalar_tensor_tensor(
                out=var, in0=ps2[:], scalar=1.0 / 512, in1=var, op0=ALU.mult, op1=ALU.subtract), reads=[pr2, varR], writes=[varR])
            epsl = self.parm[:, pc["eps"] + 1:pc["eps"] + 2]
            P.emit("act", lambda e: e.activation(out=var, in_=var, func=AF.Sqrt, bias=epsl, scale=1.0), reads=[varR, self.cR], writes=[varR])
            P.emit("dve", lambda e: e.reciprocal(out=var, in_=var), reads=[varR], writes=[varR])
            for c in range(4):
                z, zR = self.tmp()
                P.emit("dve", lambda e, z=z, c=c: e.scalar_tensor_tensor(
                    out=z[:], in0=psY[c][:], scalar=cbs[c], in1=mean, op0=ALU.add, op1=ALU.subtract),
                    reads=[psYR[c], meanR, self.cR], writes=[zR])
                P.emit("dve", lambda e, z=z: e.tensor_tensor(out=z[:], in0=z[:], in1=var, op=ALU.mult), reads=[zR, varR], writes=[zR])
                lg = self.parm[:, pc["lg"] + e_ * 4 + c: pc["lg"] + e_ * 4 + c + 1]
                lb = self.parm[:, pc["lb"] + e_ * 4 + c: pc["lb"] + e_ * 4 + c + 1]
                P.emit("act", lambda e, z=z, c=c, lg=lg, lb=lb: e.activation(out=catC[:, c, :], in_=z[:], func=AF.Silu, bias=lb, scale=lg),
                       reads=[zR, self.cR], writes=[catCR[c]])
            for c in range(DC):
                ps, pr = self.bank()
                for j in range(4):
                    P.emit("pe", lambda e, j=j, c=c, ps=ps: e.matmul(ps[:], woA[:, j, c * 128:(c + 1) * 128], catA[:, j, :], start=(j == 0), stop=False),
                           reads=[SB] + catAR, writes=[pr])
                for j in range(4):
                    P.emit("pe", lambda e, j=j, c=c, ps=ps: e.matmul(ps[:], woC[:, j, c * 128:(c + 1) * 128], catC[:, j, :], start=False, stop=(j == 3)),
                           reads=[SB, catCR[j]], writes=[pr])
                P.emit("dve", lambda e, c=c, ps=ps, tr=tr: e.tensor_tensor(out=self.h[:, c, tr], in0=ps[:], in1=self.h[:, c, tr], op=ALU.add),
                       reads=[pr, self.Hr[c][t]], writes=[self.Hr[c][t]], nosame=True)
        self.bank_pool = list(range(8))
        P.barrier(("pe", "act", "dve"))

    def mix_even_old(self, l):
        P, dr = self.P, self.dr
        pc = self.pcol
        e_ = l // 2
        wsl = self.wsl
        SA, SB = self.SA, self.SB
        wqkv = wsl[:, 0:6144].rearrange("p (kc f) -> p kc f", kc=8)
        wab = wsl[:, 6144:14336].rearrange("p (kc f) -> p kc f", kc=8)
        woA = wsl[:, 16384:20480].rearrange("p (j d) -> p j d", j=4)
        woC = wsl[:, 20480:24576].rearrange("p (j d) -> p j d", j=4)
        win = dr["even_w_in"][e_]
        wout = dr["even_w_out"][e_]
        P.emit("pool", lambda e: e.dma_start(out=wqkv, in_=win[:, 0:768].rearrange("(kc p) f -> p kc f", p=128)),
               writes=[SA], dma=("w", 8))
        P.emit("pool", lambda e: e.dma_start(out=wab, in_=win[:, 768:1792].rearrange("(kc p) f -> p kc f", p=128)),
               writes=[SA, SB], dma=("w", 8))
        for g in range(2):
            P.emit("pool", lambda e, g=g: e.dma_start(
                out=woA[g * 64:(g + 1) * 64, :, :], in_=wout[g * 256:(g + 1) * 256, :].rearrange("(j d) n -> d j n", d=64)),
                writes=[SB], dma=("w", 8))
        P.emit("pool", lambda e: e.dma_start(out=woC, in_=wout[512:1024, :].rearrange("(j p) n -> p j n", p=128)),
               writes=[SB], dma=("w", 8))
        o = 0

        def alloc(shape, dt):
            nonlocal o
            v = self.view(o, shape, dt)
            o += int(np.prod(shape)) * (4 if dt == F32 else 2)
            o = (o + 63) // 64 * 64
            return v

        qT = alloc([4, TT], BF16)
        kTb = alloc([5, 128], BF16)
        vb = alloc([5, 128], BF16)
        glu = alloc([4, 30 + TT], F32)
        y = alloc([4, TT], F32)
        PT = [alloc([TT], BF16) for _ in range(4)]
        catA = alloc([4, TT], BF16)
        catC = alloc([4, TT], BF16)
        mean_b = alloc([TT], F32)
        var_b = alloc([TT], F32)
        meanR, varR = Res(), Res()
        assert o <= 56 * 1024
        qR = [Res() for _ in range(4)]
        kR, vR, gluR, yR = Res(), Res(), [Res() for _ in range(4)], [Res() for _ in range(4)]
        PTR = [Res() for _ in range(4)]
        catAR, catCR = [Res() for _ in range(4)], [Res() for _ in range(4)]
        npt = [0]
        cw0 = pc["cw"] + e_ * 4 * 31
        P.emit("dve", lambda e: e.memset(glu[:, :, 0:30], 0.0), writes=gluR)
        sinkrow = self.sinkrow
        for i in range(8):
            P.emit("dve", lambda e, i=i: e.tensor_scalar_mul(out=sinkrow[0:1, e_ * 8 + i, :], in0=self.ones[0:1, :],
                                                             scalar1=self.der[0:1, 32 + e_ * 8 + i:33 + e_ * 8 + i]),
                   reads=[self.cR], writes=[self.gwR], nosame=(i > 0))
        for t in range(NT):
            tr = slice(t * TT, (t + 1) * TT)
            self.rmsnorm(t, pc["nm"] + l * 8, lambda c: self.hnm[:, c, :], self.hnmR)
            for cq in range(4):
                ps, pr = self.bank()
                for kc in range(DC):
                    P.emit("pe", lambda e, kc=kc, cq=cq, ps=ps: e.matmul(
                        ps[:], wqkv[:, kc, cq * 128:(cq + 1) * 128], self.hnm[:, kc, :], start=(kc == 0), stop=(kc == DC - 1)),
                        reads=[SA, self.hnmR[kc]], writes=[pr])
                P.emit("act", lambda e, cq=cq, ps=ps: e.activation(out=qT[:, cq, :], in_=ps[:], func=AF.Copy, scale=0.125),
                       reads=[pr], writes=[qR[cq]])
            ps, pr = self.bank()
            for kc in range(DC):
                P.emit("pe", lambda e, kc=kc, ps=ps: e.matmul(
                    ps[:], wqkv[:, kc, 512:640], self.hnm[:, kc, :], start=(kc == 0), stop=(kc == DC - 1)),
                    reads=[SA, self.hnmR[kc]], writes=[pr])
            P.emit("act", lambda e, ps=ps: e.activation(out=kTb[:, 1:5, :], in_=ps[:].rearrange("p (a b) -> p a b", a=4), func=AF.Copy),
                   reads=[pr], writes=[kR])
            ps, pr = self.bank()
            for blk in range(4):
                for kc in range(DC):
                    P.emit("pe", lambda e, kc=kc, blk=blk, ps=ps: e.matmul(
                        ps[:, blk * 128:(blk + 1) * 128], self.hnm[:, kc, blk * 128:(blk + 1) * 128], wqkv[:, kc, 640:768],
                        start=(kc == 0), stop=(kc == DC - 1)),
                        reads=[SA, self.hnmR[kc]], writes=[pr])
            P.emit("act", lambda e, ps=ps: e.activation(out=vb[:, 1:5, :], in_=ps[:].rearrange("p (a b) -> p a b", a=4), func=AF.Copy),
                   reads=[pr], writes=[vR])
            for n in range(4):
                nbk = t * 4 + n
                halves = ([0] if nbk > 0 else []) + [1]
                psO, prO = self.bank()
                psD, prD = self.bank()
                for gk in range(2):
                    p0, p1 = gk * 64, (gk + 1) * 64
                    for hi, half in enumerate(halves):
                        slot = n + half
                        psS, prS = self.bank()
                        P.emit("pe", lambda e, psS=psS, slot=slot, p0=p0, p1=p1, n=n: e.matmul(
                            psS[:].rearrange("p (a b) -> p a b", a=4), kTb[p0:p1, slot, :], qT[p0:p1, :, n * 128:(n + 1) * 128],
                            start=True, stop=True), reads=[kR] + qR, writes=[prS])
                        E, ER = self.tmp()
                        P.emit("act", lambda e, E=E, psS=psS: e.activation(out=E[:], in_=psS[:], func=AF.Exp),
                               reads=[prS], writes=[ER])
                        k = npt[0] % 4
                        npt[0] += 1
                        eb = self.EBT[:, half * 1024 + gk * 512: half * 1024 + (gk + 1) * 512]
                        P.emit("dve", lambda e, E=E, k=k, eb=eb: e.tensor_tensor(out=PT[k], in0=E[:], in1=eb, op=ALU.mult),
                               reads=[ER, self.cR], writes=[PTR[k]])
                        first = hi == 0
                        last = hi == len(halves) - 1
                        P.emit("pe", lambda e, k=k, slot=slot, p0=p0, p1=p1, first=first, last=last, psO=psO: e.matmul(
                            psO[p0:p1, :], vb[:, slot, p0:p1], PT[k], start=first, stop=last),
                            reads=[vR, PTR[k]], writes=[prO])
                        P.emit("pe", lambda e, k=k, p0=p0, p1=p1, first=first, psD=psD: e.matmul(
                            psD[p0:p1, :], self.ones[:, 0:64], PT[k], start=first, stop=False),
                            reads=[self.cR, PTR[k]], writes=[prD])
                    sr = sinkrow[0:1, e_ * 8 + gk * 4: e_ * 8 + gk * 4 + 4, :]
                    P.emit("pe", lambda e, p0=p0, p1=p1, sr=sr, psD=psD: e.matmul(
                        psD[p0:p1, :].rearrange("p (a b) -> p a b", a=4), self.ones[0:1, 0:64], sr, start=False, stop=True),
                        reads=[self.cR, self.gwR], writes=[prD])
                rc, rcR = self.tmp()
                P.emit("dve", lambda e, rc=rc, psD=psD: e.reciprocal(out=rc[:], in_=psD[:]), reads=[prD], writes=[rcR])
                P.emit("dve", lambda e, rc=rc, psO=psO, n=n: e.tensor_tensor(
                    out=catA[:, :, n * 128:(n + 1) * 128], in0=psO[:].rearrange("p (a b) -> p a b", a=4),
                    in1=rc[:].rearrange("p (a b) -> p a b", a=4), op=ALU.mult),
                    reads=[prO, rcR], writes=catAR, nosame=True)
            if t < NT - 1:
                P.emit("act", lambda e: e.activation(out=kTb[:, 0, :], in_=kTb[:, 4, :], func=AF.Copy), reads=[kR], writes=[kR])
                P.emit("act", lambda e: e.activation(out=vb[:, 0, :], in_=vb[:, 4, :], func=AF.Copy), reads=[vR], writes=[vR])
            for c in range(4):
                psa, pra = self.bank()
                psb, prb = self.bank()
                for kc in range(DC):
                    P.emit("pe", lambda e, kc=kc, c=c, psa=psa: e.matmul(
                        psa[:], wab[:, kc, c * 128:(c + 1) * 128], self.hnm[:, kc, :], start=(kc == 0), stop=(kc == DC - 1)),
                        reads=[SA, SB, self.hnmR[kc]], writes=[pra])
                for kc in range(DC):
                    P.emit("pe", lambda e, kc=kc, c=c, psb=psb: e.matmul(
                        psb[:], wab[:, kc, 512 + c * 128:512 + (c + 1) * 128], self.hnm[:, kc, :], start=(kc == 0), stop=(kc == DC - 1)),
                        reads=[SA, SB, self.hnmR[kc]], writes=[prb])
                sg, sgR = self.tmp()
                P.emit("act", lambda e, sg=sg, psb=psb: e.activation(out=sg[:], in_=psb[:], func=AF.Sigmoid), reads=[prb], writes=[sgR])
                P.emit("dve", lambda e, sg=sg, psa=psa, c=c: e.tensor_tensor(out=glu[:, c, 30:30 + TT], in0=sg[:], in1=psa[:], op=ALU.mult),
                       reads=[sgR, pra], writes=[gluR[c]])
            for c in range(4):
                wcol = cw0 + c * 31
                cb = self.parm[:, pc["cb"] + e_ * 4 + c: pc["cb"] + e_ * 4 + c + 1]
                P.emit("dve", lambda e, c=c, wcol=wcol, cb=cb: e.tensor_scalar(
                    out=y[:, c, :], in0=glu[:, c, 30:30 + TT], scalar1=self.parm[:, wcol + 30:wcol + 31], scalar2=cb,
                    op0=ALU.mult, op1=ALU.add), reads=[gluR[c], self.cR], writes=[yR[c]])
                for k in range(30):
                    P.emit("dve", lambda e, c=c, k=k, wcol=wcol: e.scalar_tensor_tensor(
                        out=y[:, c, :], in0=glu[:, c, k:k + TT], scalar=self.parm[:, wcol + k:wcol + k + 1], in1=y[:, c, :],
                        op0=ALU.mult, op1=ALU.add), reads=[gluR[c], yR[c]], writes=[yR[c]])
            if t < NT - 1:
                P.emit("act", lambda e: e.activation(out=glu[:, :, 0:30], in_=glu[:, :, TT:TT + 30], func=AF.Copy),
                       reads=gluR, writes=gluR)
            P.emit("act", lambda e: e.activation(out=self.sq[:, 0:4, :], in_=y[:], func=AF.Copy), reads=yR, writes=[self.sqR])
            P.emit("act", lambda e: e.activation(out=self.sq[:, 4:8, :], in_=y[:], func=AF.Square), reads=yR, writes=[self.sqR])
            ps1, pr1 = self.bank()
            ps2, pr2 = self.bank()
            for c in range(4):
                P.emit("pe", lambda e, c=c, ps1=ps1: e.matmul(ps1[:], self.ones[:], self.sq[:, c, :], start=(c == 0), stop=(c == 3)),
                       reads=[self.sqR, self.cR], writes=[pr1])
            for c in range(4):
                P.emit("pe", lambda e, c=c, ps2=ps2: e.matmul(ps2[:], self.ones[:], self.sq[:, 4 + c, :], start=(c == 0), stop=(c == 3)),
                       reads=[self.sqR, self.cR], writes=[pr2])
            mean, var = mean_b, var_b
            P.emit("dve", lambda e, ps1=ps1: e.tensor_scalar_mul(out=mean, in0=ps1[:], scalar1=1.0 / 512), reads=[pr1], writes=[meanR])
            P.emit("dve", lambda e: e.tensor_tensor(out=var, in0=mean, in1=mean, op=ALU.mult), reads=[meanR], writes=[varR])
            P.emit("dve", lambda e, ps2=ps2: e.scalar_tensor_tensor(
                out=var, in0=ps2[:], scalar=1.0 / 512, in1=var, op0=ALU.mult, op1=ALU.subtract), reads=[pr2, varR], writes=[varR])
            epsl = self.parm[:, pc["eps"] + 1:pc["eps"] + 2]
            P.emit("act", lambda e: e.activation(out=var, in_=var, func=AF.Sqrt, bias=epsl, scale=1.0), reads=[varR, self.cR], writes=[varR])
            P.emit("dve", lambda e: e.reciprocal(out=var, in_=var), reads=[varR], writes=[varR])
            for c in range(4):
                z, zR = self.tmp()
                P.emit("dve", lambda e, z=z, c=c: e.tensor_tensor(out=z[:], in0=y[:, c, :], in1=mean, op=ALU.subtract),
                       reads=[yR[c], meanR], writes=[zR])
                P.emit("dve", lambda e, z=z: e.tensor_tensor(out=z[:], in0=z[:], in1=var, op=ALU.mult), reads=[zR, varR], writes=[zR])
                lg = self.parm[:, pc["lg"] + e_ * 4 + c: pc["lg"] + e_ * 4 + c + 1]
                lb = self.parm[:, pc["lb"] + e_ * 4 + c: pc["lb"] + e_ * 4 + c + 1]
                P.emit("act", lambda e, z=z, c=c, lg=lg, lb=lb: e.activation(out=catC[:, c, :], in_=z[:], func=AF.Silu, bias=lb, scale=lg),
                       reads=[zR, self.cR], writes=[catCR[c]])
            if getattr(self, "dbg", 0):
                srcs = {1: [catA[:, c, :] for c in range(4)] + [catC[:, c, :] for c in range(4)],
                        2: [glu[:, c, 30:30 + TT] for c in range(4)] + [y[:, c, :] for c in range(4)],
                        3: [qT[:, c, :] for c in range(4)] + [kTb[:, 1:5, :], self.hnm[:, 0, :], self.hnm[:, 1, :], self.hnm[:, 7, :]],
                        4: [mean_b, var_b, self.sq[:, 0, :], self.sq[:, 4, :]] + [catC[:, c, :] for c in range(4)],
                        5: [self.EBT[:, 0:512], self.EBT[:, 1024:1536], PT[0], PT[1], PT[2], PT[3], catA[:, 0, :], catA[:, 1, :]]}[self.dbg]
                allR = catAR + catCR + gluR + yR + qR + [kR] + self.hnmR + [meanR, varR, self.sqR, self.cR] + PTR
                for c in range(8):
                    dst = self.h[:, c, tr]
                    if self.dbg == 3 and c == 4:
                        dst = dst.rearrange("p (a b) -> p a b", a=4)
                    P.emit("dve", lambda e, c=c, dst=dst: e.tensor_copy(out=dst, in_=srcs[c]), reads=allR + [self.Hr[c][t]], writes=[self.Hr[c][t]])
                continue
            for c in range(DC):
                ps, pr = self.bank()
                for j in range(4):
                    P.emit("pe", lambda e, j=j, c=c, ps=ps: e.matmul(ps[:], woA[:, j, c * 128:(c + 1) * 128], catA[:, j, :], start=(j == 0), stop=False),
                           reads=[SB] + catAR, writes=[pr])
                for j in range(4):
                    P.emit("pe", lambda e, j=j, c=c, ps=ps: e.matmul(ps[:], woC[:, j, c * 128:(c + 1) * 128], catC[:, j, :], start=False, stop=(j == 3)),
                           reads=[SB, catCR[j]], writes=[pr])
                P.emit("dve", lambda e, c=c, ps=ps, tr=tr: e.tensor_tensor(out=self.h[:, c, tr], in0=ps[:], in1=self.h[:, c, tr], op=ALU.add),
                       reads=[pr, self.Hr[c][t]], writes=[self.Hr[c][t]], nosame=True)
        P.barrier(("pe", "act", "dve"))

    def mix_odd(self, l):
        P, dr = self.P, self.dr
        pc = self.pcol
        o_ = l // 2
        wsl = self.wsl
        SA, SB = self.SA, self.SB
        wig = wsl[:, 0:8192].rearrange("p (kc f) -> p kc f", kc=8)
        wir = wsl[:, 8192:16384].rearrange("p (kc f) -> p kc f", kc=8)
        wo = wsl[:, 16384:24576].rearrange("p (j d) -> p j d", j=8)
        win = dr["odd_w_in"][o_]
        wout = dr["odd_w_out"][o_]
        P.emit("pool", lambda e: e.dma_start(out=wig, in_=win[:, 0:1024].rearrange("(kc p) f -> p kc f", p=128)), writes=[SA], dma=("w", 8))
        P.emit("pool", lambda e: e.dma_start(out=wir, in_=win[:, 1024:2048].rearrange("(kc p) f -> p kc f", p=128)), writes=[SA, SB], dma=("w", 8))
        P.emit("pool", lambda e: e.dma_start(out=wo, in_=wout.rearrange("(j p) n -> p j n", p=128)), writes=[SB], dma=("w", 8))
        gwR = self.gwR
        P.emit("pool", lambda e: e.dma_start(out=self.gw[:, 0, :, :], in_=dr["gate_a_w"][o_].rearrange("h i j -> i h j")), writes=[gwR], dma=("gw", 2))
        P.emit("pool", lambda e: e.dma_start(out=self.gw[:, 1, :, :], in_=dr["gate_x_w"][o_].rearrange("h i j -> i h j")), writes=[gwR], dma=("gw", 2))
        o = 0

        def alloc(shape, dt):
            nonlocal o
            v = self.view(o, shape, dt)
            o += int(np.prod(shape)) * (4 if dt == F32 else 2)
            o = (o + 63) // 64 * 64
            return v

        NU = 2
        UPT = DC // NU
        XW = TT + 4
        gate = [alloc([DC, TT], BF16) for _ in range(2)]
        xr = [alloc([NU, XW], F32) for _ in range(2)]
        xc = [alloc([NU, TT], F32) for _ in range(2)]
        xcb = [alloc([NU, TT], BF16) for _ in range(2)]
        bA = alloc([NU, TT], F32)
        bB = alloc([NU, TT], F32)
        bC = alloc([NU, TT], F32)
        halo = alloc([DC, 4], F32)
        carry = alloc([DC], F32)
        assert o <= 56 * 1024, o
        gateR = [[Res() for _ in range(DC)] for _ in range(2)]
        xrR = [[Res() for _ in range(NU)] for _ in range(2)]
        xcR = [[Res() for _ in range(NU)] for _ in range(2)]
        xcbR = [[Res() for _ in range(NU)] for _ in range(2)]
        AR_, BR, CR = ([Res() for _ in range(NU)] for _ in range(3))
        haloR, carryR = [Res() for _ in range(UPT)], [Res() for _ in range(UPT)]
        P.emit("dve", lambda e: e.memset(halo[:], 0.0), writes=haloR)
        P.emit("dve", lambda e: e.memset(carry[:], 0.0), writes=carryR)
        der = self.der
        q25 = self.parm[:, pc["eps"] + 3:pc["eps"] + 4]

        def norm(t):
            self.rmsnorm(t, pc["nm"] + l * 8, lambda c: self.hnm[:, c, :], self.hnmR)

        def gatebr(t):
            g = gate[t % 2]
            for c in range(DC):
                ps, pr = self.bank()
                for kc in range(DC):
                    P.emit("pe", lambda e, kc=kc, c=c, ps=ps: e.matmul(
                        ps[:], wig[:, kc, c * 128:(c + 1) * 128], self.hnm[:, kc, :], start=(kc == 0), stop=(kc == DC - 1)),
                        reads=[SA, self.hnmR[kc]], writes=[pr])
                P.emit("act", lambda e, c=c, ps=ps, g=g: e.activation(out=g[:, c, :], in_=ps[:], func=AF.Gelu_apprx_tanh),
                       reads=[pr], writes=[gateR[t % 2][c]], nosame=True)

        def stA(t, u):
            gi = t * UPT + u
            k = gi % 2
            cs = [NU * u + j for j in range(NU)]
            for j, c in enumerate(cs):
                ps, pr = self.bank()
                for kc in range(DC):
                    P.emit("pe", lambda e, kc=kc, c=c, ps=ps: e.matmul(
                        ps[:], wir[:, kc, c * 128:(c + 1) * 128], self.hnm[:, kc, :], start=(kc == 0), stop=(kc == DC - 1)),
                        reads=[SA, SB, self.hnmR[kc]], writes=[pr])
                P.emit("act", lambda e, j=j, ps=ps, k=k: e.activation(out=xr[k][:, j, 3:3 + TT], in_=ps[:], func=AF.Copy),
                       reads=[pr], writes=[xrR[k][j]], nosame=True)
            P.emit("dve", lambda e, u=u, k=k: e.tensor_copy(out=xr[k][:, :, 0:3], in_=halo[:, NU * u:NU * u + NU, 0:3]),
                   reads=[haloR[u]], writes=xrR[k])
            P.emit("act", lambda e, u=u, k=k: e.activation(out=halo[:, NU * u:NU * u + NU, 0:3], in_=xr[k][:, :, TT:TT + 3], func=AF.Copy),
                   reads=xrR[k], writes=[haloR[u]])
            for kk in (3, 0, 1, 2):
                for j, c in enumerate(cs):
                    wc = pc["lw"] + (o_ * 8 + c) * 4
                    if kk == 3:
                        cb = self.parm[:, pc["lcb"] + o_ * 8 + c: pc["lcb"] + o_ * 8 + c + 1]
                        P.emit("dve", lambda e, j=j, wc=wc, cb=cb, k=k: e.tensor_scalar(
                            out=xc[k][:, j, :], in0=xr[k][:, j, 3:3 + TT], scalar1=self.parm[:, wc + 3:wc + 4], scalar2=cb, op0=ALU.mult, op1=ALU.add),
                            reads=[xrR[k][j], self.cR], writes=[xcR[k][j]])
                    else:
                        P.emit("dve", lambda e, j=j, kk=kk, wc=wc, k=k: e.scalar_tensor_tensor(
                            out=xc[k][:, j, :], in0=xr[k][:, j, kk:kk + TT], scalar=self.parm[:, wc + kk:wc + kk + 1], in1=xc[k][:, j, :],
                            op0=ALU.mult, op1=ALU.add),
                            reads=[xrR[k][j], xcR[k][j]], writes=[xcR[k][j]])

        def stB(t, u):
            gi = t * UPT + u
            k = gi % 2
            cs = [NU * u + j for j in range(NU)]
            P.emit("act", lambda e, k=k: e.activation(out=xcb[k][:], in_=xc[k][:], func=AF.Copy), reads=xcR[k], writes=xcbR[k])
            for j, c in enumerate(cs):
                psr, prr = self.bank()
                psi, pri = self.bank()
                P.emit("pe", lambda e, j=j, c=c, psr=psr, k=k: e.matmul(psr[:], self.gw[:, 0, c, :], xcb[k][:, j, :], start=True, stop=True),
                       reads=[gwR, xcbR[k][j]], writes=[prr])
                P.emit("pe", lambda e, j=j, c=c, psi=psi, k=k: e.matmul(psi[:], self.gw[:, 1, c, :], xcb[k][:, j, :], start=True, stop=True),
                       reads=[gwR, xcbR[k][j]], writes=[pri])
                hba = der[:, 48 + o_ * 8 + c: 48 + o_ * 8 + c + 1]
                hbx = der[:, 64 + o_ * 8 + c: 64 + o_ * 8 + c + 1]
                P.emit("act", lambda e, j=j, psr=psr, hba=hba: e.activation(out=bA[:, j, :], in_=psr[:], func=AF.Tanh, bias=hba, scale=0.5),
                       reads=[prr, self.cR], writes=[AR_[j]])
                P.emit("act", lambda e, j=j, psi=psi, hbx=hbx: e.activation(out=bC[:, j, :], in_=psi[:], func=AF.Tanh, bias=hbx, scale=0.5),
                       reads=[pri, self.cR], writes=[CR[j]])
            for j, c in enumerate(cs):
                clh = der[:, 16 + o_ * 8 + c: 16 + o_ * 8 + c + 1]
                P.emit("act", lambda e, j=j, clh=clh: e.activation(out=bA[:, j, :], in_=bA[:, j, :], func=AF.Exp, bias=clh, scale=clh),
                       reads=[AR_[j], self.cR], writes=[AR_[j]])
            P.emit("act", lambda e: e.activation(out=bB[:], in_=bA[:], func=AF.Square), reads=AR_, writes=BR)
            P.emit("act", lambda e: e.activation(out=bB[:], in_=bB[:], func=AF.Sqrt, bias=q25, scale=-0.25), reads=BR + [self.cR], writes=BR)

        def stC(t, u):
            gi = t * UPT + u
            k = gi % 2
            cs = [NU * u + j for j in range(NU)]
            g = gate[t % 2]
            P.emit("dve", lambda e, k=k: e.scalar_tensor_tensor(out=bC[:], in0=bC[:], scalar=1.0, in1=xc[k][:], op0=ALU.add, op1=ALU.mult),
                   reads=CR + xcR[k], writes=CR)
            P.emit("dve", lambda e: e.tensor_tensor(out=bC[:], in0=bC[:], in1=bB[:], op=ALU.mult), reads=CR + BR, writes=CR)
            for j, c in enumerate(cs):
                P.emit("dve", lambda e, j=j, c=c, k=k: e.tensor_tensor_scan(
                    out=xc[k][:, j, :], data0=bA[:, j, :], data1=bC[:, j, :], initial=carry[:, c:c + 1], op0=ALU.mult, op1=ALU.add),
                    reads=[AR_[j], CR[j], carryR[u], xcR[k][j]], writes=[xcR[k][j]], nosame=(j > 0))
            P.emit("act", lambda e, u=u, k=k: e.activation(out=carry[:, NU * u:NU * u + NU], in_=xc[k][:, :, TT - 1], func=AF.Copy),
                   reads=xcR[k], writes=[carryR[u]])
            P.emit("dve", lambda e, u=u, k=k, g=g: e.tensor_tensor(out=g[:, NU * u:NU * u + NU, :], in0=g[:, NU * u:NU * u + NU, :], in1=xc[k][:], op=ALU.mult),
                   reads=xcR[k] + [gateR[t % 2][c] for c in cs], writes=[gateR[t % 2][c] for c in cs])

        def outp(t):
            g = gate[t % 2]
            tr = slice(t * TT, (t + 1) * TT)
            for c in range(DC):
                ps, pr = self.bank()
                for j in range(DC):
                    P.emit("pe", lambda e, j=j, c=c, ps=ps, g=g: e.matmul(ps[:], wo[:, j, c * 128:(c + 1) * 128], g[:, j, :], start=(j == 0), stop=(j == DC - 1)),
                           reads=[SB, gateR[t % 2][j]], writes=[pr])
                P.emit("dve", lambda e, c=c, ps=ps, tr=tr: e.tensor_tensor(out=self.h[:, c, tr], in0=ps[:], in1=self.h[:, c, tr], op=ALU.add),
                       reads=[pr, self.Hr[c][t]], writes=[self.Hr[c][t]], nosame=True)

        norm(0)
        gatebr(0)
        stA(0, 0)
        for t in range(NT):
            for u in range(UPT):
                if u + 1 < UPT:
                    stA(t, u + 1)
                elif t + 1 < NT:
                    norm(t + 1)
                    gatebr(t + 1)
                    stA(t + 1, 0)
                stB(t, u)
                stC(t, u)
                if u == 0 and t > 0:
                    outp(t - 1)
        outp(NT - 1)
        P.barrier(("pe", "act", "dve"))

    def mix_odd_v2(self, l):
        P, dr = self.P, self.dr
        pc = self.pcol
        o_ = l // 2
        wsl = self.wsl
        SA, SB = self.SA, self.SB
        wig = wsl[:, 0:8192].rearrange("p (kc f) -> p kc f", kc=8)
        wir = wsl[:, 8192:16384].rearrange("p (kc f) -> p kc f", kc=8)
        wo = wsl[:, 16384:24576].rearrange("p (j d) -> p j d", j=8)
        win = dr["odd_w_in"][o_]
        wout = dr["odd_w_out"][o_]
        P.emit("pool", lambda e: e.dma_start(out=wig, in_=win[:, 0:1024].rearrange("(kc p) f -> p kc f", p=128)), writes=[SA], dma=("w", 8))
        P.emit("pool", lambda e: e.dma_start(out=wir, in_=win[:, 1024:2048].rearrange("(kc p) f -> p kc f", p=128)), writes=[SA, SB], dma=("w", 8))
        P.emit("pool", lambda e: e.dma_start(out=wo, in_=wout.rearrange("(j p) n -> p j n", p=128)), writes=[SB], dma=("w", 8))
        gwR = self.gwR
        P.emit("pool", lambda e: e.dma_start(out=self.gw[:, 0, :, :], in_=dr["gate_a_w"][o_].rearrange("h i j -> i h j")), writes=[gwR], dma=("gw", 2))
        P.emit("pool", lambda e: e.dma_start(out=self.gw[:, 1, :, :], in_=dr["gate_x_w"][o_].rearrange("h i j -> i h j")), writes=[gwR], dma=("gw", 2))
        o = 0

        def alloc(shape, dt):
            nonlocal o
            v = self.view(o, shape, dt)
            o += int(np.prod(shape)) * (4 if dt == F32 else 2)
            o = (o + 63) // 64 * 64
            return v

        XW = TT + 4
        gate = alloc([DC, TT], BF16)
        xr = alloc([4, XW], F32)
        xc = alloc([4, TT], F32)
        xcb = alloc([4, TT], BF16)
        bA = alloc([4, TT], F32)
        bB = alloc([4, TT], F32)
        bC = alloc([4, TT], F32)
        halo = alloc([DC, 4], F32)
        carry = alloc([DC], F32)
        assert o <= 56 * 1024, o
        gateR = [Res() for _ in range(DC)]
        xrR, xcR, xcbR, AR_, BR, CR = ([Res() for _ in range(4)] for _ in range(6))
        haloR, carryR = [Res(), Res()], [Res(), Res()]
        P.emit("dve", lambda e: e.memset(halo[:], 0.0), writes=haloR)
        P.emit("dve", lambda e: e.memset(carry[:], 0.0), writes=carryR)
        der = self.der
        q25 = self.parm[:, pc["eps"] + 3:pc["eps"] + 4]
        self.rmsnorm(0, pc["nm"] + l * 8, lambda c: self.hnm[:, c, :], self.hnmR)
        for t in range(NT):
            tr = slice(t * TT, (t + 1) * TT)
            for c in range(DC):
                ps, pr = self.bank()
                for kc in range(DC):
                    P.emit("pe", lambda e, kc=kc, c=c, ps=ps: e.matmul(
                        ps[:], wig[:, kc, c * 128:(c + 1) * 128], self.hnm[:, kc, :], start=(kc == 0), stop=(kc == DC - 1)),
                        reads=[SA, self.hnmR[kc]], writes=[pr])
                P.emit("act", lambda e, c=c, ps=ps: e.activation(out=gate[:, c, :], in_=ps[:], func=AF.Gelu_apprx_tanh),
                       reads=[pr], writes=[gateR[c]], nosame=True)
            for b in range(2):
                cs = [4 * b + j for j in range(4)]
                for j, c in enumerate(cs):
                    ps, pr = self.bank()
                    for kc in range(DC):
                        P.emit("pe", lambda e, kc=kc, c=c, ps=ps: e.matmul(
                            ps[:], wir[:, kc, c * 128:(c + 1) * 128], self.hnm[:, kc, :], start=(kc == 0), stop=(kc == DC - 1)),
                            reads=[SA, SB, self.hnmR[kc]], writes=[pr])
                    P.emit("act", lambda e, j=j, ps=ps: e.activation(out=xr[:, j, 3:3 + TT], in_=ps[:], func=AF.Copy),
                           reads=[pr], writes=[xrR[j]])
                if b == 1 and t + 1 < NT:
                    self.rmsnorm(t + 1, pc["nm"] + l * 8, lambda c: self.hnm[:, c, :], self.hnmR)
                P.emit("dve", lambda e, b=b: e.tensor_copy(out=xr[:, :, 0:3], in_=halo[:, 4 * b:4 * b + 4, 0:3]), reads=[haloR[b]], writes=xrR)
                P.emit("act", lambda e, b=b: e.activation(out=halo[:, 4 * b:4 * b + 4, 0:3], in_=xr[:, :, TT:TT + 3], func=AF.Copy),
                       reads=xrR, writes=[haloR[b]])
                for k in (3, 0, 1, 2):
                    for j, c in enumerate(cs):
                        wc = pc["lw"] + (o_ * 8 + c) * 4
                        if k == 3:
                            cb = self.parm[:, pc["lcb"] + o_ * 8 + c: pc["lcb"] + o_ * 8 + c + 1]
                            P.emit("dve", lambda e, j=j, wc=wc, cb=cb: e.tensor_scalar(
                                out=xc[:, j, :], in0=xr[:, j, 3:3 + TT], scalar1=self.parm[:, wc + 3:wc + 4], scalar2=cb, op0=ALU.mult, op1=ALU.add),
                                reads=[xrR[j], self.cR], writes=[xcR[j]])
                        else:
                            P.emit("dve", lambda e, j=j, k=k, wc=wc: e.scalar_tensor_tensor(
                                out=xc[:, j, :], in0=xr[:, j, k:k + TT], scalar=self.parm[:, wc + k:wc + k + 1], in1=xc[:, j, :], op0=ALU.mult, op1=ALU.add),
                                reads=[xrR[j], xcR[j]], writes=[xcR[j]])
                P.emit("act", lambda e: e.activation(out=xcb[:], in_=xc[:], func=AF.Copy), reads=xcR, writes=xcbR)
                for j, c in enumerate(cs):
                    psr, prr = self.bank()
                    psi, pri = self.bank()
                    P.emit("pe", lambda e, j=j, c=c, psr=psr: e.matmul(psr[:], self.gw[:, 0, c, :], xcb[:, j, :], start=True, stop=True), reads=[gwR, xcbR[j]], writes=[prr])
                    P.emit("pe", lambda e, j=j, c=c, psi=psi: e.matmul(psi[:], self.gw[:, 1, c, :], xcb[:, j, :], start=True, stop=True), reads=[gwR, xcbR[j]], writes=[pri])
                    hba = der[:, 48 + o_ * 8 + c: 48 + o_ * 8 + c + 1]
                    hbx = der[:, 64 + o_ * 8 + c: 64 + o_ * 8 + c + 1]
                    P.emit("act", lambda e, j=j, psr=psr, hba=hba: e.activation(out=bA[:, j, :], in_=psr[:], func=AF.Tanh, bias=hba, scale=0.5),
                           reads=[prr, self.cR], writes=[AR_[j]])
                    P.emit("act", lambda e, j=j, psi=psi, hbx=hbx: e.activation(out=bC[:, j, :], in_=psi[:], func=AF.Tanh, bias=hbx, scale=0.5),
                           reads=[pri, self.cR], writes=[CR[j]])
                for j, c in enumerate(cs):
                    clh = der[:, 16 + o_ * 8 + c: 16 + o_ * 8 + c + 1]
                    P.emit("act", lambda e, j=j, clh=clh: e.activation(out=bA[:, j, :], in_=bA[:, j, :], func=AF.Exp, bias=clh, scale=clh),
                           reads=[AR_[j], self.cR], writes=[AR_[j]])
                P.emit("act", lambda e: e.activation(out=bB[:], in_=bA[:], func=AF.Square), reads=AR_, writes=BR)
                P.emit("act", lambda e: e.activation(out=bB[:], in_=bB[:], func=AF.Sqrt, bias=q25, scale=-0.25), reads=BR + [self.cR], writes=BR)
                P.emit("dve", lambda e: e.scalar_tensor_tensor(out=bC[:], in0=bC[:], scalar=1.0, in1=xc[:], op0=ALU.add, op1=ALU.mult),
                       reads=CR + xcR, writes=CR)
                P.emit("dve", lambda e: e.tensor_tensor(out=bC[:], in0=bC[:], in1=bB[:], op=ALU.mult), reads=CR + BR, writes=CR)
                for j, c in enumerate(cs):
                    P.emit("dve", lambda e, j=j, c=c: e.tensor_tensor_scan(
                        out=xc[:, j, :], data0=bA[:, j, :], data1=bC[:, j, :], initial=carry[:, c:c + 1], op0=ALU.mult, op1=ALU.add),
                        reads=[AR_[j], CR[j], carryR[b], xcR[j]], writes=[xcR[j]], nosame=(j > 0))
                P.emit("act", lambda e, b=b: e.activation(out=carry[:, 4 * b:4 * b + 4], in_=xc[:, :, TT - 1], func=AF.Copy),
                       reads=xcR, writes=[carryR[b]])
                P.emit("dve", lambda e, b=b: e.tensor_tensor(out=gate[:, 4 * b:4 * b + 4, :], in0=gate[:, 4 * b:4 * b + 4, :], in1=xc[:], op=ALU.mult),
                       reads=xcR + [gateR[c] for c in cs], writes=[gateR[c] for c in cs])
            for c in range(DC):
                ps, pr = self.bank()
                for j in range(DC):
                    P.emit("pe", lambda e, j=j, c=c, ps=ps: e.matmul(ps[:], wo[:, j, c * 128:(c + 1) * 128], gate[:, j, :], start=(j == 0), stop=(j == DC - 1)),
                           reads=[SB, gateR[j]], writes=[pr])
                P.emit("dve", lambda e, c=c, ps=ps, tr=tr: e.tensor_tensor(out=self.h[:, c, tr], in0=ps[:], in1=self.h[:, c, tr], op=ALU.add),
                       reads=[pr, self.Hr[c][t]], writes=[self.Hr[c][t]], nosame=True)
        P.barrier(("pe", "act", "dve"))

    def mix_odd_old(self, l):
        P, dr = self.P, self.dr
        pc = self.pcol
        o_ = l // 2
        wsl = self.wsl
        SA, SB = self.SA, self.SB
        wig = wsl[:, 0:8192].rearrange("p (kc f) -> p kc f", kc=8)
        wir = wsl[:, 8192:16384].rearrange("p (kc f) -> p kc f", kc=8)
        wo = wsl[:, 16384:24576].rearrange("p (j d) -> p j d", j=8)
        win = dr["odd_w_in"][o_]
        wout = dr["odd_w_out"][o_]
        P.emit("pool", lambda e: e.dma_start(out=wig, in_=win[:, 0:1024].rearrange("(kc p) f -> p kc f", p=128)), writes=[SA], dma=("w", 8))
        P.emit("pool", lambda e: e.dma_start(out=wir, in_=win[:, 1024:2048].rearrange("(kc p) f -> p kc f", p=128)), writes=[SA, SB], dma=("w", 8))
        P.emit("pool", lambda e: e.dma_start(out=wo, in_=wout.rearrange("(j p) n -> p j n", p=128)), writes=[SB], dma=("w", 8))
        gwR = self.gwR
        P.emit("pool", lambda e: e.dma_start(out=self.gw[:, 0, :, :], in_=dr["gate_a_w"][o_].rearrange("h i j -> i h j")), writes=[gwR], dma=("gw", 2))
        P.emit("pool", lambda e: e.dma_start(out=self.gw[:, 1, :, :], in_=dr["gate_x_w"][o_].rearrange("h i j -> i h j")), writes=[gwR], dma=("gw", 2))
        o = 0

        def alloc(shape, dt):
            nonlocal o
            v = self.view(o, shape, dt)
            o += int(np.prod(shape)) * (4 if dt == F32 else 2)
            o = (o + 63) // 64 * 64
            return v

        gate = alloc([DC, TT], BF16)
        mo = alloc([DC, TT], BF16)
        xr = [alloc([TT + 4], F32) for _ in range(2)]
        xcb = [alloc([TT], BF16) for _ in range(2)]
        halo = alloc([DC, 4], F32)
        carry = alloc([DC], F32)
        NB = 2
        xc = [alloc([TT], F32) for _ in range(NB)]
        bA = [alloc([TT], F32) for _ in range(NB)]
        bB = [alloc([TT], F32) for _ in range(NB)]
        bC = [alloc([TT], F32) for _ in range(NB)]
        hs = [alloc([TT], F32) for _ in range(NB)]
        assert o <= 56 * 1024
        gateR, moR = [Res() for _ in range(DC)], [Res() for _ in range(DC)]
        xrR, xcbR = [Res(), Res()], [Res(), Res()]
        haloR, carryR = [Res() for _ in range(DC)], [Res() for _ in range(DC)]
        xcR, AR_, BR, CR, hsR = ([Res() for _ in range(NB)] for _ in range(5))
        P.emit("dve", lambda e: e.memset(halo[:], 0.0), writes=haloR)
        P.emit("dve", lambda e: e.memset(carry[:], 0.0), writes=carryR)
        der = self.der
        for t in range(NT):
            tr = slice(t * TT, (t + 1) * TT)
            self.rmsnorm(t, pc["nm"] + l * 8, lambda c: self.hnm[:, c, :], self.hnmR)
            for c in range(DC):
                ps, pr = self.bank()
                for kc in range(DC):
                    P.emit("pe", lambda e, kc=kc, c=c, ps=ps: e.matmul(
                        ps[:], wig[:, kc, c * 128:(c + 1) * 128], self.hnm[:, kc, :], start=(kc == 0), stop=(kc == DC - 1)),
                        reads=[SA, self.hnmR[kc]], writes=[pr])
                P.emit("act", lambda e, c=c, ps=ps: e.activation(out=gate[:, c, :], in_=ps[:], func=AF.Gelu_apprx_tanh),
                       reads=[pr], writes=[gateR[c]])
            for c in range(DC):
                k = c % 2
                ps, pr = self.bank()
                for kc in range(DC):
                    P.emit("pe", lambda e, kc=kc, c=c, ps=ps: e.matmul(
                        ps[:], wir[:, kc, c * 128:(c + 1) * 128], self.hnm[:, kc, :], start=(kc == 0), stop=(kc == DC - 1)),
                        reads=[SA, SB, self.hnmR[kc]], writes=[pr])
                P.emit("act", lambda e, k=k, ps=ps: e.activation(out=xr[k][:, 3:3 + TT], in_=ps[:], func=AF.Copy), reads=[pr], writes=[xrR[k]])
                P.emit("dve", lambda e, k=k, c=c: e.tensor_copy(out=xr[k][:, 0:3], in_=halo[:, c, 0:3]), reads=[haloR[c]], writes=[xrR[k]])
                P.emit("act", lambda e, k=k, c=c: e.activation(out=halo[:, c, 0:3], in_=xr[k][:, TT:TT + 3], func=AF.Copy), reads=[xrR[k]], writes=[haloR[c]])
                wc = pc["lw"] + (o_ * 8 + c) * 4
                cb = self.parm[:, pc["lcb"] + o_ * 8 + c: pc["lcb"] + o_ * 8 + c + 1]
                P.emit("dve", lambda e, k=k, wc=wc, cb=cb: e.tensor_scalar(
                    out=xc[k], in0=xr[k][:, 3:3 + TT], scalar1=self.parm[:, wc + 3:wc + 4], scalar2=cb, op0=ALU.mult, op1=ALU.add),
                    reads=[xrR[k], self.cR], writes=[xcR[k]])
                for j in range(3):
                    P.emit("dve", lambda e, k=k, j=j, wc=wc: e.scalar_tensor_tensor(
                        out=xc[k], in0=xr[k][:, j:j + TT], scalar=self.parm[:, wc + j:wc + j + 1], in1=xc[k], op0=ALU.mult, op1=ALU.add),
                        reads=[xrR[k], xcR[k]], writes=[xcR[k]])
                P.emit("act", lambda e, k=k: e.activation(out=xcb[k], in_=xc[k], func=AF.Copy), reads=[xcR[k]], writes=[xcbR[k]])
                psr, prr = self.bank()
                psi, pri = self.bank()
                P.emit("pe", lambda e, k=k, c=c, psr=psr: e.matmul(psr[:], self.gw[:, 0, c, :], xcb[k], start=True, stop=True), reads=[gwR, xcbR[k]], writes=[prr])
                P.emit("pe", lambda e, k=k, c=c, psi=psi: e.matmul(psi[:], self.gw[:, 1, c, :], xcb[k], start=True, stop=True), reads=[gwR, xcbR[k]], writes=[pri])
                gab = self.parm[:, pc["gab"] + o_ * 8 + c: pc["gab"] + o_ * 8 + c + 1]
                gxb = self.parm[:, pc["gxb"] + o_ * 8 + c: pc["gxb"] + o_ * 8 + c + 1]
                cl = der[:, o_ * 8 + c: o_ * 8 + c + 1]
                cl2 = der[:, 16 + o_ * 8 + c: 16 + o_ * 8 + c + 1]
                one = self.parm[:, pc["eps"] + 2:pc["eps"] + 3]
                P.emit("act", lambda e, k=k, psr=psr, gab=gab: e.activation(out=bA[k], in_=psr[:], func=AF.Sigmoid, bias=gab, scale=1.0), reads=[prr, self.cR], writes=[AR_[k]])
                P.emit("act", lambda e, k=k, psi=psi, gxb=gxb: e.activation(out=bC[k], in_=psi[:], func=AF.Sigmoid, bias=gxb, scale=1.0), reads=[pri, self.cR], writes=[CR[k]])
                P.emit("act", lambda e, k=k, cl2=cl2: e.activation(out=bB[k], in_=bA[k], func=AF.Exp, scale=cl2), reads=[AR_[k]], writes=[BR[k]])
                P.emit("act", lambda e, k=k, cl=cl: e.activation(out=bA[k], in_=bA[k], func=AF.Exp, scale=cl), reads=[AR_[k], BR[k]], writes=[AR_[k]])
                P.emit("act", lambda e, k=k, one=one: e.activation(out=bB[k], in_=bB[k], func=AF.Sqrt, bias=one, scale=-1.0), reads=[BR[k], self.cR], writes=[BR[k]])
                P.emit("dve", lambda e, k=k: e.tensor_tensor(out=bC[k], in0=bC[k], in1=xc[k], op=ALU.mult), reads=[CR[k], xcR[k]], writes=[CR[k]])
                P.emit("dve", lambda e, k=k: e.tensor_tensor(out=bC[k], in0=bC[k], in1=bB[k], op=ALU.mult), reads=[CR[k], BR[k]], writes=[CR[k]])
                P.emit("dve", lambda e, k=k, c=c: e.tensor_tensor_scan(out=hs[k], data0=bA[k], data1=bC[k], initial=carry[:, c:c + 1], op0=ALU.mult, op1=ALU.add),
                       reads=[AR_[k], CR[k], carryR[c]], writes=[hsR[k]])
                P.emit("act", lambda e, k=k, c=c: e.activation(out=carry[:, c:c + 1], in_=hs[k][:, TT - 1:TT], func=AF.Copy), reads=[hsR[k]], writes=[carryR[c]])
                P.emit("dve", lambda e, k=k, c=c: e.tensor_tensor(out=mo[:, c, :], in0=hs[k], in1=gate[:, c, :], op=ALU.mult), reads=[hsR[k], gateR[c]], writes=[moR[c]])
            for c in range(DC):
                ps, pr = self.bank()
                for j in range(DC):
                    P.emit("pe", lambda e, j=j, c=c, ps=ps: e.matmul(ps[:], wo[:, j, c * 128:(c + 1) * 128], mo[:, j, :], start=(j == 0), stop=(j == DC - 1)),
                           reads=[SB, moR[j]], writes=[pr])
                P.emit("dve", lambda e, c=c, ps=ps, tr=tr: e.tensor_tensor(out=self.h[:, c, tr], in0=ps[:], in1=self.h[:, c, tr], op=ALU.add),
                       reads=[pr, self.Hr[c][t]], writes=[self.Hr[c][t]], nosame=True)
        P.barrier(("pe", "act", "dve"))


def prep_shared(inp):
    params, pcol = pack_params(inp)
    biasT, maskneg = attn_consts(inp["rel_bias"])
    shared = {
        "params": params,
        "ident": np.eye(128, dtype=np.float32),
        "biasT": biasT.reshape(128, 2048),
        "maskneg": maskneg.reshape(128, 2048),
        "even_w_in": np.ascontiguousarray(np.asarray(inp["even_w_in"], np.float32)[:, :, qperm()]),
    }
    for k in ("ffn1_wg", "ffn1_wu", "ffn1_wd", "ffn2_wg", "ffn2_wu", "ffn2_wd", "even_w_out", "odd_w_in",
              "odd_w_out", "gate_a_w", "gate_x_w"):
        shared[k] = np.ascontiguousarray(np.asarray(inp[k], np.float32))
    return shared, pcol, params.shape[1]


def kernel(**inputs):
    x = np.asarray(inputs["x"], np.float32)
    shared, pcol, npar = prep_shared(inputs)
    n_seq = x.shape[0] // N_CORES
    b = Builder(n_seq, DEPTH)
    b.npar = npar
    nc = b.build(pcol)
    in_maps = []
    for c in range(N_CORES):
        m = dict(shared)
        m["x"] = np.ascontiguousarray(x[c * n_seq:(c + 1) * n_seq])
        in_maps.append(m)
    res = run_bass_kernel_spmd(nc, in_maps, core_ids=list(range(N_CORES)))
    return np.concatenate([r["out"] for r in res.results], axis=0)
```

```python
import math
from contextlib import ExitStack

import numpy as np
import concourse.bass as bass
import concourse.mybir as mybir
from concourse.bass_utils import run_bass_kernel_spmd

F32 = mybir.dt.float32
BF16 = mybir.dt.bfloat16
AF = mybir.ActivationFunctionType
ALU = mybir.AluOpType

N_CORES = 8
D = 1024
S = 2048
DEPTH = 4
DFF = 2816
DC = 8
FC = 22
TT = 512
NT = 4
EVEN_IN = 1792
RMS_EPS = 1e-6
LN_EPS = 1e-5
GROUPS = [(0, 4), (4, 8), (8, 12), (12, 16), (16, 19), (19, 22)]

ENGS = ("pe", "act", "dve", "pool", "sp")
SEM_CAP = 30000


class Res:
    __slots__ = ("w", "r", "pr")

    def __init__(self):
        self.w = []
        self.r = []
        self.pr = []


class Op:
    __slots__ = ("eng", "fn", "deps", "dma", "signal", "tok", "waits")

    def __init__(self, eng, fn, dma):
        self.eng = eng
        self.fn = fn
        self.dma = dma
        self.deps = []
        self.signal = dma is not None
        self.tok = None
        self.waits = None


class Prog:
    def __init__(self, nc):
        self.nc = nc
        self.ops = {e: [] for e in ENGS}
        self.last = {e: None for e in ENGS}
        self.pend = {}
        self.dmacnt = {}

    def emit(self, eng, fn, reads=(), writes=(), dma=None, nosame=False):
        if dma is not None:
            name, n = dma
            j = self.dmacnt.get(name, 0)
            self.dmacnt[name] = j + 1
            dma = (name, j % n)
        op = Op(eng, fn, dma)
        deps = {}
        for r in reads:
            for o in r.w:
                deps[id(o)] = o
        joins = []
        for w in writes:
            if dma is not None and w.w and not w.r and all(o.dma is not None for o in w.w):
                joins.append(w)
                for o in w.pr:
                    deps[id(o)] = o
                continue
            for o in w.w:
                deps[id(o)] = o
            for o in w.r:
                deps[id(o)] = o
        for d in deps.values():
            if d.eng == eng and d.dma is None and dma is None:
                if eng == "pe" or nosame:
                    continue
            op.deps.append(d)
        if eng in self.pend:
            for d in self.pend.pop(eng):
                if d is not None and not (d.eng == eng and d.dma is None):
                    op.deps.append(d)
        for r in reads:
            r.r.append(op)
        for w in writes:
            if any(w is j for j in joins):
                w.w.append(op)
            else:
                w.pr = w.r
                w.w = [op]
                w.r = []
        self.ops[eng].append(op)
        self.last[eng] = op
        return op

    def barrier(self, engs):
        lasts = [self.last[e] for e in engs]
        for e in engs:
            self.pend[e] = list(self.pend.get(e, [])) + lasts

    def build(self, stack):
        nc = self.nc
        for e in ENGS:
            for op in self.ops[e]:
                for d in op.deps:
                    d.signal = True
        sems = {}
        cnt = {}
        for e in ENGS:
            c = 0
            for op in self.ops[e]:
                if op.dma is not None:
                    k = ("d",) + op.dma
                    cnt[k] = cnt.get(k, 0) + 16
                    op.tok = (k, cnt[k])
                elif op.signal:
                    c += 1
                    op.tok = (("e", e, (c - 1) // SEM_CAP), (c - 1) % SEM_CAP + 1)
        for e in ENGS:
            seen = {}
            for op in self.ops[e]:
                ws = {}
                for d in op.deps:
                    k, v = d.tok
                    if seen.get(k, 0) >= v:
                        continue
                    if ws.get(k, 0) < v:
                        ws[k] = v
                seen.update(ws)
                op.waits = list(ws.items())
                if op.tok is not None and op.tok[0] not in sems:
                    sems[op.tok[0]] = None
        for k in list(sems):
            sems[k] = stack.enter_context(nc.semaphore("s_" + "_".join(str(x) for x in k)))
        block = stack.enter_context(nc.Block())
        names = {"pe": "tensor", "act": "scalar", "dve": "vector", "pool": "gpsimd", "sp": "sync"}
        ops = self.ops

        def run(e, eng):
            for op in ops[e]:
                for k, v in op.waits:
                    eng.wait_ge(sems[k], v)
                ins = op.fn(eng)
                if op.tok is not None:
                    ins.then_inc(sems[op.tok[0]], 16 if op.dma is not None else 1)

        for e in ENGS:
            if not ops[e]:
                continue

            def f(eng, e=e):
                run(e, eng)

            getattr(block, names[e])(f)


def _pcols(v, nchunk):
    v = np.asarray(v, np.float32)
    lead = v.shape[:-1]
    v = v.reshape(lead + (nchunk, 128))
    v = np.moveaxis(v, -1, 0)
    return np.ascontiguousarray(v).reshape(128, -1)


class PL:
    pass


def pack_params(inp):
    cols = []
    off = {}

    def add(name, arr):
        off[name] = sum(c.shape[1] for c in cols)
        cols.append(np.ascontiguousarray(arr, dtype=np.float32))

    add("n1", _pcols(inp["norm_ffn1"], 8))
    add("nm", _pcols(inp["norm_mix"], 8))
    add("n2", _pcols(inp["norm_ffn2"], 8))
    add("nf", _pcols(inp["norm_final"], 8))
    cw = np.asarray(inp["conv_b_w"], np.float32)
    cw = cw.reshape(2, 31, 4, 128).transpose(3, 0, 2, 1)
    add("cw", cw.reshape(128, -1))
    add("cb", _pcols(inp["conv_b_b"], 4))
    add("lg", _pcols(inp["conv_ln_g"], 4))
    add("lb", _pcols(inp["conv_ln_b"], 4))
    lw = np.asarray(inp["lru_conv_w"], np.float32)
    lw = lw.reshape(2, 4, 8, 128).transpose(3, 0, 2, 1)
    add("lw", lw.reshape(128, -1))
    add("lcb", _pcols(inp["lru_conv_b"], 8))
    add("gab", _pcols(inp["gate_a_b"], 8))
    add("gxb", _pcols(inp["gate_x_b"], 8))
    add("lam", _pcols(inp["lru_lambda"], 8))
    sk = np.asarray(inp["attn_sinks"], np.float32).reshape(1, 16)
    add("sink", np.broadcast_to(sk, (128, 16)))
    add("eps", np.broadcast_to(np.array([[RMS_EPS, LN_EPS, 1.0, 0.25]], np.float32), (128, 4)))
    return np.concatenate(cols, axis=1), off


def t5_bucket_np(dist):
    n = np.maximum(dist, 0)
    max_exact = 16
    nf = np.maximum(n, max_exact).astype(np.float32)
    large = max_exact + (np.log(nf / np.float32(max_exact)) / np.float32(math.log(128 / max_exact))
                         * np.float32(32 - max_exact)).astype(np.int32)
    large = np.minimum(large, 31)
    return np.where(n < max_exact, n, large)


def attn_consts(rel_bias):
    q = np.arange(128)[None, :]
    s = np.arange(128)[:, None]
    out_b = np.zeros((128, 2, 8, 128), np.float32)
    out_m = np.zeros((128, 2, 8, 128), np.float32)
    rb = np.asarray(rel_bias, np.float32)
    for half in range(2):
        dist = q + 128 - (s + 128 * half)
        ok = (dist >= 0) & (dist < 128)
        bk = t5_bucket_np(dist)
        g = rb[bk]
        out_b[:, half] = np.transpose(g, (0, 2, 1))
        out_m[:, half] = np.where(ok, 0.0, -1e30)[:, None, :]
    return out_b, out_m


def qperm():
    idx = []
    for cq in range(4):
        idx += list(range(cq * 64, cq * 64 + 64))
        idx += list(range((4 + cq) * 64, (4 + cq) * 64 + 64))
    return np.array(idx + list(range(512, EVEN_IN)))


class Builder:
    def __init__(self, n_seq, layers, stages=("ffn1", "mix", "ffn2")):
        self.n_seq = n_seq
        self.layers = layers
        self.stages = stages
        self.nc = bass.Bass("TRN2", target_bir_lowering=False)
        self.nb = 0
        self.bank_pool = list(range(8))

    def bank(self):
        pool = self.bank_pool
        b = pool[self.nb % len(pool)]
        self.nb += 1
        return self.ps[b], self.psR[b]

    def tmp(self):
        k = self.ntmp % 2
        self.ntmp += 1
        return self.T[k], self.TR[k]

    def view(self, off_bytes, shape, dt, p0=0, p1=128):
        n = int(np.prod(shape))
        if dt == F32:
            assert off_bytes % 4 == 0
            a = self.arena32[p0:p1, off_bytes // 4: off_bytes // 4 + n]
        else:
            assert off_bytes % 2 == 0
            a = self.arena[p0:p1, off_bytes // 2: off_bytes // 2 + n]
        if len(shape) == 2:
            a = a.rearrange("p (a b) -> p a b", a=shape[0])
        elif len(shape) == 3:
            a = a.rearrange("p (a b c) -> p a b c", a=shape[0], b=shape[1])
        return a

    def build(self, pcol):
        nc = self.nc
        n_seq = self.n_seq
        self.pcol = pcol
        npar = self.npar
        dr = {}

        def din(name, shape):
            dr[name] = nc.dram_tensor(name, list(shape), F32, kind="ExternalInput").ap()

        din("x", [n_seq, S, D])
        din("params", [128, npar])
        din("ident", [128, 128])
        din("biasT", [128, 2048])
        din("maskneg", [128, 2048])
        for w in ("ffn1", "ffn2"):
            din(w + "_wg", [DEPTH, D, DFF])
            din(w + "_wu", [DEPTH, D, DFF])
            din(w + "_wd", [DEPTH, DFF, D])
        din("even_w_in", [2, D, EVEN_IN])
        din("even_w_out", [2, D, D])
        din("odd_w_in", [2, D, 2 * D])
        din("odd_w_out", [2, D, D])
        din("gate_a_w", [2, 8, 128, 128])
        din("gate_x_w", [2, 8, 128, 128])
        self.dr = dr
        self.out = nc.dram_tensor("out", [n_seq, S, D], F32, kind="ExternalOutput").ap()

        with ExitStack() as st:
            sb = lambda name, shape, dt: st.enter_context(nc.sbuf_tensor(name, shape, dt))
            self.h = sb("h", [128, DC, S], F32)
            self.wsl = sb("wsl", [128, 24576], BF16)
            self.arena = sb("arena", [128, 28 * 1024], BF16)
            self.arena32 = self.arena[:].bitcast(F32)
            self.hnm = sb("hnm", [128, DC, TT], BF16)
            self.parm = sb("parm", [128, npar], F32)
            self.der = sb("der", [128, 128], F32)
            self.ident = sb("ident_sb", [128, 128], F32)
            self.ones = sb("ones", [128, 128], BF16)
            self.identb = sb("identb", [128, 128], BF16)
            self.EBT = sb("EBT", [128, 2048], F32)
            self.sq = sb("sq", [128, DC, TT], BF16)
            self.gw = sb("gw", [128, 2, 8, 128], BF16)
            self.sinkrow = self.gw[0:1, :, :, :].rearrange("p a b c -> p (a b) c")
            self.T = [sb("T%d" % i, [128, TT], F32) for i in range(4)]
            self.TR = [Res() for _ in range(4)]
            self.ntmp = 0
            self.ps = [st.enter_context(nc.psum_tensor("ps%d" % i, [128, TT], F32)) for i in range(8)]
            self.psR = [Res() for _ in range(8)]
            self.hnf = self.view(0, [DC, S], BF16)
            self.AR = 32 * 1024

            self.P = Prog(nc)
            self.Hr = [[Res() for _ in range(NT)] for _ in range(DC)]
            self.hnfR = [[Res() for _ in range(NT)] for _ in range(DC)]
            self.hnmR = [Res() for _ in range(DC)]
            self.sqR = Res()
            self.SA, self.SB = Res(), Res()
            self.cR = Res()
            self.gwR = Res()

            self.setup()
            for i in range(n_seq):
                self.load(i)
                for l in range(self.layers):
                    if "ffn1" in self.stages:
                        self.ffn(l, "ffn1", pcol["n1"] + l * 8)
                    if "mix" in self.stages:
                        if l % 2 == 0:
                            self.mix_even(l)
                        else:
                            self.mix_odd(l)
                    if "ffn2" in self.stages:
                        self.ffn(l, "ffn2", pcol["n2"] + l * 8)
                self.store(i)
            P = self.P
            fin = P.emit("sp", lambda e: e.nop(), reads=self.outR, writes=self.outR)
            for o in self.out_ops[-4:]:
                if o not in fin.deps:
                    fin.deps.append(o)
            P.build(st)
        return nc

    def setup(self):
        P, dr = self.P, self.dr
        pc = self.pcol
        parm, der = self.parm, self.der
        cR = self.cR
        self.outR = [Res(), Res()]
        self.out_ops = []
        P.emit("sp", lambda e: e.dma_start(out=parm[:], in_=dr["params"]), writes=[cR], dma=("c0", 1))
        P.emit("sp", lambda e: e.dma_start(out=self.ident[:], in_=dr["ident"]), writes=[cR], dma=("c1", 1))
        bt = self.view(self.AR, [2048], F32)
        mk = self.view(self.AR + 8192, [2048], F32)
        tR = Res()
        P.emit("sp", lambda e: e.dma_start(out=bt, in_=dr["biasT"]), writes=[tR], dma=("c2", 1))
        P.emit("sp", lambda e: e.dma_start(out=mk, in_=dr["maskneg"]), writes=[tR], dma=("c3", 1))
        P.emit("dve", lambda e: e.memset(self.ones[:], 1.0), writes=[cR])
        P.emit("dve", lambda e: e.tensor_copy(out=self.identb[:], in_=self.ident[:]), reads=[cR], writes=[cR])
        P.emit("dve", lambda e: e.tensor_tensor(out=bt, in0=bt, in1=mk, op=ALU.add), reads=[tR], writes=[tR])
        P.emit("act", lambda e: e.activation(out=self.EBT[:], in_=bt, func=AF.Exp), reads=[tR], writes=[cR])
        lam = parm[:, pc["lam"]:pc["lam"] + 16]
        dR = Res()
        P.emit("act", lambda e: e.activation(out=der[:, 0:16], in_=lam, func=AF.Exp, scale=-1.0), reads=[cR], writes=[dR])
        P.emit("dve", lambda e: e.tensor_scalar_add(out=der[:, 0:16], in0=der[:, 0:16], scalar1=1.0), reads=[dR], writes=[dR])
        P.emit("act", lambda e: e.activation(out=der[:, 0:16], in_=der[:, 0:16], func=AF.Ln), reads=[dR], writes=[dR])
        P.emit("dve", lambda e: e.tensor_scalar_mul(out=der[:, 16:32], in0=der[:, 0:16], scalar1=-16.0), reads=[dR], writes=[dR])
        P.emit("dve", lambda e: e.tensor_scalar_mul(out=der[:, 0:16], in0=der[:, 0:16], scalar1=-8.0), reads=[dR], writes=[dR])
        P.emit("act", lambda e: e.activation(out=der[:, 32:48], in_=parm[:, pc["sink"]:pc["sink"] + 16], func=AF.Exp),
               reads=[cR, dR], writes=[dR])
        P.emit("dve", lambda e: e.tensor_scalar_mul(out=der[:, 16:32], in0=der[:, 0:16], scalar1=0.5), reads=[dR], writes=[dR])
        P.emit("dve", lambda e: e.tensor_scalar_mul(out=der[:, 48:64], in0=parm[:, pc["gab"]:pc["gab"] + 16], scalar1=0.5), reads=[dR, cR], writes=[dR])
        P.emit("dve", lambda e: e.tensor_scalar_mul(out=der[:, 64:80], in0=parm[:, pc["gxb"]:pc["gxb"] + 16], scalar1=0.5), reads=[dR, cR], writes=[dR])
        last = P.emit("dve", lambda e: e.memset(self.T[0][:], 0.0), reads=[dR, tR], writes=[cR, tR, dR])
        P.barrier(("pe", "act", "dve", "sp"))

    def load(self, i):
        P = self.P
        x = self.dr["x"]
        xs = [self.view(self.AR + k * 4096, [D], F32) for k in range(2)]
        xsR = [Res(), Res()]
        for j in range(16):
            k = j % 2
            t = j // 4
            P.emit("sp", lambda e, j=j, k=k: e.dma_start(out=xs[k], in_=x[i, j * 128:(j + 1) * 128, :]),
                   writes=[xsR[k]], dma=("xin", 2))
            for half in range(2):
                ps, pr = self.bank()
                for q in range(4):
                    c = half * 4 + q
                    P.emit("pe", lambda e, ps=ps, q=q, c=c, k=k: e.transpose(
                        out=ps[:, q * 128:(q + 1) * 128], in_=xs[k][:, c * 128:(c + 1) * 128], identity=self.ident[:]),
                        reads=[xsR[k], self.cR], writes=[pr])
                eng = "dve" if half == 0 else "act"
                dst = self.h[:, half * 4:(half + 1) * 4, j * 128:(j + 1) * 128]
                src = ps[:].rearrange("p (a b) -> p a b", a=4)
                if eng == "dve":
                    fn = lambda e, dst=dst, src=src: e.tensor_copy(out=dst, in_=src)
                else:
                    fn = lambda e, dst=dst, src=src: e.activation(out=dst, in_=src, func=AF.Copy)
                P.emit(eng, fn, reads=[pr], writes=[self.Hr[c2][t] for c2 in range(half * 4, half * 4 + 4)], nosame=True)
        P.barrier(("pe", "act", "dve", "sp"))

    def store(self, i):
        P = self.P
        pc = self.pcol
        hn = self.view(self.AR, [DC, TT], F32)
        ys = [self.view(self.AR + 16384 + k * 4096, [D], F32) for k in range(2)]
        hnR = [Res() for _ in range(DC)]
        ysR = self.outR
        for t in range(NT):
            if getattr(self, "dbg", False):
                for c in range(DC):
                    P.emit("dve", lambda e, c=c, t=t: e.tensor_copy(out=hn[:, c, :], in_=self.h[:, c, t * TT:(t + 1) * TT]),
                           reads=[self.Hr[c][t]], writes=[hnR[c]])
            else:
                self.rmsnorm(t, pc["nf"], lambda c: hn[:, c, :], hnR)
            for blk in range(4):
                j = t * 4 + blk
                k = j % 2
                for half in range(2):
                    ps, pr = self.bank()
                    for q in range(4):
                        c = half * 4 + q
                        P.emit("pe", lambda e, ps=ps, q=q, c=c, blk=blk: e.transpose(
                            out=ps[:, q * 128:(q + 1) * 128], in_=hn[:, c, blk * 128:(blk + 1) * 128],
                            identity=self.ident[:]), reads=[hnR[c], self.cR], writes=[pr])
                    dst = ys[k][:, half * 512:(half + 1) * 512]
                    if half == 0:
                        P.emit("act", lambda e, dst=dst, ps=ps: e.activation(out=dst, in_=ps[:], func=AF.Copy),
                               reads=[pr], writes=[ysR[k]], nosame=True)
                    else:
                        P.emit("dve", lambda e, dst=dst, ps=ps: e.tensor_copy(out=dst, in_=ps[:]),
                               reads=[pr], writes=[ysR[k]], nosame=True)
                o = P.emit("sp", lambda e, j=j, k=k: e.dma_start(out=self.out[i, j * 128:(j + 1) * 128, :], in_=ys[k]),
                           reads=[ysR[k]], dma=("yout", 2))
                self.out_ops.append(o)
        P.barrier(("pe", "act", "dve", "sp"))

    def rmsnorm(self, t, gcol, outf, outR):
        P = self.P
        pc = self.pcol
        tr = slice(t * TT, (t + 1) * TT)
        hR = [self.Hr[c][t] for c in range(DC)]
        P.emit("act", lambda e: e.activation(out=self.sq[:], in_=self.h[:, :, tr], func=AF.Square),
               reads=hR, writes=[self.sqR])
        ps, pr = self.bank()
        for c in range(DC):
            P.emit("pe", lambda e, c=c, ps=ps: e.matmul(ps[:], self.ones[:], self.sq[:, c, :], start=(c == 0), stop=(c == DC - 1)),
                   reads=[self.sqR, self.cR], writes=[pr])
        rs, rr = self.tmp()
        eps = self.parm[:, pc["eps"]:pc["eps"] + 1]
        P.emit("act", lambda e, ps=ps, rs=rs: e.activation(out=rs[:], in_=ps[:], func=AF.Sqrt, bias=eps, scale=1.0 / D),
               reads=[pr, self.cR], writes=[rr])
        P.emit("dve", lambda e, rs=rs: e.reciprocal(out=rs[:], in_=rs[:]), reads=[rr], writes=[rr])
        for c in range(DC):
            g = self.parm[:, gcol + c:gcol + c + 1]
            P.emit("dve", lambda e, c=c, g=g, rs=rs: e.scalar_tensor_tensor(
                out=outf(c), in0=self.h[:, c, tr], scalar=g, in1=rs[:], op0=ALU.mult, op1=ALU.mult),
                reads=[hR[c], rr, self.cR], writes=[outR[c]], nosame=True)

    def wslot(self, k):
        return self.wsl[:, k * 12288:(k + 1) * 12288]

    def load_ffn_group(self, l, which, gi, slot):
        P, dr = self.P, self.dr
        f0, f1 = GROUPS[gi]
        nf = f1 - f0
        gwid = nf * 128
        sl = self.wslot(slot)
        R = self.SA if slot == 0 else self.SB
        wg = dr[which + "_wg"][l, :, f0 * 128:f1 * 128].rearrange("(kc p) f -> p kc f", p=128)
        wu = dr[which + "_wu"][l, :, f0 * 128:f1 * 128].rearrange("(kc p) f -> p kc f", p=128)
        wd = dr[which + "_wd"][l, f0 * 128:f1 * 128, :].rearrange("(fc p) d -> p fc d", p=128)
        og = sl[:, 0:8 * gwid].rearrange("p (kc f) -> p kc f", kc=8)
        ou = sl[:, 4096:4096 + 8 * gwid].rearrange("p (kc f) -> p kc f", kc=8)
        od = sl[:, 8192:8192 + nf * 1024].rearrange("p (fc d) -> p fc d", fc=nf)
        P.emit("pool", lambda e: e.dma_start(out=og, in_=wg), writes=[R], dma=("w", 8))
        P.emit("pool", lambda e: e.dma_start(out=ou, in_=wu), writes=[R], dma=("w", 8))
        P.emit("pool", lambda e: e.dma_start(out=od, in_=wd), writes=[R], dma=("w", 8))
        return og, ou, od, R

    def ffn(self, l, which, gcol):
        P = self.P
        act = [self.view(self.AR + k * 4096, [4, TT], BF16) for k in range(2)]
        actR = [[Res() for _ in range(4)] for _ in range(2)]
        stm = [self.view(self.AR + 8192 + k * 2048, [TT], F32) for k in range(2)]
        stR = [Res(), Res()]
        nst = [0]
        pieces = {}
        pieces[0] = self.load_ffn_group(l, which, 0, 0)

        def up(gi, t, mid=None):
            og, ou, od, R = pieces[gi]
            f0, f1 = GROUPS[gi]
            tr = slice(t * TT, (t + 1) * TT)
            for i in range(f1 - f0):
                if i == 2 and mid is not None:
                    mid()
                psg, prg = self.bank()
                psu, pru = self.bank()
                for kc in range(DC):
                    P.emit("pe", lambda e, kc=kc, i=i, psg=psg: e.matmul(
                        psg[:], og[:, kc, i * 128:(i + 1) * 128], self.hnf[:, kc, tr], start=(kc == 0), stop=(kc == DC - 1)),
                        reads=[R, self.hnfR[kc][t]], writes=[prg])
                for kc in range(DC):
                    P.emit("pe", lambda e, kc=kc, i=i, psu=psu: e.matmul(
                        psu[:], ou[:, kc, i * 128:(i + 1) * 128], self.hnf[:, kc, tr], start=(kc == 0), stop=(kc == DC - 1)),
                        reads=[R, self.hnfR[kc][t]], writes=[pru])
                k = nst[0] % 2
                nst[0] += 1
                P.emit("act", lambda e, k=k, psg=psg: e.activation(out=stm[k], in_=psg[:], func=AF.Silu),
                       reads=[prg], writes=[stR[k]])
                P.emit("dve", lambda e, k=k, i=i, psu=psu, t=t: e.tensor_tensor(
                    out=act[t % 2][:, i, :], in0=stm[k], in1=psu[:], op=ALU.mult),
                    reads=[stR[k], pru], writes=[actR[t % 2][i]])

        def down(gi, t):
            og, ou, od, R = pieces[gi]
            f0, f1 = GROUPS[gi]
            nf = f1 - f0
            tr = slice(t * TT, (t + 1) * TT)
            for c in range(DC):
                ps, pr = self.bank()
                for i in range(nf):
                    P.emit("pe", lambda e, i=i, c=c, ps=ps: e.matmul(
                        ps[:], od[:, i, c * 128:(c + 1) * 128], act[t % 2][:, i, :], start=(i == 0), stop=(i == nf - 1)),
                        reads=[R, actR[t % 2][i]], writes=[pr])
                P.emit("dve", lambda e, c=c, ps=ps: e.scalar_tensor_tensor(
                    out=self.h[:, c, tr], in0=ps[:], scalar=0.5, in1=self.h[:, c, tr], op0=ALU.mult, op1=ALU.add),
                    reads=[pr, self.Hr[c][t]], writes=[self.Hr[c][t]], nosame=True)

        for gi in range(len(GROUPS)):
            if gi + 1 < len(GROUPS):
                pieces[gi + 1] = self.load_ffn_group(l, which, gi + 1, (gi + 1) % 2)
            def nrm(t):
                self.rmsnorm(t, gcol, lambda c, t=t: self.hnf[:, c, t * TT:(t + 1) * TT], [self.hnfR[c][t] for c in range(DC)])

            for t in range(NT):
                if gi == 0 and t == 0:
                    nrm(0)
                up(gi, t, mid=(lambda t=t: nrm(t + 1)) if (gi == 0 and t + 1 < NT) else None)
                if t > 0:
                    down(gi, t - 1)
            down(gi, NT - 1)
        P.barrier(("pe", "act", "dve"))

    def mix_even(self, l):
        P, dr = self.P, self.dr
        pc = self.pcol
        e_ = l // 2
        wsl = self.wsl
        SA, SB = self.SA, self.SB
        lastpe = P.last["pe"]
        wqkv = wsl[:, 0:6144].rearrange("p (kc f) -> p kc f", kc=8)
        wab = wsl[:, 6144:14336].rearrange("p (kc f) -> p kc f", kc=8)
        woA = wsl[:, 16384:20480].rearrange("p (j d) -> p j d", j=4)
        woC = wsl[:, 20480:24576].rearrange("p (j d) -> p j d", j=4)
        win = dr["even_w_in"][e_]
        wout = dr["even_w_out"][e_]
        P.emit("pool", lambda e: e.dma_start(out=wqkv, in_=win[:, 0:768].rearrange("(kc p) f -> p kc f", p=128)),
               writes=[SA], dma=("w", 8))
        P.emit("pool", lambda e: e.dma_start(out=wab, in_=win[:, 768:1792].rearrange("(kc p) f -> p kc f", p=128)),
               writes=[SA, SB], dma=("w", 8))
        for g in range(2):
            P.emit("pool", lambda e, g=g: e.dma_start(
                out=woA[g * 64:(g + 1) * 64, :, :], in_=wout[g * 256:(g + 1) * 256, :].rearrange("(j d) n -> d j n", d=64)),
                writes=[SB], dma=("w", 8))
        P.emit("pool", lambda e: e.dma_start(out=woC, in_=wout[512:1024, :].rearrange("(j p) n -> p j n", p=128)),
               writes=[SB], dma=("w", 8))
        o = 0

        def alloc(shape, dt):
            nonlocal o
            v = self.view(o, shape, dt)
            o += int(np.prod(shape)) * (4 if dt == F32 else 2)
            o = (o + 63) // 64 * 64
            return v

        dg = alloc([124, 128], BF16)
        qT = alloc([4, TT], BF16)
        kTb = alloc([5, 128], BF16)
        vb = alloc([5, 128], BF16)
        glub = alloc([4, 30 + TT], BF16)
        PT = [alloc([TT], BF16) for _ in range(4)]
        catA = alloc([4, TT], BF16)
        catC = alloc([4, TT], BF16)
        assert o <= 56 * 1024, o
        mean, var = self.T[2][:], self.T[3][:]
        meanR, varR = self.TR[2], self.TR[3]
        qR = [Res() for _ in range(4)]
        kR, vR, gluR = Res(), Res(), [Res() for _ in range(4)]
        PTR = [Res() for _ in range(4)]
        catAR, catCR = [Res() for _ in range(4)], [Res() for _ in range(4)]
        dgR = [[Res(), Res()] for _ in range(4)]
        npt = [0]
        cw0 = pc["cw"] + e_ * 4 * 31
        for idx in range(124):
            c = idx // 31
            which = idx % 2
            wcol_ = self.parm[:, cw0 + idx:cw0 + idx + 1]
            if which:
                P.emit("act", lambda e, idx=idx, wcol_=wcol_: e.activation(out=dg[:, idx, :], in_=self.identb[:], func=AF.Copy, scale=wcol_),
                       reads=[self.cR], writes=[dgR[c][which]], nosame=True)
            else:
                P.emit("dve", lambda e, idx=idx, wcol_=wcol_: e.tensor_scalar_mul(out=dg[:, idx, :], in0=self.identb[:], scalar1=wcol_),
                       reads=[self.cR], writes=[dgR[c][which]], nosame=True)
        P.emit("dve", lambda e: e.memset(glub[:, :, 0:30], 0.0), writes=gluR)
        sinkrow = self.sinkrow
        for i in range(8):
            P.emit("dve", lambda e, i=i: e.tensor_scalar_mul(out=sinkrow[0:1, e_ * 8 + i, :], in0=self.ones[0:1, :],
                                                             scalar1=self.der[0:1, 32 + e_ * 8 + i:33 + e_ * 8 + i]),
                   reads=[self.cR], writes=[self.gwR], nosame=(i > 0))
        self.bank_pool = [0, 1, 2, 3]
        psY = [self.ps[4 + c] for c in range(4)]
        psYR = [self.psR[4 + c] for c in range(4)]
        self.rmsnorm(0, pc["nm"] + l * 8, lambda c: self.hnm[:, c, :], self.hnmR)
        for t in range(NT):
            tr = slice(t * TT, (t + 1) * TT)
            for cq in range(4):
                ps, pr = self.bank()
                for kc in range(DC):
                    P.emit("pe", lambda e, kc=kc, cq=cq, ps=ps: e.matmul(
                        ps[:], wqkv[:, kc, cq * 128:(cq + 1) * 128], self.hnm[:, kc, :], start=(kc == 0), stop=(kc == DC - 1)),
                        reads=[SA, self.hnmR[kc]], writes=[pr])
                P.emit("act", lambda e, cq=cq, ps=ps: e.activation(out=qT[:, cq, :], in_=ps[:], func=AF.Copy, scale=0.125),
                       reads=[pr], writes=[qR[cq]])
            ps, pr = self.bank()
            for kc in range(DC):
                P.emit("pe", lambda e, kc=kc, ps=ps: e.matmul(
                    ps[:], wqkv[:, kc, 512:640], self.hnm[:, kc, :], start=(kc == 0), stop=(kc == DC - 1)),
                    reads=[SA, self.hnmR[kc]], writes=[pr])
            P.emit("act", lambda e, ps=ps: e.activation(out=kTb[:, 1:5, :], in_=ps[:].rearrange("p (a b) -> p a b", a=4), func=AF.Copy),
                   reads=[pr], writes=[kR])
            ps, pr = self.bank()
            for blk in range(4):
                for kc in range(DC):
                    P.emit("pe", lambda e, kc=kc, blk=blk, ps=ps: e.matmul(
                        ps[:, blk * 128:(blk + 1) * 128], self.hnm[:, kc, blk * 128:(blk + 1) * 128], wqkv[:, kc, 640:768],
                        start=(kc == 0), stop=(kc == DC - 1)),
                        reads=[SA, self.hnmR[kc]], writes=[pr])
            P.emit("act", lambda e, ps=ps: e.activation(out=vb[:, 1:5, :], in_=ps[:].rearrange("p (a b) -> p a b", a=4), func=AF.Copy),
                   reads=[pr], writes=[vR])
            for c in range(4):
                psa, pra = self.bank()
                psb, prb = self.bank()
                for kc in range(DC):
                    P.emit("pe", lambda e, kc=kc, c=c, psa=psa: e.matmul(
                        psa[:], wab[:, kc, c * 128:(c + 1) * 128], self.hnm[:, kc, :], start=(kc == 0), stop=(kc == DC - 1)),
                        reads=[SA, SB, self.hnmR[kc]], writes=[pra])
                for kc in range(DC):
                    P.emit("pe", lambda e, kc=kc, c=c, psb=psb: e.matmul(
                        psb[:], wab[:, kc, 512 + c * 128:512 + (c + 1) * 128], self.hnm[:, kc, :], start=(kc == 0), stop=(kc == DC - 1)),
                        reads=[SA, SB, self.hnmR[kc]], writes=[prb])
                sg, sgR = self.tmp()
                P.emit("act", lambda e, sg=sg, psb=psb: e.activation(out=sg[:], in_=psb[:], func=AF.Sigmoid), reads=[prb], writes=[sgR])
                P.emit("dve", lambda e, sg=sg, psa=psa, c=c: e.tensor_tensor(out=glub[:, c, 30:30 + TT], in0=sg[:], in1=psa[:], op=ALU.mult),
                       reads=[sgR, pra], writes=[gluR[c]])
            if t + 1 < NT:
                self.rmsnorm(t + 1, pc["nm"] + l * 8, lambda c: self.hnm[:, c, :], self.hnmR)
            for n in range(4):
                nbk = t * 4 + n
                halves = ([0] if nbk > 0 else []) + [1]
                psO, prO = self.bank()
                psD, prD = self.bank()
                psSs = [self.bank() for _ in range(2)]
                for gk in range(2):
                    p0, p1 = gk * 64, (gk + 1) * 64
                    units = []
                    for hi, half in enumerate(halves):
                        slot = n + half
                        psS, prS = psSs[hi]
                        P.emit("pe", lambda e, psS=psS, slot=slot, p0=p0, p1=p1, n=n: e.matmul(
                            psS[:].rearrange("p (a b) -> p a b", a=4), kTb[p0:p1, slot, :], qT[p0:p1, :, n * 128:(n + 1) * 128],
                            start=True, stop=True), reads=[kR] + qR, writes=[prS])
                        units.append((hi, half, slot, psS, prS))
                    if True:
                        c = n
                        for k in (range(0, 16) if gk == 0 else range(16, 31)):
                            P.emit("pe", lambda e, c=c, k=k: e.matmul(
                                psY[c][:], dg[:, c * 31 + k, :], glub[:, c, k:k + TT], start=(k == 0), stop=(k == 30)),
                                reads=[dgR[c][0], dgR[c][1], gluR[c]], writes=[psYR[c]])
                    pts = []
                    for hi, half, slot, psS, prS in units:
                        E, ER = self.tmp()
                        P.emit("act", lambda e, E=E, psS=psS: e.activation(out=E[:], in_=psS[:], func=AF.Exp),
                               reads=[prS], writes=[ER])
                        k = npt[0] % 4
                        npt[0] += 1
                        eb = self.EBT[:, half * 1024 + gk * 512: half * 1024 + (gk + 1) * 512]
                        P.emit("dve", lambda e, E=E, k=k, eb=eb: e.tensor_tensor(out=PT[k], in0=E[:], in1=eb, op=ALU.mult),
                               reads=[ER, self.cR], writes=[PTR[k]])
                        pts.append((hi, slot, k))
                    for hi, slot, k in pts:
                        first = hi == 0
                        last = hi == len(halves) - 1
                        P.emit("pe", lambda e, k=k, slot=slot, p0=p0, p1=p1, first=first, last=last, psO=psO: e.matmul(
                            psO[p0:p1, :], vb[:, slot, p0:p1], PT[k], start=first, stop=last),
                            reads=[vR, PTR[k]], writes=[prO])
                        P.emit("pe", lambda e, k=k, p0=p0, p1=p1, first=first, psD=psD: e.matmul(
                            psD[p0:p1, :], self.ones[:, 0:64], PT[k], start=first, stop=False),
                            reads=[self.cR, PTR[k]], writes=[prD])
                    sr = sinkrow[0:1, e_ * 8 + gk * 4: e_ * 8 + gk * 4 + 4, :]
                    P.emit("pe", lambda e, p0=p0, p1=p1, sr=sr, psD=psD: e.matmul(
                        psD[p0:p1, :].rearrange("p (a b) -> p a b", a=4), self.ones[0:1, 0:64], sr, start=False, stop=True),
                        reads=[self.cR, self.gwR], writes=[prD])
                rc, rcR = self.tmp()
                P.emit("dve", lambda e, rc=rc, psD=psD: e.reciprocal(out=rc[:], in_=psD[:]), reads=[prD], writes=[rcR])
                P.emit("dve", lambda e, rc=rc, psO=psO, n=n: e.tensor_tensor(
                    out=catA[:, :, n * 128:(n + 1) * 128], in0=psO[:].rearrange("p (a b) -> p a b", a=4),
                    in1=rc[:].rearrange("p (a b) -> p a b", a=4), op=ALU.mult),
                    reads=[prO, rcR], writes=catAR, nosame=True)
            if t < NT - 1:
                P.emit("act", lambda e: e.activation(out=kTb[:, 0, :], in_=kTb[:, 4, :], func=AF.Copy), reads=[kR], writes=[kR])
                P.emit("act", lambda e: e.activation(out=vb[:, 0, :], in_=vb[:, 4, :], func=AF.Copy), reads=[vR], writes=[vR])
                P.emit("act", lambda e: e.activation(out=glub[:, :, 0:30], in_=glub[:, :, TT:TT + 30], func=AF.Copy),
                       reads=gluR, writes=gluR)
            cbs = [self.parm[:, pc["cb"] + e_ * 4 + c: pc["cb"] + e_ * 4 + c + 1] for c in range(4)]
            for c in range(4):
                P.emit("act", lambda e, c=c: e.activation(out=self.sq[:, c, :], in_=psY[c][:], func=AF.Identity, bias=cbs[c], scale=1.0),
                       reads=[psYR[c], self.cR], writes=[self.sqR], nosame=(c > 0))
            for c in range(4):
                P.emit("act", lambda e, c=c: e.activation(out=self.sq[:, 4 + c, :], in_=psY[c][:], func=AF.Square, bias=cbs[c], scale=1.0),
                       reads=[psYR[c], self.cR], writes=[self.sqR], nosame=True)
            ps1, pr1 = self.bank()
            ps2, pr2 = self.bank()
            for c in range(4):
                P.emit("pe", lambda e, c=c, ps1=ps1: e.matmul(ps1[:], self.ones[:], self.sq[:, c, :], start=(c == 0), stop=(c == 3)),
                       reads=[self.sqR, self.cR], writes=[pr1])
            for c in range(4):
                P.emit("pe", lambda e, c=c, ps2=ps2: e.matmul(ps2[:], self.ones[:], self.sq[:, 4 + c, :], start=(c == 0), stop=(c == 3)),
                       reads=[self.sqR, self.cR], writes=[pr2])
            P.emit("dve", lambda e, ps1=ps1: e.tensor_scalar_mul(out=mean, in0=ps1[:], scalar1=1.0 / 512), reads=[pr1], writes=[meanR])
            P.emit("dve", lambda e: e.tensor_tensor(out=var, in0=mean, in1=mean, op=ALU.mult), reads=[meanR], writes=[varR])
            P.emit("dve", lambda e, ps2=ps2: e.scalar_tensor_tensor(
                out=var, in0=ps2[:], scalar=1.0 / 512, in1=var, op0=ALU.mult, op1=ALU.subtract), reads=[pr2, varR], writes=[varR])
            epsl = self.parm[:, pc["eps"] + 1:pc["eps"] + 2]
            P.emit("act", lambda e: e.activation(out=var, in_=var, func=AF.Sqrt, bias=epsl, scale=1.0), reads=[varR, self.cR], writes=[varR])
            P.emit("dve", lambda e: e.reciprocal(out=var, in_=var), reads=[varR], writes=[varR])
            for c in range(4):
                z, zR = self.tmp()
                P.emit("dve", lambda e, z=z, c=c: e.scalar_tensor_tensor(
                    out=z[:], in0=psY[c][:], scalar=cbs[c], in1=mean, op0=ALU.add, op1=ALU.subtract),
                    reads=[psYR[c], meanR, self.cR], writes=[zR])
                P.emit("dve", lambda e, z=z: e.tensor_tensor(out=z[:], in0=z[:], in1=var, op=ALU.mult), reads=[zR, varR], writes=[zR])
                lg = self.parm[:, pc["lg"] + e_ * 4 + c: pc["lg"] + e_ * 4 + c + 1]
                lb = self.parm[:, pc["lb"] + e_ * 4 + c: pc["lb"] + e_ * 4 + c + 1]
                P.emit("act", lambda e, z=z, c=c, lg=lg, lb=lb: e.activation(out=catC[:, c, :], in_=z[:], func=AF.Silu, bias=lb, scale=lg),
                       reads=[zR, self.cR], writes=[catCR[c]])
            for c in range(DC):
                ps, pr = self.bank()
                for j in range(4):
                    P.emit("pe", lambda e, j=j, c=c, ps=ps: e.matmul(ps[:], woA[:, j, c * 128:(c + 1) * 128], catA[:, j, :], start=(j == 0), stop=False),
                           reads=[SB] + catAR, writes=[pr])
                for j in range(4):
                    P.emit("pe", lambda e, j=j, c=c, ps=ps: e.matmul(ps[:], woC[:, j, c * 128:(c + 1) * 128], catC[:, j, :], start=False, stop=(j == 3)),
                           reads=[SB, catCR[j]], writes=[pr])
                P.emit("dve", lambda e, c=c, ps=ps, tr=tr: e.tensor_tensor(out=self.h[:, c, tr], in0=ps[:], in1=self.h[:, c, tr], op=ALU.add),
                       reads=[pr, self.Hr[c][t]], writes=[self.Hr[c][t]], nosame=True)
        self.bank_pool = list(range(8))
        P.barrier(("pe", "act", "dve"))

    def mix_even_old(self, l):
        P, dr = self.P, self.dr
        pc = self.pcol
        e_ = l // 2
        wsl = self.wsl
        SA, SB = self.SA, self.SB
        wqkv = wsl[:, 0:6144].rearrange("p (kc f) -> p kc f", kc=8)
        wab = wsl[:, 6144:14336].rearrange("p (kc f) -> p kc f", kc=8)
        woA = wsl[:, 16384:20480].rearrange("p (j d) -> p j d", j=4)
        woC = wsl[:, 20480:24576].rearrange("p (j d) -> p j d", j=4)
        win = dr["even_w_in"][e_]
        wout = dr["even_w_out"][e_]
        P.emit("pool", lambda e: e.dma_start(out=wqkv, in_=win[:, 0:768].rearrange("(kc p) f -> p kc f", p=128)),
               writes=[SA], dma=("w", 8))
        P.emit("pool", lambda e: e.dma_start(out=wab, in_=win[:, 768:1792].rearrange("(kc p) f -> p kc f", p=128)),
               writes=[SA, SB], dma=("w", 8))
        for g in range(2):
            P.emit("pool", lambda e, g=g: e.dma_start(
                out=woA[g * 64:(g + 1) * 64, :, :], in_=wout[g * 256:(g + 1) * 256, :].rearrange("(j d) n -> d j n", d=64)),
                writes=[SB], dma=("w", 8))
        P.emit("pool", lambda e: e.dma_start(out=woC, in_=wout[512:1024, :].rearrange("(j p) n -> p j n", p=128)),
               writes=[SB], dma=("w", 8))
        o = 0

        def alloc(shape, dt):
            nonlocal o
            v = self.view(o, shape, dt)
            o += int(np.prod(shape)) * (4 if dt == F32 else 2)
            o = (o + 63) // 64 * 64
            return v

        qT = alloc([4, TT], BF16)
        kTb = alloc([5, 128], BF16)
        vb = alloc([5, 128], BF16)
        glu = alloc([4, 30 + TT], F32)
        y = alloc([4, TT], F32)
        PT = [alloc([TT], BF16) for _ in range(4)]
        catA = alloc([4, TT], BF16)
        catC = alloc([4, TT], BF16)
        mean_b = alloc([TT], F32)
        var_b = alloc([TT], F32)
        meanR, varR = Res(), Res()
        assert o <= 56 * 1024
        qR = [Res() for _ in range(4)]
        kR, vR, gluR, yR = Res(), Res(), [Res() for _ in range(4)], [Res() for _ in range(4)]
        PTR = [Res() for _ in range(4)]
        catAR, catCR = [Res() for _ in range(4)], [Res() for _ in range(4)]
        npt = [0]
        cw0 = pc["cw"] + e_ * 4 * 31
        P.emit("dve", lambda e: e.memset(glu[:, :, 0:30], 0.0), writes=gluR)
        sinkrow = self.sinkrow
        for i in range(8):
            P.emit("dve", lambda e, i=i: e.tensor_scalar_mul(out=sinkrow[0:1, e_ * 8 + i, :], in0=self.ones[0:1, :],
                                                             scalar1=self.der[0:1, 32 + e_ * 8 + i:33 + e_ * 8 + i]),
                   reads=[self.cR], writes=[self.gwR], nosame=(i > 0))
        for t in range(NT):
            tr = slice(t * TT, (t + 1) * TT)
            self.rmsnorm(t, pc["nm"] + l * 8, lambda c: self.hnm[:, c, :], self.hnmR)
            for cq in range(4):
                ps, pr = self.bank()
                for kc in range(DC):
                    P.emit("pe", lambda e, kc=kc, cq=cq, ps=ps: e.matmul(
                        ps[:], wqkv[:, kc, cq * 128:(cq + 1) * 128], self.hnm[:, kc, :], start=(kc == 0), stop=(kc == DC - 1)),
                        reads=[SA, self.hnmR[kc]], writes=[pr])
                P.emit("act", lambda e, cq=cq, ps=ps: e.activation(out=qT[:, cq, :], in_=ps[:], func=AF.Copy, scale=0.125),
                       reads=[pr], writes=[qR[cq]])
            ps, pr = self.bank()
            for kc in range(DC):
                P.emit("pe", lambda e, kc=kc, ps=ps: e.matmul(
                    ps[:], wqkv[:, kc, 512:640], self.hnm[:, kc, :], start=(kc == 0), stop=(kc == DC - 1)),
                    reads=[SA, self.hnmR[kc]], writes=[pr])
            P.emit("act", lambda e, ps=ps: e.activation(out=kTb[:, 1:5, :], in_=ps[:].rearrange("p (a b) -> p a b", a=4), func=AF.Copy),
                   reads=[pr], writes=[kR])
            ps, pr = self.bank()
            for blk in range(4):
                for kc in range(DC):
                    P.emit("pe", lambda e, kc=kc, blk=blk, ps=ps: e.matmul(
                        ps[:, blk * 128:(blk + 1) * 128], self.hnm[:, kc, blk * 128:(blk + 1) * 128], wqkv[:, kc, 640:768],
                        start=(kc == 0), stop=(kc == DC - 1)),
                        reads=[SA, self.hnmR[kc]], writes=[pr])
            P.emit("act", lambda e, ps=ps: e.activation(out=vb[:, 1:5, :], in_=ps[:].rearrange("p (a b) -> p a b", a=4), func=AF.Copy),
                   reads=[pr], writes=[vR])
            for n in range(4):
                nbk = t * 4 + n
                halves = ([0] if nbk > 0 else []) + [1]
                psO, prO = self.bank()
                psD, prD = self.bank()
                for gk in range(2):
                    p0, p1 = gk * 64, (gk + 1) * 64
                    for hi, half in enumerate(halves):
                        slot = n + half
                        psS, prS = self.bank()
                        P.emit("pe", lambda e, psS=psS, slot=slot, p0=p0, p1=p1, n=n: e.matmul(
                            psS[:].rearrange("p (a b) -> p a b", a=4), kTb[p0:p1, slot, :], qT[p0:p1, :, n * 128:(n + 1) * 128],
                            start=True, stop=True), reads=[kR] + qR, writes=[prS])
                        E, ER = self.tmp()
                        P.emit("act", lambda e, E=E, psS=psS: e.activation(out=E[:], in_=psS[:], func=AF.Exp),
                               reads=[prS], writes=[ER])
                        k = npt[0] % 4
                        npt[0] += 1
                        eb = self.EBT[:, half * 1024 + gk * 512: half * 1024 + (gk + 1) * 512]
                        P.emit("dve", lambda e, E=E, k=k, eb=eb: e.tensor_tensor(out=PT[k], in0=E[:], in1=eb, op=ALU.mult),
                               reads=[ER, self.cR], writes=[PTR[k]])
                        first = hi == 0
                        last = hi == len(halves) - 1
                        P.emit("pe", lambda e, k=k, slot=slot, p0=p0, p1=p1, first=first, last=last, psO=psO: e.matmul(
                            psO[p0:p1, :], vb[:, slot, p0:p1], PT[k], start=first, stop=last),
                            reads=[vR, PTR[k]], writes=[prO])
                        P.emit("pe", lambda e, k=k, p0=p0, p1=p1, first=first, psD=psD: e.matmul(
                            psD[p0:p1, :], self.ones[:, 0:64], PT[k], start=first, stop=False),
                            reads=[self.cR, PTR[k]], writes=[prD])
                    sr = sinkrow[0:1, e_ * 8 + gk * 4: e_ * 8 + gk * 4 + 4, :]
                    P.emit("pe", lambda e, p0=p0, p1=p1, sr=sr, psD=psD: e.matmul(
                        psD[p0:p1, :].rearrange("p (a b) -> p a b", a=4), self.ones[0:1, 0:64], sr, start=False, stop=True),
                        reads=[self.cR, self.gwR], writes=[prD])
                rc, rcR = self.tmp()
                P.emit("dve", lambda e, rc=rc, psD=psD: e.reciprocal(out=rc[:], in_=psD[:]), reads=[prD], writes=[rcR])
                P.emit("dve", lambda e, rc=rc, psO=psO, n=n: e.tensor_tensor(
                    out=catA[:, :, n * 128:(n + 1) * 128], in0=psO[:].rearrange("p (a b) -> p a b", a=4),
                    in1=rc[:].rearrange("p (a b) -> p a b", a=4), op=ALU.mult),
                    reads=[prO, rcR], writes=catAR, nosame=True)
            if t < NT - 1:
                P.emit("act", lambda e: e.activation(out=kTb[:, 0, :], in_=kTb[:, 4, :], func=AF.Copy), reads=[kR], writes=[kR])
                P.emit("act", lambda e: e.activation(out=vb[:, 0, :], in_=vb[:, 4, :], func=AF.Copy), reads=[vR], writes=[vR])
            for c in range(4):
                psa, pra = self.bank()
                psb, prb = self.bank()
                for kc in range(DC):
                    P.emit("pe", lambda e, kc=kc, c=c, psa=psa: e.matmul(
                        psa[:], wab[:, kc, c * 128:(c + 1) * 128], self.hnm[:, kc, :], start=(kc == 0), stop=(kc == DC - 1)),
                        reads=[SA, SB, self.hnmR[kc]], writes=[pra])
                for kc in range(DC):
                    P.emit("pe", lambda e, kc=kc, c=c, psb=psb: e.matmul(
                        psb[:], wab[:, kc, 512 + c * 128:512 + (c + 1) * 128], self.hnm[:, kc, :], start=(kc == 0), stop=(kc == DC - 1)),
                        reads=[SA, SB, self.hnmR[kc]], writes=[prb])
                sg, sgR = self.tmp()
                P.emit("act", lambda e, sg=sg, psb=psb: e.activation(out=sg[:], in_=psb[:], func=AF.Sigmoid), reads=[prb], writes=[sgR])
                P.emit("dve", lambda e, sg=sg, psa=psa, c=c: e.tensor_tensor(out=glu[:, c, 30:30 + TT], in0=sg[:], in1=psa[:], op=ALU.mult),
                       reads=[sgR, pra], writes=[gluR[c]])
            for c in range(4):
                wcol = cw0 + c * 31
                cb = self.parm[:, pc["cb"] + e_ * 4 + c: pc["cb"] + e_ * 4 + c + 1]
                P.emit("dve", lambda e, c=c, wcol=wcol, cb=cb: e.tensor_scalar(
                    out=y[:, c, :], in0=glu[:, c, 30:30 + TT], scalar1=self.parm[:, wcol + 30:wcol + 31], scalar2=cb,
                    op0=ALU.mult, op1=ALU.add), reads=[gluR[c], self.cR], writes=[yR[c]])
                for k in range(30):
                    P.emit("dve", lambda e, c=c, k=k, wcol=wcol: e.scalar_tensor_tensor(
                        out=y[:, c, :], in0=glu[:, c, k:k + TT], scalar=self.parm[:, wcol + k:wcol + k + 1], in1=y[:, c, :],
                        op0=ALU.mult, op1=ALU.add), reads=[gluR[c], yR[c]], writes=[yR[c]])
            if t < NT - 1:
                P.emit("act", lambda e: e.activation(out=glu[:, :, 0:30], in_=glu[:, :, TT:TT + 30], func=AF.Copy),
                       reads=gluR, writes=gluR)
            P.emit("act", lambda e: e.activation(out=self.sq[:, 0:4, :], in_=y[:], func=AF.Copy), reads=yR, writes=[self.sqR])
            P.emit("act", lambda e: e.activation(out=self.sq[:, 4:8, :], in_=y[:], func=AF.Square), reads=yR, writes=[self.sqR])
            ps1, pr1 = self.bank()
            ps2, pr2 = self.bank()
            for c in range(4):
                P.emit("pe", lambda e, c=c, ps1=ps1: e.matmul(ps1[:], self.ones[:], self.sq[:, c, :], start=(c == 0), stop=(c == 3)),
                       reads=[self.sqR, self.cR], writes=[pr1])
            for c in range(4):
                P.emit("pe", lambda e, c=c, ps2=ps2: e.matmul(ps2[:], self.ones[:], self.sq[:, 4 + c, :], start=(c == 0), stop=(c == 3)),
                       reads=[self.sqR, self.cR], writes=[pr2])
            mean, var = mean_b, var_b
            P.emit("dve", lambda e, ps1=ps1: e.tensor_scalar_mul(out=mean, in0=ps1[:], scalar1=1.0 / 512), reads=[pr1], writes=[meanR])
            P.emit("dve", lambda e: e.tensor_tensor(out=var, in0=mean, in1=mean, op=ALU.mult), reads=[meanR], writes=[varR])
            P.emit("dve", lambda e, ps2=ps2: e.scalar_tensor_tensor(
                out=var, in0=ps2[:], scalar=1.0 / 512, in1=var, op0=ALU.mult, op1=ALU.subtract), reads=[pr2, varR], writes=[varR])
            epsl = self.parm[:, pc["eps"] + 1:pc["eps"] + 2]
            P.emit("act", lambda e: e.activation(out=var, in_=var, func=AF.Sqrt, bias=epsl, scale=1.0), reads=[varR, self.cR], writes=[varR])
            P.emit("dve", lambda e: e.reciprocal(out=var, in_=var), reads=[varR], writes=[varR])
            for c in range(4):
                z, zR = self.tmp()
                P.emit("dve", lambda e, z=z, c=c: e.tensor_tensor(out=z[:], in0=y[:, c, :], in1=mean, op=ALU.subtract),
                       reads=[yR[c], meanR], writes=[zR])
                P.emit("dve", lambda e, z=z: e.tensor_tensor(out=z[:], in0=z[:], in1=var, op=ALU.mult), reads=[zR, varR], writes=[zR])
                lg = self.parm[:, pc["lg"] + e_ * 4 + c: pc["lg"] + e_ * 4 + c + 1]
                lb = self.parm[:, pc["lb"] + e_ * 4 + c: pc["lb"] + e_ * 4 + c + 1]
                P.emit("act", lambda e, z=z, c=c, lg=lg, lb=lb: e.activation(out=catC[:, c, :], in_=z[:], func=AF.Silu, bias=lb, scale=lg),
                       reads=[zR, self.cR], writes=[catCR[c]])
            if getattr(self, "dbg", 0):
                srcs = {1: [catA[:, c, :] for c in range(4)] + [catC[:, c, :] for c in range(4)],
                        2: [glu[:, c, 30:30 + TT] for c in range(4)] + [y[:, c, :] for c in range(4)],
                        3: [qT[:, c, :] for c in range(4)] + [kTb[:, 1:5, :], self.hnm[:, 0, :], self.hnm[:, 1, :], self.hnm[:, 7, :]],
                        4: [mean_b, var_b, self.sq[:, 0, :], self.sq[:, 4, :]] + [catC[:, c, :] for c in range(4)],
                        5: [self.EBT[:, 0:512], self.EBT[:, 1024:1536], PT[0], PT[1], PT[2], PT[3], catA[:, 0, :], catA[:, 1, :]]}[self.dbg]
                allR = catAR + catCR + gluR + yR + qR + [kR] + self.hnmR + [meanR, varR, self.sqR, self.cR] + PTR
                for c in range(8):
                    dst = self.h[:, c, tr]
                    if self.dbg == 3 and c == 4:
                        dst = dst.rearrange("p (a b) -> p a b", a=4)
                    P.emit("dve", lambda e, c=c, dst=dst: e.tensor_copy(out=dst, in_=srcs[c]), reads=allR + [self.Hr[c][t]], writes=[self.Hr[c][t]])
                continue
            for c in range(DC):
                ps, pr = self.bank()
                for j in range(4):
                    P.emit("pe", lambda e, j=j, c=c, ps=ps: e.matmul(ps[:], woA[:, j, c * 128:(c + 1) * 128], catA[:, j, :], start=(j == 0), stop=False),
                           reads=[SB] + catAR, writes=[pr])
                for j in range(4):
                    P.emit("pe", lambda e, j=j, c=c, ps=ps: e.matmul(ps[:], woC[:, j, c * 128:(c + 1) * 128], catC[:, j, :], start=False, stop=(j == 3)),
                           reads=[SB, catCR[j]], writes=[pr])
                P.emit("dve", lambda e, c=c, ps=ps, tr=tr: e.tensor_tensor(out=self.h[:, c, tr], in0=ps[:], in1=self.h[:, c, tr], op=ALU.add),
                       reads=[pr, self.Hr[c][t]], writes=[self.Hr[c][t]], nosame=True)
        P.barrier(("pe", "act", "dve"))

    def mix_odd(self, l):
        P, dr = self.P, self.dr
        pc = self.pcol
        o_ = l // 2
        wsl = self.wsl
        SA, SB = self.SA, self.SB
        wig = wsl[:, 0:8192].rearrange("p (kc f) -> p kc f", kc=8)
        wir = wsl[:, 8192:16384].rearrange("p (kc f) -> p kc f", kc=8)
        wo = wsl[:, 16384:24576].rearrange("p (j d) -> p j d", j=8)
        win = dr["odd_w_in"][o_]
        wout = dr["odd_w_out"][o_]
        P.emit("pool", lambda e: e.dma_start(out=wig, in_=win[:, 0:1024].rearrange("(kc p) f -> p kc f", p=128)), writes=[SA], dma=("w", 8))
        P.emit("pool", lambda e: e.dma_start(out=wir, in_=win[:, 1024:2048].rearrange("(kc p) f -> p kc f", p=128)), writes=[SA, SB], dma=("w", 8))
        P.emit("pool", lambda e: e.dma_start(out=wo, in_=wout.rearrange("(j p) n -> p j n", p=128)), writes=[SB], dma=("w", 8))
        gwR = self.gwR
        P.emit("pool", lambda e: e.dma_start(out=self.gw[:, 0, :, :], in_=dr["gate_a_w"][o_].rearrange("h i j -> i h j")), writes=[gwR], dma=("gw", 2))
        P.emit("pool", lambda e: e.dma_start(out=self.gw[:, 1, :, :], in_=dr["gate_x_w"][o_].rearrange("h i j -> i h j")), writes=[gwR], dma=("gw", 2))
        o = 0

        def alloc(shape, dt):
            nonlocal o
            v = self.view(o, shape, dt)
            o += int(np.prod(shape)) * (4 if dt == F32 else 2)
            o = (o + 63) // 64 * 64
            return v

        NU = 2
        UPT = DC // NU
        XW = TT + 4
        gate = [alloc([DC, TT], BF16) for _ in range(2)]
        xr = [alloc([NU, XW], F32) for _ in range(2)]
        xc = [alloc([NU, TT], F32) for _ in range(2)]
        xcb = [alloc([NU, TT], BF16) for _ in range(2)]
        bA = alloc([NU, TT], F32)
        bB = alloc([NU, TT], F32)
        bC = alloc([NU, TT], F32)
        halo = alloc([DC, 4], F32)
        carry = alloc([DC], F32)
        assert o <= 56 * 1024, o
        gateR = [[Res() for _ in range(DC)] for _ in range(2)]
        xrR = [[Res() for _ in range(NU)] for _ in range(2)]
        xcR = [[Res() for _ in range(NU)] for _ in range(2)]
        xcbR = [[Res() for _ in range(NU)] for _ in range(2)]
        AR_, BR, CR = ([Res() for _ in range(NU)] for _ in range(3))
        haloR, carryR = [Res() for _ in range(UPT)], [Res() for _ in range(UPT)]
        P.emit("dve", lambda e: e.memset(halo[:], 0.0), writes=haloR)
        P.emit("dve", lambda e: e.memset(carry[:], 0.0), writes=carryR)
        der = self.der
        q25 = self.parm[:, pc["eps"] + 3:pc["eps"] + 4]

        def norm(t):
            self.rmsnorm(t, pc["nm"] + l * 8, lambda c: self.hnm[:, c, :], self.hnmR)

        def gatebr(t):
            g = gate[t % 2]
            for c in range(DC):
                ps, pr = self.bank()
                for kc in range(DC):
                    P.emit("pe", lambda e, kc=kc, c=c, ps=ps: e.matmul(
                        ps[:], wig[:, kc, c * 128:(c + 1) * 128], self.hnm[:, kc, :], start=(kc == 0), stop=(kc == DC - 1)),
                        reads=[SA, self.hnmR[kc]], writes=[pr])
                P.emit("act", lambda e, c=c, ps=ps, g=g: e.activation(out=g[:, c, :], in_=ps[:], func=AF.Gelu_apprx_tanh),
                       reads=[pr], writes=[gateR[t % 2][c]], nosame=True)

        def stA(t, u):
            gi = t * UPT + u
            k = gi % 2
            cs = [NU * u + j for j in range(NU)]
            for j, c in enumerate(cs):
                ps, pr = self.bank()
                for kc in range(DC):
                    P.emit("pe", lambda e, kc=kc, c=c, ps=ps: e.matmul(
                        ps[:], wir[:, kc, c * 128:(c + 1) * 128], self.hnm[:, kc, :], start=(kc == 0), stop=(kc == DC - 1)),
                        reads=[SA, SB, self.hnmR[kc]], writes=[pr])
                P.emit("act", lambda e, j=j, ps=ps, k=k: e.activation(out=xr[k][:, j, 3:3 + TT], in_=ps[:], func=AF.Copy),
                       reads=[pr], writes=[xrR[k][j]], nosame=True)
            P.emit("dve", lambda e, u=u, k=k: e.tensor_copy(out=xr[k][:, :, 0:3], in_=halo[:, NU * u:NU * u + NU, 0:3]),
                   reads=[haloR[u]], writes=xrR[k])
            P.emit("act", lambda e, u=u, k=k: e.activation(out=halo[:, NU * u:NU * u + NU, 0:3], in_=xr[k][:, :, TT:TT + 3], func=AF.Copy),
                   reads=xrR[k], writes=[haloR[u]])
            for kk in (3, 0, 1, 2):
                for j, c in enumerate(cs):
                    wc = pc["lw"] + (o_ * 8 + c) * 4
                    if kk == 3:
                        cb = self.parm[:, pc["lcb"] + o_ * 8 + c: pc["lcb"] + o_ * 8 + c + 1]
                        P.emit("dve", lambda e, j=j, wc=wc, cb=cb, k=k: e.tensor_scalar(
                            out=xc[k][:, j, :], in0=xr[k][:, j, 3:3 + TT], scalar1=self.parm[:, wc + 3:wc + 4], scalar2=cb, op0=ALU.mult, op1=ALU.add),
                            reads=[xrR[k][j], self.cR], writes=[xcR[k][j]])
                    else:
                        P.emit("dve", lambda e, j=j, kk=kk, wc=wc, k=k: e.scalar_tensor_tensor(
                            out=xc[k][:, j, :], in0=xr[k][:, j, kk:kk + TT], scalar=self.parm[:, wc + kk:wc + kk + 1], in1=xc[k][:, j, :],
                            op0=ALU.mult, op1=ALU.add),
                            reads=[xrR[k][j], xcR[k][j]], writes=[xcR[k][j]])

        def stB(t, u):
            gi = t * UPT + u
            k = gi % 2
            cs = [NU * u + j for j in range(NU)]
            P.emit("act", lambda e, k=k: e.activation(out=xcb[k][:], in_=xc[k][:], func=AF.Copy), reads=xcR[k], writes=xcbR[k])
            for j, c in enumerate(cs):
                psr, prr = self.bank()
                psi, pri = self.bank()
                P.emit("pe", lambda e, j=j, c=c, psr=psr, k=k: e.matmul(psr[:], self.gw[:, 0, c, :], xcb[k][:, j, :], start=True, stop=True),
                       reads=[gwR, xcbR[k][j]], writes=[prr])
                P.emit("pe", lambda e, j=j, c=c, psi=psi, k=k: e.matmul(psi[:], self.gw[:, 1, c, :], xcb[k][:, j, :], start=True, stop=True),
                       reads=[gwR, xcbR[k][j]], writes=[pri])
                hba = der[:, 48 + o_ * 8 + c: 48 + o_ * 8 + c + 1]
                hbx = der[:, 64 + o_ * 8 + c: 64 + o_ * 8 + c + 1]
                P.emit("act", lambda e, j=j, psr=psr, hba=hba: e.activation(out=bA[:, j, :], in_=psr[:], func=AF.Tanh, bias=hba, scale=0.5),
                       reads=[prr, self.cR], writes=[AR_[j]])
                P.emit("act", lambda e, j=j, psi=psi, hbx=hbx: e.activation(out=bC[:, j, :], in_=psi[:], func=AF.Tanh, bias=hbx, scale=0.5),
                       reads=[pri, self.cR], writes=[CR[j]])
            for j, c in enumerate(cs):
                clh = der[:, 16 + o_ * 8 + c: 16 + o_ * 8 + c + 1]
                P.emit("act", lambda e, j=j, clh=clh: e.activation(out=bA[:, j, :], in_=bA[:, j, :], func=AF.Exp, bias=clh, scale=clh),
                       reads=[AR_[j], self.cR], writes=[AR_[j]])
            P.emit("act", lambda e: e.activation(out=bB[:], in_=bA[:], func=AF.Square), reads=AR_, writes=BR)
            P.emit("act", lambda e: e.activation(out=bB[:], in_=bB[:], func=AF.Sqrt, bias=q25, scale=-0.25), reads=BR + [self.cR], writes=BR)

        def stC(t, u):
            gi = t * UPT + u
            k = gi % 2
            cs = [NU * u + j for j in range(NU)]
            g = gate[t % 2]
            P.emit("dve", lambda e, k=k: e.scalar_tensor_tensor(out=bC[:], in0=bC[:], scalar=1.0, in1=xc[k][:], op0=ALU.add, op1=ALU.mult),
                   reads=CR + xcR[k], writes=CR)
            P.emit("dve", lambda e: e.tensor_tensor(out=bC[:], in0=bC[:], in1=bB[:], op=ALU.mult), reads=CR + BR, writes=CR)
            for j, c in enumerate(cs):
                P.emit("dve", lambda e, j=j, c=c, k=k: e.tensor_tensor_scan(
                    out=xc[k][:, j, :], data0=bA[:, j, :], data1=bC[:, j, :], initial=carry[:, c:c + 1], op0=ALU.mult, op1=ALU.add),
                    reads=[AR_[j], CR[j], carryR[u], xcR[k][j]], writes=[xcR[k][j]], nosame=(j > 0))
            P.emit("act", lambda e, u=u, k=k: e.activation(out=carry[:, NU * u:NU * u + NU], in_=xc[k][:, :, TT - 1], func=AF.Copy),
                   reads=xcR[k], writes=[carryR[u]])
            P.emit("dve", lambda e, u=u, k=k, g=g: e.tensor_tensor(out=g[:, NU * u:NU * u + NU, :], in0=g[:, NU * u:NU * u + NU, :], in1=xc[k][:], op=ALU.mult),
                   reads=xcR[k] + [gateR[t % 2][c] for c in cs], writes=[gateR[t % 2][c] for c in cs])

        def outp(t):
            g = gate[t % 2]
            tr = slice(t * TT, (t + 1) * TT)
            for c in range(DC):
                ps, pr = self.bank()
                for j in range(DC):
                    P.emit("pe", lambda e, j=j, c=c, ps=ps, g=g: e.matmul(ps[:], wo[:, j, c * 128:(c + 1) * 128], g[:, j, :], start=(j == 0), stop=(j == DC - 1)),
                           reads=[SB, gateR[t % 2][j]], writes=[pr])
                P.emit("dve", lambda e, c=c, ps=ps, tr=tr: e.tensor_tensor(out=self.h[:, c, tr], in0=ps[:], in1=self.h[:, c, tr], op=ALU.add),
                       reads=[pr, self.Hr[c][t]], writes=[self.Hr[c][t]], nosame=True)

        norm(0)
        gatebr(0)
        stA(0, 0)
        for t in range(NT):
            for u in range(UPT):
                if u + 1 < UPT:
                    stA(t, u + 1)
                elif t + 1 < NT:
                    norm(t + 1)
                    gatebr(t + 1)
                    stA(t + 1, 0)
                stB(t, u)
                stC(t, u)
                if u == 0 and t > 0:
                    outp(t - 1)
        outp(NT - 1)
        P.barrier(("pe", "act", "dve"))

    def mix_odd_v2(self, l):
        P, dr = self.P, self.dr
        pc = self.pcol
        o_ = l // 2
        wsl = self.wsl
        SA, SB = self.SA, self.SB
        wig = wsl[:, 0:8192].rearrange("p (kc f) -> p kc f", kc=8)
        wir = wsl[:, 8192:16384].rearrange("p (kc f) -> p kc f", kc=8)
        wo = wsl[:, 16384:24576].rearrange("p (j d) -> p j d", j=8)
        win = dr["odd_w_in"][o_]
        wout = dr["odd_w_out"][o_]
        P.emit("pool", lambda e: e.dma_start(out=wig, in_=win[:, 0:1024].rearrange("(kc p) f -> p kc f", p=128)), writes=[SA], dma=("w", 8))
        P.emit("pool", lambda e: e.dma_start(out=wir, in_=win[:, 1024:2048].rearrange("(kc p) f -> p kc f", p=128)), writes=[SA, SB], dma=("w", 8))
        P.emit("pool", lambda e: e.dma_start(out=wo, in_=wout.rearrange("(j p) n -> p j n", p=128)), writes=[SB], dma=("w", 8))
        gwR = self.gwR
        P.emit("pool", lambda e: e.dma_start(out=self.gw[:, 0, :, :], in_=dr["gate_a_w"][o_].rearrange("h i j -> i h j")), writes=[gwR], dma=("gw", 2))
        P.emit("pool", lambda e: e.dma_start(out=self.gw[:, 1, :, :], in_=dr["gate_x_w"][o_].rearrange("h i j -> i h j")), writes=[gwR], dma=("gw", 2))
        o = 0

        def alloc(shape, dt):
            nonlocal o
            v = self.view(o, shape, dt)
            o += int(np.prod(shape)) * (4 if dt == F32 else 2)
            o = (o + 63) // 64 * 64
            return v

        XW = TT + 4
        gate = alloc([DC, TT], BF16)
        xr = alloc([4, XW], F32)
        xc = alloc([4, TT], F32)
        xcb = alloc([4, TT], BF16)
        bA = alloc([4, TT], F32)
        bB = alloc([4, TT], F32)
        bC = alloc([4, TT], F32)
        halo = alloc([DC, 4], F32)
        carry = alloc([DC], F32)
        assert o <= 56 * 1024, o
        gateR = [Res() for _ in range(DC)]
        xrR, xcR, xcbR, AR_, BR, CR = ([Res() for _ in range(4)] for _ in range(6))
        haloR, carryR = [Res(), Res()], [Res(), Res()]
        P.emit("dve", lambda e: e.memset(halo[:], 0.0), writes=haloR)
        P.emit("dve", lambda e: e.memset(carry[:], 0.0), writes=carryR)
        der = self.der
        q25 = self.parm[:, pc["eps"] + 3:pc["eps"] + 4]
        self.rmsnorm(0, pc["nm"] + l * 8, lambda c: self.hnm[:, c, :], self.hnmR)
        for t in range(NT):
            tr = slice(t * TT, (t + 1) * TT)
            for c in range(DC):
                ps, pr = self.bank()
                for kc in range(DC):
                    P.emit("pe", lambda e, kc=kc, c=c, ps=ps: e.matmul(
                        ps[:], wig[:, kc, c * 128:(c + 1) * 128], self.hnm[:, kc, :], start=(kc == 0), stop=(kc == DC - 1)),
                        reads=[SA, self.hnmR[kc]], writes=[pr])
                P.emit("act", lambda e, c=c, ps=ps: e.activation(out=gate[:, c, :], in_=ps[:], func=AF.Gelu_apprx_tanh),
                       reads=[pr], writes=[gateR[c]], nosame=True)
            for b in range(2):
                cs = [4 * b + j for j in range(4)]
                for j, c in enumerate(cs):
                    ps, pr = self.bank()
                    for kc in range(DC):
                        P.emit("pe", lambda e, kc=kc, c=c, ps=ps: e.matmul(
                            ps[:], wir[:, kc, c * 128:(c + 1) * 128], self.hnm[:, kc, :], start=(kc == 0), stop=(kc == DC - 1)),
                            reads=[SA, SB, self.hnmR[kc]], writes=[pr])
                    P.emit("act", lambda e, j=j, ps=ps: e.activation(out=xr[:, j, 3:3 + TT], in_=ps[:], func=AF.Copy),
                           reads=[pr], writes=[xrR[j]])
                if b == 1 and t + 1 < NT:
                    self.rmsnorm(t + 1, pc["nm"] + l * 8, lambda c: self.hnm[:, c, :], self.hnmR)
                P.emit("dve", lambda e, b=b: e.tensor_copy(out=xr[:, :, 0:3], in_=halo[:, 4 * b:4 * b + 4, 0:3]), reads=[haloR[b]], writes=xrR)
                P.emit("act", lambda e, b=b: e.activation(out=halo[:, 4 * b:4 * b + 4, 0:3], in_=xr[:, :, TT:TT + 3], func=AF.Copy),
                       reads=xrR, writes=[haloR[b]])
                for k in (3, 0, 1, 2):
                    for j, c in enumerate(cs):
                        wc = pc["lw"] + (o_ * 8 + c) * 4
                        if k == 3:
                            cb = self.parm[:, pc["lcb"] + o_ * 8 + c: pc["lcb"] + o_ * 8 + c + 1]
                            P.emit("dve", lambda e, j=j, wc=wc, cb=cb: e.tensor_scalar(
                                out=xc[:, j, :], in0=xr[:, j, 3:3 + TT], scalar1=self.parm[:, wc + 3:wc + 4], scalar2=cb, op0=ALU.mult, op1=ALU.add),
                                reads=[xrR[j], self.cR], writes=[xcR[j]])
                        else:
                            P.emit("dve", lambda e, j=j, k=k, wc=wc: e.scalar_tensor_tensor(
                                out=xc[:, j, :], in0=xr[:, j, k:k + TT], scalar=self.parm[:, wc + k:wc + k + 1], in1=xc[:, j, :], op0=ALU.mult, op1=ALU.add),
                                reads=[xrR[j], xcR[j]], writes=[xcR[j]])
                P.emit("act", lambda e: e.activation(out=xcb[:], in_=xc[:], func=AF.Copy), reads=xcR, writes=xcbR)
                for j, c in enumerate(cs):
                    psr, prr = self.bank()
                    psi, pri = self.bank()
                    P.emit("pe", lambda e, j=j, c=c, psr=psr: e.matmul(psr[:], self.gw[:, 0, c, :], xcb[:, j, :], start=True, stop=True), reads=[gwR, xcbR[j]], writes=[prr])
                    P.emit("pe", lambda e, j=j, c=c, psi=psi: e.matmul(psi[:], self.gw[:, 1, c, :], xcb[:, j, :], start=True, stop=True), reads=[gwR, xcbR[j]], writes=[pri])
                    hba = der[:, 48 + o_ * 8 + c: 48 + o_ * 8 + c + 1]
                    hbx = der[:, 64 + o_ * 8 + c: 64 + o_ * 8 + c + 1]
                    P.emit("act", lambda e, j=j, psr=psr, hba=hba: e.activation(out=bA[:, j, :], in_=psr[:], func=AF.Tanh, bias=hba, scale=0.5),
                           reads=[prr, self.cR], writes=[AR_[j]])
                    P.emit("act", lambda e, j=j, psi=psi, hbx=hbx: e.activation(out=bC[:, j, :], in_=psi[:], func=AF.Tanh, bias=hbx, scale=0.5),
                           reads=[pri, self.cR], writes=[CR[j]])
                for j, c in enumerate(cs):
                    clh = der[:, 16 + o_ * 8 + c: 16 + o_ * 8 + c + 1]
                    P.emit("act", lambda e, j=j, clh=clh: e.activation(out=bA[:, j, :], in_=bA[:, j, :], func=AF.Exp, bias=clh, scale=clh),
                           reads=[AR_[j], self.cR], writes=[AR_[j]])
                P.emit("act", lambda e: e.activation(out=bB[:], in_=bA[:], func=AF.Square), reads=AR_, writes=BR)
                P.emit("act", lambda e: e.activation(out=bB[:], in_=bB[:], func=AF.Sqrt, bias=q25, scale=-0.25), reads=BR + [self.cR], writes=BR)
                P.emit("dve", lambda e: e.scalar_tensor_tensor(out=bC[:], in0=bC[:], scalar=1.0, in1=xc[:], op0=ALU.add, op1=ALU.mult),
                       reads=CR + xcR, writes=CR)
                P.emit("dve", lambda e: e.tensor_tensor(out=bC[:], in0=bC[:], in1=bB[:], op=ALU.mult), reads=CR + BR, writes=CR)
                for j, c in enumerate(cs):
                    P.emit("dve", lambda e, j=j, c=c: e.tensor_tensor_scan(
                        out=xc[:, j, :], data0=bA[:, j, :], data1=bC[:, j, :], initial=carry[:, c:c + 1], op0=ALU.mult, op1=ALU.add),
                        reads=[AR_[j], CR[j], carryR[b], xcR[j]], writes=[xcR[j]], nosame=(j > 0))
                P.emit("act", lambda e, b=b: e.activation(out=carry[:, 4 * b:4 * b + 4], in_=xc[:, :, TT - 1], func=AF.Copy),
                       reads=xcR, writes=[carryR[b]])
                P.emit("dve", lambda e, b=b: e.tensor_tensor(out=gate[:, 4 * b:4 * b + 4, :], in0=gate[:, 4 * b:4 * b + 4, :], in1=xc[:], op=ALU.mult),
                       reads=xcR + [gateR[c] for c in cs], writes=[gateR[c] for c in cs])
            for c in range(DC):
                ps, pr = self.bank()
                for j in range(DC):
                    P.emit("pe", lambda e, j=j, c=c, ps=ps: e.matmul(ps[:], wo[:, j, c * 128:(c + 1) * 128], gate[:, j, :], start=(j == 0), stop=(j == DC - 1)),
                           reads=[SB, gateR[j]], writes=[pr])
                P.emit("dve", lambda e, c=c, ps=ps, tr=tr: e.tensor_tensor(out=self.h[:, c, tr], in0=ps[:], in1=self.h[:, c, tr], op=ALU.add),
                       reads=[pr, self.Hr[c][t]], writes=[self.Hr[c][t]], nosame=True)
        P.barrier(("pe", "act", "dve"))

    def mix_odd_old(self, l):
        P, dr = self.P, self.dr
        pc = self.pcol
        o_ = l // 2
        wsl = self.wsl
        SA, SB = self.SA, self.SB
        wig = wsl[:, 0:8192].rearrange("p (kc f) -> p kc f", kc=8)
        wir = wsl[:, 8192:16384].rearrange("p (kc f) -> p kc f", kc=8)
        wo = wsl[:, 16384:24576].rearrange("p (j d) -> p j d", j=8)
        win = dr["odd_w_in"][o_]
        wout = dr["odd_w_out"][o_]
        P.emit("pool", lambda e: e.dma_start(out=wig, in_=win[:, 0:1024].rearrange("(kc p) f -> p kc f", p=128)), writes=[SA], dma=("w", 8))
        P.emit("pool", lambda e: e.dma_start(out=wir, in_=win[:, 1024:2048].rearrange("(kc p) f -> p kc f", p=128)), writes=[SA, SB], dma=("w", 8))
        P.emit("pool", lambda e: e.dma_start(out=wo, in_=wout.rearrange("(j p) n -> p j n", p=128)), writes=[SB], dma=("w", 8))
        gwR = self.gwR
        P.emit("pool", lambda e: e.dma_start(out=self.gw[:, 0, :, :], in_=dr["gate_a_w"][o_].rearrange("h i j -> i h j")), writes=[gwR], dma=("gw", 2))
        P.emit("pool", lambda e: e.dma_start(out=self.gw[:, 1, :, :], in_=dr["gate_x_w"][o_].rearrange("h i j -> i h j")), writes=[gwR], dma=("gw", 2))
        o = 0

        def alloc(shape, dt):
            nonlocal o
            v = self.view(o, shape, dt)
            o += int(np.prod(shape)) * (4 if dt == F32 else 2)
            o = (o + 63) // 64 * 64
            return v

        gate = alloc([DC, TT], BF16)
        mo = alloc([DC, TT], BF16)
        xr = [alloc([TT + 4], F32) for _ in range(2)]
        xcb = [alloc([TT], BF16) for _ in range(2)]
        halo = alloc([DC, 4], F32)
        carry = alloc([DC], F32)
        NB = 2
        xc = [alloc([TT], F32) for _ in range(NB)]
        bA = [alloc([TT], F32) for _ in range(NB)]
        bB = [alloc([TT], F32) for _ in range(NB)]
        bC = [alloc([TT], F32) for _ in range(NB)]
        hs = [alloc([TT], F32) for _ in range(NB)]
        assert o <= 56 * 1024
        gateR, moR = [Res() for _ in range(DC)], [Res() for _ in range(DC)]
        xrR, xcbR = [Res(), Res()], [Res(), Res()]
        haloR, carryR = [Res() for _ in range(DC)], [Res() for _ in range(DC)]
        xcR, AR_, BR, CR, hsR = ([Res() for _ in range(NB)] for _ in range(5))
        P.emit("dve", lambda e: e.memset(halo[:], 0.0), writes=haloR)
        P.emit("dve", lambda e: e.memset(carry[:], 0.0), writes=carryR)
        der = self.der
        for t in range(NT):
            tr = slice(t * TT, (t + 1) * TT)
            self.rmsnorm(t, pc["nm"] + l * 8, lambda c: self.hnm[:, c, :], self.hnmR)
            for c in range(DC):
                ps, pr = self.bank()
                for kc in range(DC):
                    P.emit("pe", lambda e, kc=kc, c=c, ps=ps: e.matmul(
                        ps[:], wig[:, kc, c * 128:(c + 1) * 128], self.hnm[:, kc, :], start=(kc == 0), stop=(kc == DC - 1)),
                        reads=[SA, self.hnmR[kc]], writes=[pr])
                P.emit("act", lambda e, c=c, ps=ps: e.activation(out=gate[:, c, :], in_=ps[:], func=AF.Gelu_apprx_tanh),
                       reads=[pr], writes=[gateR[c]])
            for c in range(DC):
                k = c % 2
                ps, pr = self.bank()
                for kc in range(DC):
                    P.emit("pe", lambda e, kc=kc, c=c, ps=ps: e.matmul(
                        ps[:], wir[:, kc, c * 128:(c + 1) * 128], self.hnm[:, kc, :], start=(kc == 0), stop=(kc == DC - 1)),
                        reads=[SA, SB, self.hnmR[kc]], writes=[pr])
                P.emit("act", lambda e, k=k, ps=ps: e.activation(out=xr[k][:, 3:3 + TT], in_=ps[:], func=AF.Copy), reads=[pr], writes=[xrR[k]])
                P.emit("dve", lambda e, k=k, c=c: e.tensor_copy(out=xr[k][:, 0:3], in_=halo[:, c, 0:3]), reads=[haloR[c]], writes=[xrR[k]])
                P.emit("act", lambda e, k=k, c=c: e.activation(out=halo[:, c, 0:3], in_=xr[k][:, TT:TT + 3], func=AF.Copy), reads=[xrR[k]], writes=[haloR[c]])
                wc = pc["lw"] + (o_ * 8 + c) * 4
                cb = self.parm[:, pc["lcb"] + o_ * 8 + c: pc["lcb"] + o_ * 8 + c + 1]
                P.emit("dve", lambda e, k=k, wc=wc, cb=cb: e.tensor_scalar(
                    out=xc[k], in0=xr[k][:, 3:3 + TT], scalar1=self.parm[:, wc + 3:wc + 4], scalar2=cb, op0=ALU.mult, op1=ALU.add),
                    reads=[xrR[k], self.cR], writes=[xcR[k]])
                for j in range(3):
                    P.emit("dve", lambda e, k=k, j=j, wc=wc: e.scalar_tensor_tensor(
                        out=xc[k], in0=xr[k][:, j:j + TT], scalar=self.parm[:, wc + j:wc + j + 1], in1=xc[k], op0=ALU.mult, op1=ALU.add),
                        reads=[xrR[k], xcR[k]], writes=[xcR[k]])
                P.emit("act", lambda e, k=k: e.activation(out=xcb[k], in_=xc[k], func=AF.Copy), reads=[xcR[k]], writes=[xcbR[k]])
                psr, prr = self.bank()
                psi, pri = self.bank()
                P.emit("pe", lambda e, k=k, c=c, psr=psr: e.matmul(psr[:], self.gw[:, 0, c, :], xcb[k], start=True, stop=True), reads=[gwR, xcbR[k]], writes=[prr])
                P.emit("pe", lambda e, k=k, c=c, psi=psi: e.matmul(psi[:], self.gw[:, 1, c, :], xcb[k], start=True, stop=True), reads=[gwR, xcbR[k]], writes=[pri])
                gab = self.parm[:, pc["gab"] + o_ * 8 + c: pc["gab"] + o_ * 8 + c + 1]
                gxb = self.parm[:, pc["gxb"] + o_ * 8 + c: pc["gxb"] + o_ * 8 + c + 1]
                cl = der[:, o_ * 8 + c: o_ * 8 + c + 1]
                cl2 = der[:, 16 + o_ * 8 + c: 16 + o_ * 8 + c + 1]
                one = self.parm[:, pc["eps"] + 2:pc["eps"] + 3]
                P.emit("act", lambda e, k=k, psr=psr, gab=gab: e.activation(out=bA[k], in_=psr[:], func=AF.Sigmoid, bias=gab, scale=1.0), reads=[prr, self.cR], writes=[AR_[k]])
                P.emit("act", lambda e, k=k, psi=psi, gxb=gxb: e.activation(out=bC[k], in_=psi[:], func=AF.Sigmoid, bias=gxb, scale=1.0), reads=[pri, self.cR], writes=[CR[k]])
                P.emit("act", lambda e, k=k, cl2=cl2: e.activation(out=bB[k], in_=bA[k], func=AF.Exp, scale=cl2), reads=[AR_[k]], writes=[BR[k]])
                P.emit("act", lambda e, k=k, cl=cl: e.activation(out=bA[k], in_=bA[k], func=AF.Exp, scale=cl), reads=[AR_[k], BR[k]], writes=[AR_[k]])
                P.emit("act", lambda e, k=k, one=one: e.activation(out=bB[k], in_=bB[k], func=AF.Sqrt, bias=one, scale=-1.0), reads=[BR[k], self.cR], writes=[BR[k]])
                P.emit("dve", lambda e, k=k: e.tensor_tensor(out=bC[k], in0=bC[k], in1=xc[k], op=ALU.mult), reads=[CR[k], xcR[k]], writes=[CR[k]])
                P.emit("dve", lambda e, k=k: e.tensor_tensor(out=bC[k], in0=bC[k], in1=bB[k], op=ALU.mult), reads=[CR[k], BR[k]], writes=[CR[k]])
                P.emit("dve", lambda e, k=k, c=c: e.tensor_tensor_scan(out=hs[k], data0=bA[k], data1=bC[k], initial=carry[:, c:c + 1], op0=ALU.mult, op1=ALU.add),
                       reads=[AR_[k], CR[k], carryR[c]], writes=[hsR[k]])
                P.emit("act", lambda e, k=k, c=c: e.activation(out=carry[:, c:c + 1], in_=hs[k][:, TT - 1:TT], func=AF.Copy), reads=[hsR[k]], writes=[carryR[c]])
                P.emit("dve", lambda e, k=k, c=c: e.tensor_tensor(out=mo[:, c, :], in0=hs[k], in1=gate[:, c, :], op=ALU.mult), reads=[hsR[k], gateR[c]], writes=[moR[c]])
            for c in range(DC):
                ps, pr = self.bank()
                for j in range(DC):
                    P.emit("pe", lambda e, j=j, c=c, ps=ps: e.matmul(ps[:], wo[:, j, c * 128:(c + 1) * 128], mo[:, j, :], start=(j == 0), stop=(j == DC - 1)),
                           reads=[SB, moR[j]], writes=[pr])
                P.emit("dve", lambda e, c=c, ps=ps, tr=tr: e.tensor_tensor(out=self.h[:, c, tr], in0=ps[:], in1=self.h[:, c, tr], op=ALU.add),
                       reads=[pr, self.Hr[c][t]], writes=[self.Hr[c][t]], nosame=True)
        P.barrier(("pe", "act", "dve"))


def prep_shared(inp):
    params, pcol = pack_params(inp)
    biasT, maskneg = attn_consts(inp["rel_bias"])
    shared = {
        "params": params,
        "ident": np.eye(128, dtype=np.float32),
        "biasT": biasT.reshape(128, 2048),
        "maskneg": maskneg.reshape(128, 2048),
        "even_w_in": np.ascontiguousarray(np.asarray(inp["even_w_in"], np.float32)[:, :, qperm()]),
    }
    for k in ("ffn1_wg", "ffn1_wu", "ffn1_wd", "ffn2_wg", "ffn2_wu", "ffn2_wd", "even_w_out", "odd_w_in",
              "odd_w_out", "gate_a_w", "gate_x_w"):
        shared[k] = np.ascontiguousarray(np.asarray(inp[k], np.float32))
    return shared, pcol, params.shape[1]


def kernel(**inputs):
    x = np.asarray(inputs["x"], np.float32)
    shared, pcol, npar = prep_shared(inputs)
    n_seq = x.shape[0] // N_CORES
    b = Builder(n_seq, DEPTH)
    b.npar = npar
    nc = b.build(pcol)
    in_maps = []
    for c in range(N_CORES):
        m = dict(shared)
        m["x"] = np.ascontiguousarray(x[c * n_seq:(c + 1) * n_seq])
        in_maps.append(m)
    res = run_bass_kernel_spmd(nc, in_maps, core_ids=list(range(N_CORES)))
    return np.concatenate([r["out"] for r in res.results], axis=0)
```
